# Optimizing a Trainium2 kernel written in Bass

```python
import functools
import jax, jax.numpy as jnp
from jax import lax
import numpy as np

D_MODEL = 4096
BATCH = 4
SEQ = 2048
DEPTH = 1
DEC_BATCH = 128
DEC_SEQ = 4
PAST_LEN = 16384
PAGE_SIZE = 128

D_MIX = D_MODEL
GLA_WIDTH = D_MIX // 2
SSD_WIDTH = D_MIX - GLA_WIDTH
GLA_HEADS = 4
GLA_DV = GLA_WIDTH // GLA_HEADS
GLA_DK = GLA_DV // 2
GLA_QK = GLA_HEADS * GLA_DK
GATE_RANK = 16
GATE_NORM = 16.0
SSD_HEAD_DIM = 64
SSD_HEADS = SSD_WIDTH // SSD_HEAD_DIM
SSD_GROUPS = 8
HEADS_PER_GROUP = SSD_HEADS // SSD_GROUPS
D_STATE = 128
CONV_W = 4
CONV_DIM = SSD_WIDTH + 2 * SSD_GROUPS * D_STATE
D_FF = 4 * D_MODEL
N_META = 16
CHUNK = 64
SPLIT_SIZES = (GLA_QK, GLA_QK, GLA_WIDTH, GLA_WIDTH, GATE_RANK, SSD_WIDTH, CONV_DIM, SSD_HEADS)
D_IN = GLA_QK * 2 + GLA_WIDTH * 2 + GATE_RANK + SSD_WIDTH + CONV_DIM + SSD_HEADS
ALPHA = (2.0 * DEPTH) ** 0.25
BETA = (8.0 * DEPTH) ** -0.25
EPS = 1e-5

kernel_name = "hymba_gla_ssd_deepnorm_step"


def _layer_norm(x, g, b):
    xf = x.astype(jnp.float32)
    mu = jnp.mean(xf, axis=-1, keepdims=True)
    var = jnp.mean(jnp.square(xf - mu), axis=-1, keepdims=True)
    return ((xf - mu) * lax.rsqrt(var + EPS) * g.astype(jnp.float32) + b.astype(jnp.float32)).astype(x.dtype)


def _rms(xf):
    return xf * lax.rsqrt(jnp.mean(jnp.square(xf), axis=-1, keepdims=True) + EPS)


def _gla_chunk(S, q, k, v, g):
    T = q.shape[1]
    b = jnp.cumsum(g, axis=1)
    mask = jnp.tril(jnp.ones((T, T), dtype=bool))
    diff = b[:, :, None] - b[:, None, :]
    decay = jnp.exp(jnp.where(mask[None, :, :, None, None], diff, -jnp.inf))
    scores = jnp.einsum('bthk,bshk,btshk->bhts', q, k, decay)
    o = jnp.einsum('bhts,bshv->bthv', scores, v) + jnp.einsum('bthk,bhkv->bthv', q * jnp.exp(b), S)
    b_last = b[:, -1]
    S_new = jnp.exp(b_last)[..., None] * S + jnp.einsum(
        'bshk,bshv->bhkv', k * jnp.exp(b_last[:, None] - b), v)
    return S_new, o


def _ssd_chunk(h, x, bm, cm, dt, a):
    T = x.shape[1]
    cum = jnp.cumsum(dt * a, axis=1)
    mask = jnp.tril(jnp.ones((T, T), dtype=bool))
    lmat = jnp.exp(jnp.where(mask[None, :, :, None, None], cum[:, :, None] - cum[:, None], -jnp.inf))
    cb = jnp.einsum('btgn,bsgn->btsg', cm, bm)
    y = jnp.einsum('btsg,btsgj,bsgj,bsgjp->btgjp', cb, lmat, dt, x)
    y = y + jnp.einsum('btgn,bgjpn->btgjp', cm, h) * jnp.exp(cum)[..., None]
    c_last = cum[:, -1]
    h_new = jnp.exp(c_last)[..., None, None] * h + jnp.einsum(
        'bsgn,bsgj,bsgjp->bgjpn', bm, jnp.exp(c_last[:, None] - cum) * dt, x)
    return h_new, y


def _run_chunked(step, state, xs, lead):
    T = xs[0].shape[1]
    outs = []
    if lead > 0:
        state, o = step(state, *[a[:, :lead] for a in xs])
        outs.append(o)
    n_full = (T - lead) // CHUNK
    if n_full > 0:
        blocks = tuple(
            jnp.moveaxis(a[:, lead:lead + n_full * CHUNK].reshape(a.shape[0], n_full, CHUNK, *a.shape[2:]), 1, 0)
            for a in xs)
        state, o = lax.scan(lambda s, blk: step(s, *blk), state, blocks)
        o = jnp.moveaxis(o, 0, 1)
        outs.append(o.reshape(o.shape[0], n_full * CHUNK, *o.shape[3:]))
    start = lead + n_full * CHUNK
    if start < T:
        state, o = step(state, *[a[:, start:] for a in xs])
        outs.append(o)
    return state, jnp.concatenate(outs, axis=1)


def _layer(x, gla_s, ssm_h, conv_buf, lead, w_in, w_gk_up, b_gk, gla_norm_w, conv_w, conv_b,
           dt_bias, a_log, d_skip, ssd_norm_w, w_out, ln1_g, ln1_b, w_up, w_down, ln2_g, ln2_b):
    f32 = jnp.float32
    B, T, _ = x.shape
    u = x @ w_in
    pts, acc = [], 0
    for s in SPLIT_SIZES[:-1]:
        acc += s
        pts.append(acc)
    q, k, v, r, gk_low, z, xbc, dt_raw = jnp.split(u, pts, axis=-1)

    qh = q.reshape(B, T, GLA_HEADS, GLA_DK).astype(f32) * (GLA_DK ** -0.5)
    kh = k.reshape(B, T, GLA_HEADS, GLA_DK).astype(f32)
    vh = v.reshape(B, T, GLA_HEADS, GLA_DV).astype(f32)
    g = jax.nn.log_sigmoid((gk_low @ w_gk_up + b_gk).astype(f32)) / GATE_NORM
    g = g.reshape(B, T, GLA_HEADS, GLA_DK)
    gla_new, o = _run_chunked(_gla_chunk, gla_s.astype(f32), (qh, kh, vh, g), lead)
    o = _rms(o) * gla_norm_w.astype(f32) * jax.nn.silu(r.reshape(B, T, GLA_HEADS, GLA_DV).astype(f32))
    o = o.reshape(B, T, GLA_WIDTH)

    xbc_full = jnp.concatenate([conv_buf.astype(xbc.dtype), xbc], axis=1)
    conv_new = xbc_full[:, -(CONV_W - 1):]
    xc = lax.conv_general_dilated(xbc_full, conv_w[:, None, :].astype(xbc.dtype), window_strides=(1,),
                                  padding='VALID', dimension_numbers=('NWC', 'WIO', 'NWC'),
                                  feature_group_count=CONV_DIM)
    xc = jax.nn.silu((xc + conv_b).astype(f32))
    xs, bm, cm = jnp.split(xc, [SSD_WIDTH, SSD_WIDTH + SSD_GROUPS * D_STATE], axis=-1)
    xs = xs.reshape(B, T, SSD_GROUPS, HEADS_PER_GROUP, SSD_HEAD_DIM)
    bm = bm.reshape(B, T, SSD_GROUPS, D_STATE)
    cm = cm.reshape(B, T, SSD_GROUPS, D_STATE)
    dt = jax.nn.softplus(dt_raw.astype(f32) + dt_bias.astype(f32)).reshape(B, T, SSD_GROUPS, HEADS_PER_GROUP)
    a = -jnp.exp(a_log.astype(f32)).reshape(SSD_GROUPS, HEADS_PER_GROUP)
    h0 = ssm_h.astype(f32).reshape(B, SSD_GROUPS, HEADS_PER_GROUP, SSD_HEAD_DIM, D_STATE)
    ssm_new, y = _run_chunked(functools.partial(_ssd_chunk, a=a), h0, (xs, bm, cm, dt), lead)
    y = y + d_skip.astype(f32).reshape(SSD_GROUPS, HEADS_PER_GROUP)[..., None] * xs
    y = y.reshape(B, T, SSD_WIDTH) * jax.nn.silu(z.astype(f32))
    y = _rms(y.reshape(B, T, SSD_GROUPS, SSD_WIDTH // SSD_GROUPS)).reshape(B, T, SSD_WIDTH)
    y = y * ssd_norm_w.astype(f32)

    mix = jnp.concatenate([o, y], axis=-1).astype(x.dtype) @ w_out
    h = _layer_norm(ALPHA * x + mix, ln1_g, ln1_b)
    f = jnp.square(jax.nn.relu(h @ w_up)) @ w_down
    out = _layer_norm(ALPHA * h + f, ln2_g, ln2_b)
    return (out, gla_new.astype(x.dtype),
            ssm_new.reshape(B, SSD_HEADS, SSD_HEAD_DIM, D_STATE).astype(x.dtype), conv_new.astype(x.dtype))


def setup_inputs(seed: int = 0) -> dict:
    key = jax.random.key(seed)
    ks = jax.random.split(key, 26)
    nrm = jax.random.normal
    f32 = jnp.float32
    dt = jnp.exp(jax.random.uniform(ks[12], (DEPTH, SSD_HEADS), f32) * (np.log(0.1) - np.log(0.001)) + np.log(0.001))
    return {
        "x_prompt": nrm(ks[0], (BATCH, SEQ, D_MODEL), f32),
        "x_sample": nrm(ks[1], (DEC_BATCH, DEC_SEQ, D_MODEL), f32),
        "state_gla": 0.5 * nrm(ks[2], (DEPTH, DEC_BATCH, GLA_HEADS, GLA_DK, GLA_DV), f32),
        "state_ssm": 0.1 * nrm(ks[3], (DEPTH, DEC_BATCH, SSD_HEADS, SSD_HEAD_DIM, D_STATE), f32),
        "state_conv": nrm(ks[4], (DEPTH, DEC_BATCH, CONV_W - 1, CONV_DIM), f32),
        "meta_tokens": nrm(ks[5], (N_META, D_MODEL), f32),
        "w_in": nrm(ks[6], (DEPTH, D_MODEL, D_IN), f32) * D_MODEL ** -0.5,
        "w_gk_up": nrm(ks[7], (DEPTH, GATE_RANK, GLA_QK), f32) * GATE_RANK ** -0.5,
        "b_gk": 0.1 * nrm(ks[8], (DEPTH, GLA_QK), f32),
        "gla_norm_w": 1.0 + 0.02 * nrm(ks[9], (DEPTH, GLA_DV), f32),
        "conv_w": nrm(ks[10], (DEPTH, CONV_W, CONV_DIM), f32) * CONV_W ** -0.5,
        "conv_b": 0.02 * nrm(ks[11], (DEPTH, CONV_DIM), f32),
        "dt_bias": dt + jnp.log(-jnp.expm1(-dt)),
        "a_log": jnp.log(jax.random.uniform(ks[13], (DEPTH, SSD_HEADS), f32, 1.0, 16.0)),
        "d_skip": 1.0 + 0.1 * nrm(ks[14], (DEPTH, SSD_HEADS), f32),
        "ssd_norm_w": 1.0 + 0.02 * nrm(ks[15], (DEPTH, SSD_WIDTH), f32),
        "w_out": nrm(ks[16], (DEPTH, D_MIX, D_MODEL), f32) * (D_MIX ** -0.5) * BETA,
        "ln1_g": 1.0 + 0.02 * nrm(ks[17], (DEPTH, D_MODEL), f32),
        "ln1_b": 0.02 * nrm(ks[18], (DEPTH, D_MODEL), f32),
        "w_up": nrm(ks[19], (DEPTH, D_MODEL, D_FF), f32) * D_MODEL ** -0.5,
        "w_down": nrm(ks[20], (DEPTH, D_FF, D_MODEL), f32) * (D_FF ** -0.5) * BETA,
        "ln2_g": 1.0 + 0.02 * nrm(ks[21], (DEPTH, D_MODEL), f32),
        "ln2_b": 0.02 * nrm(ks[22], (DEPTH, D_MODEL), f32),
    }


def reference(x_prompt, x_sample, state_gla, state_ssm, state_conv, meta_tokens, w_in, w_gk_up, b_gk,
              gla_norm_w, conv_w, conv_b, dt_bias, a_log, d_skip, ssd_norm_w, w_out, ln1_g, ln1_b,
              w_up, w_down, ln2_g, ln2_b):
    bp = x_prompt.shape[0]
    dt_ = x_prompt.dtype
    meta = jnp.broadcast_to(meta_tokens.astype(dt_)[None], (bp, N_META, D_MODEL))
    xp = jnp.concatenate([meta, x_prompt], axis=1)
    xs = x_sample
    gla_p, ssm_p, conv_p, gla_s, ssm_s, conv_s = [], [], [], [], [], []
    for l in range(DEPTH):
        p = (w_in[l], w_gk_up[l], b_gk[l], gla_norm_w[l], conv_w[l], conv_b[l], dt_bias[l], a_log[l],
             d_skip[l], ssd_norm_w[l], w_out[l], ln1_g[l], ln1_b[l], w_up[l], w_down[l], ln2_g[l], ln2_b[l])
        zg = jnp.zeros((bp, GLA_HEADS, GLA_DK, GLA_DV), dt_)
        zs = jnp.zeros((bp, SSD_HEADS, SSD_HEAD_DIM, D_STATE), dt_)
        zc = jnp.zeros((bp, CONV_W - 1, CONV_DIM), dt_)
        xp, g1, s1, c1 = _layer(xp, zg, zs, zc, N_META, *p)
        xs, g2, s2, c2 = _layer(xs, state_gla[l], state_ssm[l], state_conv[l], 0, *p)
        gla_p.append(g1); ssm_p.append(s1); conv_p.append(c1)
        gla_s.append(g2); ssm_s.append(s2); conv_s.append(c2)
    y_prompt = xp[:, N_META:]
    return (y_prompt, xs, jnp.stack(gla_p), jnp.stack(ssm_p), jnp.stack(conv_p),
            jnp.stack(gla_s), jnp.stack(ssm_s), jnp.stack(conv_s))
```

```python
import contextlib
import numpy as np
import concourse.bass as bass
import concourse.mybir as mybir
from concourse.bass_utils import run_bass_kernel_spmd

F32 = mybir.dt.float32
BF16 = mybir.dt.bfloat16
AF = mybir.ActivationFunctionType
ALU = mybir.AluOpType

NT = 1096
NPRE = 1032
NTX = 1147
ALPHA = 2.0 ** 0.25
EPS = 1e-5
NEG = -30000.0

CM = {}
_o = 0
for _n, _w in [("ident", 128), ("tri", 128), ("ntri16", 128), ("nR16", 128), ("negtri", 128), ("ones", 128),
               ("negmask4", 512), ("Rs", 128), ("btri", 64), ("nbtri16", 64), ("nRb16", 64), ("negbtri", 64),
               ("negbmask4", 256), ("Rbs", 64), ("seqmask", 16), ("colmask", 1024)]:
    CM[_n] = (_o, _o + _w)
    _o += _w
NCM = _o
PV = {}
_o = 0
for _n, _w in [("dtb", 32), ("alog", 32), ("bgk", 1024), ("wgk", 1024), ("gnw", 4), ("convw", 128), ("convb", 32),
               ("dsk", 16), ("snw", 16), ("flag", 1)]:
    PV[_n] = (_o, _o + _w)
    _o += _w
NPV = _o


class Ev:
    __slots__ = ("sem", "val", "eng")

    def __init__(self, sem, val, eng):
        self.sem, self.val, self.eng = sem, val, eng


class Buf:
    def __init__(self, name):
        self.name = name
        self.last_write = None
        self.readers = []
        self.dsem = None
        self.dcount = 0


class TT_:
    def __init__(self, t, name):
        self.t = t
        self.b = Buf(name)


class Prog:
    ENGS = ("pe", "act", "dve", "pool", "sp")

    def __init__(self, nc, stack):
        self.nc = nc
        self.stack = stack
        self.ops = {e: [] for e in self.ENGS}
        self.esem = {e: stack.enter_context(nc.semaphore("es_" + e)) for e in self.ENGS}
        self.ecount = {e: 0 for e in self.ENGS}
        self.known = {e: {} for e in self.ENGS}
        self.out_events = []
        self.dma_owners = []

    def _need(self, eng, ev, raw):
        if ev is None:
            return None
        if ev.eng == eng and ev.sem is self.esem[eng]:
            if eng == "pe":
                return None
        if self.known[eng].get(id(ev.sem), 0) >= ev.val:
            return None
        return ev

    def _collect(self, eng, reads, writes):
        waits = {}

        def add(ev, raw):
            ev = self._need(eng, ev, raw)
            if ev is None:
                return
            cur = waits.get(id(ev.sem))
            if cur is None or cur.val < ev.val:
                waits[id(ev.sem)] = ev

        for b in reads:
            add(b.last_write, True)
        for b in writes:
            add(b.last_write, False)
            for r in b.readers:
                add(r, False)
        out = []
        for ev in waits.values():
            self.known[eng][id(ev.sem)] = ev.val
            out.append((ev.sem, ev.val))
        return out

    def _commit(self, ev, reads, writes):
        for b in reads:
            b.readers.append(ev)
            if len(b.readers) > 16:
                best = {}
                for r in b.readers:
                    c = best.get(id(r.sem))
                    if c is None or c.val < r.val:
                        best[id(r.sem)] = r
                b.readers = list(best.values())
        for b in writes:
            b.last_write = ev
            b.readers = []

    def op(self, eng, fn, reads=(), writes=()):
        waits = self._collect(eng, reads, writes)
        self.ecount[eng] += 1
        ev = Ev(self.esem[eng], self.ecount[eng], eng)
        self._commit(ev, reads, writes)
        self.ops[eng].append((waits, fn, (self.esem[eng], 1)))
        return ev

    def dma(self, eng, out_ap, in_ap, owner, reads=(), writes=(), is_out=False):
        if owner.dsem is None:
            owner.dsem = self.stack.enter_context(self.nc.semaphore("ds_" + owner.name))
            self.dma_owners.append(owner)
        waits = self._collect(eng, reads, writes)
        owner.dcount += 16
        ev = Ev(owner.dsem, owner.dcount, "dma")
        self._commit(ev, reads, writes)
        self.ops[eng].append((waits, lambda e: e.dma_start(out=out_ap, in_=in_ap), (owner.dsem, 16)))
        if is_out:
            self.out_events.append(ev)
        return ev

    def finish(self):
        self.ops["sp"].append(([(o.dsem, o.dcount) for o in self.dma_owners], None, None))

    def emit(self):
        nc = self.nc
        with nc.Block() as block:
            def run(e, lst):
                for waits, fn, inc in lst:
                    for sem, val in waits:
                        e.wait_ge(sem, val)
                    if fn is not None:
                        ins = fn(e)
                        if inc is not None:
                            ins.then_inc(inc[0], inc[1])

            @block.sync
            def _(e):
                run(e, self.ops["sp"])

            @block.tensor
            def _(e):
                run(e, self.ops["pe"])

            @block.scalar
            def _(e):
                run(e, self.ops["act"])

            @block.vector
            def _(e):
                run(e, self.ops["dve"])

            @block.gpsimd
            def _(e):
                run(e, self.ops["pool"])


class Ring:
    def __init__(self, items):
        self.items = items
        self.i = 0

    def __call__(self):
        x = self.items[self.i % len(self.items)]
        self.i += 1
        return x


def build_nc(stage=99):
    nc = bass.Bass("TRN2", target_bir_lowering=False)

    def D(name, shape, dt=F32, kind="ExternalInput"):
        return nc.dram_tensor(name, shape, dt, kind=kind).ap()

    xm = D("xm", [NT, 4096])
    xp = D("xp", [NPRE, 4096])
    cmask_d = D("cmask", [128, NCM])
    pvec_d = D("pvec", [128, NPV])
    w_in_t = D("w_in_t", [97, 128, 32, 128])
    w_out_t = D("w_out_t", [32, 128, 32, 128])
    w_up_t = D("w_up_t", [128, 128, 32, 128])
    w_down = D("w_down", [16384, 4096])
    lnp = D("lnp", [4, 4096])
    sgla = D("sgla", [16, 4, 256, 512])
    sssm = D("sssm", [16, 32, 64, 128])
    sconv = D("sconv", [48, 4096])
    y_o = D("y", [NT, 4096], kind="ExternalOutput")
    gla_p_o = D("gla_p", [4, 256, 512], kind="ExternalOutput")
    ssm_p_o = D("ssm_p", [32, 64, 128], kind="ExternalOutput")
    conv_p_o = D("conv_p", [3, 4096], kind="ExternalOutput")
    gla_s_o = D("gla_s", [16, 4, 256, 512], kind="ExternalOutput")
    ssm_s_o = D("ssm_s", [16, 32, 64, 128], kind="ExternalOutput")
    conv_s_o = D("conv_s", [48, 4096], kind="ExternalOutput")
    mix_scr = D("mix_scr", [9, 128, 32, 128], BF16, kind="Internal")
    S_scr = D("S_scr", [4, 128, 2, 512], F32, kind="Internal")
    hT_scr = D("hT_scr", [8, 128, 256], F32, kind="Internal")
    mixb = [Buf(f"mixb{t}") for t in range(9)]
    Sscrb = [Buf(f"Sscrb{h}") for h in range(4)]
    hscrb = [Buf(f"hscrb{g}") for g in range(8)]

    with contextlib.ExitStack() as st:
        P = Prog(nc, st)
        P.fence = {}

        class Scope:
            def __init__(self):
                self.stack = contextlib.ExitStack()
                self.bufs = []

            def __enter__(self):
                self.stack.__enter__()
                return self

            def __exit__(self, *a):
                for b in self.bufs:
                    for ev in ([b.last_write] if b.last_write else []) + b.readers:
                        c = P.fence.get(id(ev.sem))
                        if c is None or c.val < ev.val:
                            P.fence[id(ev.sem)] = ev
                return self.stack.__exit__(*a)

        root = Scope()
        root.stack = st
        uid = [0]

        def sb(name, shape, dt=F32, sc=None):
            sc = sc or root
            uid[0] += 1
            nm = f"{name}_{uid[0]}"
            t = TT_(sc.stack.enter_context(nc.sbuf_tensor(nm, shape, dt)), nm)
            t.b.readers = list(P.fence.values())
            sc.bufs.append(t.b)
            return t

        def ring(name, shape, dt, n, sc=None):
            return Ring([sb(f"{name}{i}", shape, dt, sc) for i in range(n)])

        def ACT(out, in_, func, reads, writes, **kw):
            return P.op("act", lambda e: e.activation(out=out, in_=in_, func=func, **kw), reads, writes)

        def TTo(out, in0, in1, op, reads, writes, eng="dve"):
            return P.op(eng, lambda e: e.tensor_tensor(out=out, in0=in0, in1=in1, op=op), reads, writes)

        def TSM(out, in0, s1, reads, writes, eng="dve"):
            return P.op(eng, lambda e: e.tensor_scalar_mul(out=out, in0=in0, scalar1=s1), reads, writes)

        def STT(out, in0, scalar, in1, op0, op1, reads, writes, eng="dve"):
            return P.op(eng, lambda e: e.scalar_tensor_tensor(out=out, in0=in0, scalar=scalar, in1=in1, op0=op0,
                                                              op1=op1), reads, writes)

        def CP(eng, out, in_, reads, writes):
            if eng == "act":
                return P.op("act", lambda e: e.copy(out=out, in_=in_), reads, writes)
            return P.op(eng, lambda e: e.tensor_copy(out=out, in_=in_), reads, writes)

        def MM(lst, reads, writes):
            def fn(e):
                ins = None
                for (o, l, r, s0, s1) in lst:
                    ins = e.matmul(o, lhsT=l, rhs=r, start=s0, stop=s1)
                return ins
            return P.op("pe", fn, reads, writes)

        def TR(lst, reads, writes):
            def fn(e):
                ins = None
                for (o, i, idn) in lst:
                    ins = e.transpose(o, i, idn)
                return ins
            return P.op("pe", fn, reads, writes)

        def MEMSET(ap, val, writes):
            return P.op("dve", lambda e: e.memset(ap, val), [], writes)

        def RSTD(ss_, tmp, rstd_, L, scale):
            ACT(tmp.t[0:L, :], ss_.t[0:L, :], AF.Sqrt, [ss_.b, epsc.b], [tmp.b], scale=scale, bias=epsc.t[0:L, :])
            P.op("dve", lambda e: e.reciprocal(out=rstd_.t[0:L, :], in_=tmp.t[0:L, :]), [tmp.b], [rstd_.b])

        banks = [TT_(st.enter_context(nc.psum_tensor(f"pb{i}", [128, 512], F32)), f"pb{i}") for i in range(8)]
        nb = Ring(banks[0:7])
        acc_bank = banks[7]

        def bf(bank):
            return bank.t[:].bitcast(BF16)

        identF = sb("identF", [128, 128])
        identb = sb("identb", [128, 128], BF16)
        epsc = sb("epsc", [128, 1])
        P.dma("sp", identF.t[:], cmask_d[:, CM["ident"][0]:CM["ident"][1]], identF.b, writes=[identF.b])
        CP("act", identb.t[:], identF.t[:], [identF.b], [identb.b])
        MEMSET(epsc.t[:], EPS, [epsc.b])
        identf = identF.t[:]
        wring = ring("wsl", [128, 32, 128], BF16, 3)

        def wl(dram_ap):
            s = wring()
            P.dma("pool", s.t[:], dram_ap, s.b, writes=[s.b])
            return s

        with Scope() as phAB:
            cm = sb("cm", [128, NCM], F32, phAB)
            pv = sb("pv", [128, NPV], F32, phAB)
            a_bc = sb("a_bc", [128, 32], F32, phAB)
            diagD = sb("diagD", [128, 16, 128], BF16, phAB)
            halo_all = sb("halo_all", [128, 32, 3], F32, phAB)
            P.dma("sp", cm.t[:], cmask_d, cm.b, writes=[cm.b])
            P.dma("sp", pv.t[:], pvec_d, pv.b, writes=[pv.b])

            def C(name, r=slice(0, 128), c=None):
                a, b_ = CM[name]
                if c is None:
                    return cm.t[r, a:b_]
                return cm.t[r, a + c.start:a + c.stop]

            def Pv(name, r=slice(0, 128), c=None):
                a, b_ = PV[name]
                if c is None:
                    return pv.t[r, a:b_]
                return pv.t[r, a + c.start:a + c.stop]

            ACT(a_bc.t[:], Pv("alog"), AF.Exp, [pv.b], [a_bc.b])
            TSM(a_bc.t[:], a_bc.t[:], -1.0, [a_bc.b], [a_bc.b])
            for xt in range(16):
                TSM(diagD.t[:, xt, :], identf, Pv("dsk", c=slice(xt, xt + 1)), [identF.b, pv.b], [diagD.b])
            MEMSET(halo_all.t[:], 0.0, [halo_all.b])
            xT = sb("xT", [128, 32, NT], BF16, phAB)
            xTb = [[Buf(f"xT{t}_{k}") for k in range(2)] for t in range(9)]
            for l_ in xTb:
                phAB.bufs.extend(l_)

            def build_xT(x_d, ntok):
                bufs = []
                with Scope() as sx:
                    xst = ring("xst", [128, 4096], F32, 2, sx)
                    for t in range((ntok + 127) // 128):
                        r0 = t * 128
                        n = min(128, ntok - r0)
                        xs_ = xst()
                        P.dma("sp", xs_.t[0:n, :], x_d[r0:r0 + n, :], xs_.b, writes=[xs_.b])
                        for q in range(8):
                            bk = nb()
                            TR([(bk.t[:, j * 128:j * 128 + n], xs_.t[0:n, (4 * q + j) * 128:(4 * q + j + 1) * 128],
                                 identf[0:n, 0:n]) for j in range(4)], [xs_.b, identF.b], [bk.b])
                            src = bk.t[:].rearrange("p (a t) -> p a t", a=4)[:, :, 0:n]
                            CP("act" if q % 2 == 0 else "dve", xT.t[:, 4 * q:4 * q + 4, r0:r0 + n], src, [bk.b],
                               [xTb[t][q % 2]])
                        bufs += xTb[t]
                return bufs

            def gemm_tile(slot, M, xbufs, chunks, evac, mcol0=0):
                for (c0, c1) in chunks:
                    bk = nb()
                    MM([(bk.t[0:M, 0:c1 - c0], slot.t[:, kt, mcol0:mcol0 + M], xT.t[:, kt, c0:c1], kt == 0, kt == 31)
                        for kt in range(32)], [slot.b] + xbufs, [bk.b])
                    evac(bk, c0, c1)

            for pre in (True, False):
                ntok = NPRE if pre else NT
                xbufs = build_xT(xp if pre else xm, ntok)
                if pre:
                    chunks = [(0, 512), (512, 1024), (1024, 1032)]
                    scan = [(128, 128 * c, c) for c in range(8)] + [(8, 1024, 9)]
                else:
                    chunks = [(0, 512), (512, 1024), (1024, 1096)]
                    scan = [(128, 128 * c, c) for c in range(8)] + [(8, 1088, 9)]
                with Scope() as ph:
                    gklT = sb("gklT", [16, NT], F32, ph)
                    dt_tm = sb("dt_tm", [128, 10, 32], F32, ph)
                    dta_tm = sb("dta_tm", [128, 10, 32], F32, ph)
                    sdt = Scope()
                    sdt.__enter__()
                    dtT = sb("dtT", [32, NT], F32, sdt)
                    slot = wl(w_in_t[0])
                    gemm_tile(slot, 16, xbufs, chunks,
                              lambda bk, c0, c1: CP("act", gklT.t[:, c0:c1], bk.t[0:16, 0:c1 - c0], [bk.b], [gklT.b]))
                    gemm_tile(slot, 32, xbufs, chunks,
                              lambda bk, c0, c1: CP("act", dtT.t[:, c0:c1], bk.t[0:32, 0:c1 - c0], [bk.b], [dtT.b]),
                              mcol0=32)
                    dtl = list(scan) + ([] if pre else [(64, 1024, 8)])
                    dtmp = sb("dtmp", [128, 32], F32, sdt)
                    for (L, col, ci) in dtl:
                        bk = nb()
                        TR([(bk.t[0:L, 0:32], dtT.t[0:32, col:col + L], identf[0:32, 0:32])], [dtT.b, identF.b], [bk.b])
                        TTo(dtmp.t[0:L, :], bk.t[0:L, 0:32], Pv("dtb", slice(0, L)), ALU.add, [bk.b, pv.b], [dtmp.b])
                        ACT(dtmp.t[0:L, :], dtmp.t[0:L, :], AF.Exp, [dtmp.b], [dtmp.b])
                        ACT(dt_tm.t[0:L, ci, :], dtmp.t[0:L, :], AF.Ln, [dtmp.b], [dt_tm.b], bias=1.0)
                        TTo(dta_tm.t[0:L, ci, :], dt_tm.t[0:L, ci, :], a_bc.t[0:L, :], ALU.mult, [dt_tm.b, a_bc.b],
                            [dta_tm.b])
                    sdt.__exit__(None, None, None)

                    if stage >= 1:
                      with Scope() as pg:
                        kT = sb("kT", [128, 2, NT], F32, pg)
                        qT = sb("qT", [128, 2, NT], F32, pg)
                        v_tm = sb("v_tm", [128, 10, 512], BF16, pg)
                        rgT = sb("rgT", [128, 4, NT], BF16, pg)
                        vstage = ring("vstage", [128, 512], BF16, 2, pg)
                        rtmp = ring("rtmp", [128, 512], F32, 1, pg)
                        e1 = ring("e1", [128, 256], F32, 1, pg)
                        spb = ring("spb", [128, 256], F32, 1, pg)
                        E4 = ring("E4", [128, 4, 128], F32, 1, pg)
                        Ekn = ring("Ekn", [128, 2, 128], F32, 1, pg)
                        qp = ring("qp", [128, 2, 128], BF16, 2, pg)
                        kp = ring("kp", [128, 2, 128], BF16, 2, pg)
                        kpp = ring("kpp", [128, 2, 128], BF16, 2, pg)
                        kpptm = ring("kpptm", [128, 256], BF16, 2, pg)
                        AT = ring("AT", [128, 128], BF16, 2, pg)
                        ss = ring("ss", [128, 1], F32, 2, pg)
                        ss2 = ring("ss2", [128, 1], F32, 2, pg)
                        rstd = ring("rstd", [128, 1], F32, 2, pg)
                        junk = sb("junk", [128, 512], BF16, pg)
                        on = ring("on", [128, 512], BF16, 2, pg)
                        mixst = ring("mixst", [128, 4, 128], BF16, 2, pg)
                        S = sb("S", [128, 2, 512], F32, pg)
                        Sbf = sb("Sbf", [128, 2, 512], BF16, pg)
                        if not pre:
                            kppm = ring("kppm", [64, 256], BF16, 2, pg)
                            qpm = ring("qpm", [128, 2, 64], BF16, 2, pg)
                            Sin = ring("Sin", [128, 2, 512], F32, 2, pg)
                            Sinb = ring("Sinb", [128, 2, 512], BF16, 2, pg)

                        def v_evac(j):
                            def f(bk, c0, c1):
                                vs = vstage()
                                n = c1 - c0
                                CP("act", vs.t[:, 0:n], bk.t[:, 0:n], [bk.b], [vs.b])
                                tb = nb()
                                tbv = bf(tb)
                                if n == 512:
                                    TR([(tbv[:, a * 128:(a + 1) * 128], vs.t[:, a * 128:(a + 1) * 128], identb.t[:, :])
                                        for a in range(4)], [vs.b, identb.b], [tb.b])
                                    ci0 = c0 // 128
                                    CP("dve", v_tm.t[:, ci0:ci0 + 4, j * 128:(j + 1) * 128],
                                       tbv[:, 0:512].rearrange("p (a c) -> p a c", a=4), [tb.b], [v_tm.b])
                                elif pre:
                                    TR([(tbv[0:8, 0:128], vs.t[:, 0:8], identb.t[:, :])], [vs.b, identb.b], [tb.b])
                                    CP("dve", v_tm.t[0:8, 9, j * 128:(j + 1) * 128], tbv[0:8, 0:128], [tb.b], [v_tm.b])
                                else:
                                    TR([(tbv[0:64, 0:128], vs.t[:, 0:64], identb.t[:, :]),
                                        (tbv[0:8, 128:256], vs.t[:, 64:72], identb.t[:, :])], [vs.b, identb.b], [tb.b])
                                    CP("dve", v_tm.t[0:64, 8, j * 128:(j + 1) * 128], tbv[0:64, 0:128], [tb.b], [v_tm.b])
                                    CP("dve", v_tm.t[0:8, 9, j * 128:(j + 1) * 128], tbv[0:8, 128:256], [tb.b], [v_tm.b])
                            return f

                        def r_evac(j):
                            def f(bk, c0, c1):
                                rt = rtmp()
                                n = c1 - c0
                                ACT(rt.t[:, 0:n], bk.t[:, 0:n], AF.Silu, [bk.b], [rt.b])
                                TSM(rgT.t[:, j, c0:c1], rt.t[:, 0:n], Pv("gnw", c=slice(j, j + 1)), [rt.b, pv.b], [rgT.b])
                            return f

                        def gla_front(h, L, col, blk):
                            tri16 = C("nbtri16" if blk else "ntri16", slice(0, L), slice(0, L))
                            r16 = C("nRb16" if blk else "nR16", slice(0, L), slice(0, L))
                            bk = nb()
                            MM([(bk.t[0:L, 0:256], gklT.t[0:16, col:col + L],
                                 Pv("wgk", slice(0, 16), slice(h * 256, h * 256 + 256)), True, False),
                                (bk.t[0:L, 0:256], C("ones", slice(0, 1), slice(0, L)),
                                 Pv("bgk", slice(0, 1), slice(h * 256, h * 256 + 256)), False, True)],
                               [gklT.b, pv.b, cm.b], [bk.b])
                            e1_, sp_ = e1(), spb()
                            ACT(e1_.t[0:L, :], bk.t[0:L, 0:256], AF.Exp, [bk.b], [e1_.b], scale=-1.0)
                            ACT(sp_.t[0:L, :], e1_.t[0:L, :], AF.Ln, [e1_.b], [sp_.b], bias=1.0)
                            b5 = nb()
                            b5v = b5.t[:].rearrange("p (a t) -> p a t", a=4)
                            MM([(b5v[:, kt, 0:L], sp_.t[0:L, kt * 128:(kt + 1) * 128], tri16, True, True) for kt in range(2)] +
                               [(b5v[:, 2 + kt, 0:L], sp_.t[0:L, kt * 128:(kt + 1) * 128], r16, True, True) for kt in range(2)],
                               [sp_.b, cm.b], [b5.b])
                            E4_ = E4()
                            ACT(E4_.t[:, :, 0:L], b5v[:, :, 0:L], AF.Exp, [b5.b], [E4_.b])
                            qp_ = kp_ = None
                            if not pre:
                                Ekn_ = Ekn()
                                ACT(Ekn_.t[:, :, 0:L], b5v[:, 0:2, 0:L], AF.Exp, [b5.b], [Ekn_.b], scale=-1.0)
                                qp_, kp_ = qp(), kp()
                                STT(qp_.t[:, :, 0:L], qT.t[:, :, col:col + L], 0.0625, E4_.t[:, 0:2, 0:L], ALU.mult, ALU.mult,
                                    [qT.b, E4_.b], [qp_.b])
                                TTo(kp_.t[:, :, 0:L], kT.t[:, :, col:col + L], Ekn_.t[:, :, 0:L], ALU.mult, [kT.b, Ekn_.b], [kp_.b])
                            kpp_ = kpp()
                            TTo(kpp_.t[:, :, 0:L], kT.t[:, :, col:col + L], E4_.t[:, 2:4, 0:L], ALU.mult, [kT.b, E4_.b], [kpp_.b])
                            tb = nb()
                            tbv = bf(tb)
                            TR([(tbv[0:L, kt * 128:(kt + 1) * 128], kpp_.t[:, kt, 0:L], identb.t[:, :]) for kt in range(2)],
                               [kpp_.b, identb.b], [tb.b])
                            kt_ = kpptm()
                            CP("act", kt_.t[0:L, :], tbv[0:L, 0:256], [tb.b], [kt_.b])
                            return E4_, qp_, kp_, kt_

                        def gla_out(h, L, col, ob, tile, off):
                            ss_, ss2_, rstd_, on_, mx = ss(), ss2(), rstd(), on(), mixst()
                            ACT(junk.t[0:L, :], ob.t[0:L, :], AF.Square, [ob.b], [junk.b, ss_.b], accum_out=ss_.t[0:L, 0:1])
                            RSTD(ss_, ss2_, rstd_, L, 1.0 / 512)
                            ACT(on_.t[0:L, :], ob.t[0:L, :], AF.Identity, [ob.b, rstd_.b], [on_.b], scale=rstd_.t[0:L, 0:1])
                            tb = nb()
                            tbv = bf(tb)
                            TR([(tbv[:, j * 128:j * 128 + L], on_.t[0:L, j * 128:(j + 1) * 128], identb.t[0:L, 0:L])
                                for j in range(4)], [on_.b, identb.b], [tb.b])
                            TTo(mx.t[:, :, 0:L], tbv[:, 0:512].rearrange("p (a c) -> p a c", a=4)[:, :, 0:L],
                                rgT.t[:, :, col:col + L], ALU.mult, [tb.b, rgT.b], [mx.b])
                            P.dma("sp", mix_scr[tile, :, 4 * h:4 * h + 4, off:off + L], mx.t[:, :, 0:L], mx.b,
                                  reads=[mx.b, mixb[tile]])

                        for h in range(4):
                            base = 1 + 12 * h
                            for j in range(2):
                                gemm_tile(wl(w_in_t[base + j]), 128, xbufs, chunks,
                                          lambda bk, c0, c1, j=j: CP("act", kT.t[:, j, c0:c1], bk.t[:, 0:c1 - c0], [bk.b], [kT.b]))
                            for j in range(4):
                                gemm_tile(wl(w_in_t[base + 2 + j]), 128, xbufs, chunks, v_evac(j))
                            if pre:
                                MEMSET(S.t[:], 0.0, [S.b])
                            else:
                                for j in range(2):
                                    gemm_tile(wl(w_in_t[base + 6 + j]), 128, xbufs, chunks,
                                              lambda bk, c0, c1, j=j: CP("act", qT.t[:, j, c0:c1], bk.t[:, 0:c1 - c0], [bk.b], [qT.b]))
                                for j in range(4):
                                    gemm_tile(wl(w_in_t[base + 8 + j]), 128, xbufs, chunks, r_evac(j))
                                P.dma("sp", S.t[:], S_scr[h], S.b, reads=[Sscrb[h]], writes=[S.b])
                                CP("act", Sbf.t[:], S.t[:], [S.b], [Sbf.b])
                            for (L, col, ci) in scan:
                                E4_, qp_, kp_, kt_ = gla_front(h, L, col, False)
                                if not pre:
                                    sbk = nb()
                                    MM([(sbk.t[0:L, 0:L], kp_.t[:, kt, 0:L], qp_.t[:, kt, 0:L], kt == 0, kt == 1) for kt in range(2)],
                                       [kp_.b, qp_.b], [sbk.b])
                                    AT_ = AT()
                                    TTo(AT_.t[0:L, 0:L], sbk.t[0:L, 0:L], C("tri", slice(0, L), slice(0, L)), ALU.mult,
                                        [sbk.b, cm.b], [AT_.b])
                                    ob = nb()
                                    MM([(ob.t[0:L, :], AT_.t[0:L, 0:L], v_tm.t[0:L, ci, :], True, False)] +
                                       [(ob.t[0:L, :], qp_.t[:, kt, 0:L], Sbf.t[:, kt, :], False, kt == 1) for kt in range(2)],
                                       [AT_.b, v_tm.b, qp_.b, Sbf.b], [ob.b])
                                    tile, off = (ci, 0) if ci < 8 else (8, 64)
                                    gla_out(h, L, col, ob, tile, off)
                                for kt in range(2):
                                    ub = nb()
                                    MM([(ub.t[:, :], kt_.t[0:L, kt * 128:(kt + 1) * 128], v_tm.t[0:L, ci, :], True, True)],
                                       [kt_.b, v_tm.b], [ub.b])
                                    STT(S.t[:, kt, :], S.t[:, kt, :], E4_.t[:, kt, L - 1:L], ub.t[:, :], ALU.mult, ALU.add,
                                        [S.b, E4_.b, ub.b], [S.b])
                                if not pre:
                                    CP("act", Sbf.t[:], S.t[:], [S.b], [Sbf.b])
                            if pre:
                                TSM(S.t[:], S.t[:], Pv("flag"), [S.b, pv.b], [S.b])
                                P.dma("sp", S_scr[h], S.t[:], S.b, reads=[S.b], writes=[Sscrb[h]])
                            else:
                                P.dma("sp", gla_p_o[h].rearrange("(kt p) v -> p kt v", p=128), S.t[:], S.b, reads=[S.b], is_out=True)
                                L, col, ci = 64, 1024, 8

                                def ld(i, h=h):
                                    s1, s2 = Sin(), Sinb()
                                    src = sgla[i, h].rearrange("(kt p) v -> p kt v", p=128)
                                    P.dma("sp", s1.t[:], src, s1.b, writes=[s1.b])
                                    P.dma("pool", s2.t[:], src, s2.b, writes=[s2.b])
                                    return s1, s2
                                lds = [ld(0)]
                                E4_, qp_, kp_, kt_ = gla_front(h, L, col, True)
                                sbk = nb()
                                MM([(sbk.t[0:L, 0:L], kp_.t[:, kt, 0:L], qp_.t[:, kt, 0:L], kt == 0, kt == 1) for kt in range(2)],
                                   [kp_.b, qp_.b], [sbk.b])
                                AT_ = AT()
                                TTo(AT_.t[0:L, 0:L], sbk.t[0:L, 0:L], C("btri", slice(0, L)), ALU.mult, [sbk.b, cm.b], [AT_.b])
                                ob = acc_bank
                                MM([(ob.t[0:L, :], AT_.t[0:L, 0:L], v_tm.t[0:L, ci, :], True, False)], [AT_.b, v_tm.b], [ob.b])
                                for i in range(16):
                                    if i + 1 < 16:
                                        lds.append(ld(i + 1))
                                    s1, s2 = lds[i]
                                    qm, km = qpm(), kppm()
                                    TTo(qm.t[:, :, :], qp_.t[:, :, 0:64],
                                        C("colmask", c=slice(64 * i, 64 * i + 64)).unsqueeze(1).broadcast_to([128, 2, 64]),
                                        ALU.mult, [qp_.b, cm.b], [qm.b])
                                    TSM(km.t[0:64, :], kt_.t[0:64, :], C("seqmask", slice(0, 64), slice(i, i + 1)), [kt_.b, cm.b], [km.b])
                                    MM([(ob.t[0:L, :], qm.t[:, kt, :], s2.t[:, kt, :], False, (i == 15 and kt == 1))
                                        for kt in range(2)], [qm.b, s2.b], [ob.b])
                                    so = s1
                                    for kt in range(2):
                                        ub = nb()
                                        MM([(ub.t[:, :], km.t[0:64, kt * 128:(kt + 1) * 128], v_tm.t[0:64, ci, :], True, True)],
                                           [km.b, v_tm.b], [ub.b])
                                        STT(so.t[:, kt, :], s1.t[:, kt, :], E4_.t[:, kt, 4 * i + 3:4 * i + 4], ub.t[:, :], ALU.mult,
                                            ALU.add, [E4_.b, ub.b], [so.b])
                                    P.dma("sp", gla_s_o[i, h].rearrange("(kt p) v -> p kt v", p=128), so.t[:], so.b,
                                          reads=[so.b], is_out=True)
                                gla_out(h, L, col, ob, 8, 0)

                    DBG = 9.0
                    if stage >= 2 and (pre or DBG >= 2.1):
                      with Scope() as pq:
                        xbcT = sb("xbcT", [128, 4, NTX], F32, pq)
                        acc = sb("acc", [128, NTX], F32, pq)
                        xcp = sb("xcp", [128, 4, NPRE], BF16, pq)
                        xcs = sb("xcs", [128, 4, 64], BF16, pq)
                        zs_tm = sb("zs_tm", [128, 10, 256], BF16, pq)
                        zstage = ring("zstage", [128, 512], BF16, 2, pq)
                        R1 = ring("R1", [128, 4, 128], F32, 1, pq)
                        R2 = ring("R2", [128, 4, 128], F32, 1, pq)
                        Lm = ring("Lm", [128, 4, 128], F32, 2, pq)
                        MT = ring("MT", [128, 4, 128], BF16, 2, pq)
                        xpr = ring("xpr", [128, 256], BF16, 2, pq)
                        xpp = ring("xpp", [128, 256], BF16, 2, pq)
                        Btm = ring("Btm", [128, 128], BF16, 2, pq)
                        wend = ring("wend", [128, 4], F32, 2, pq)
                        etm = ring("etm", [128, 4], F32, 2, pq)
                        El = ring("El", [128, 4], F32, 2, pq)
                        t0 = ring("t0", [128, 256], F32, 2, pq)
                        t1 = ring("t1", [128, 256], F32, 2, pq)
                        ssd_ss = ring("sss", [128, 1], F32, 2, pq)
                        ssd_s2 = ring("sss2", [128, 1], F32, 2, pq)
                        ssd_rs = ring("ssrs", [128, 1], F32, 2, pq)
                        junk2 = sb("junk2", [128, 256], BF16, pq)
                        yn = ring("yn", [128, 256], BF16, 2, pq)
                        mixs2 = ring("mixs2", [128, 2, 128], BF16, 2, pq)
                        hT = sb("hTw", [128, 256], F32, pq)
                        hTb = sb("hTb", [128, 256], BF16, pq)
                        htmp = sb("htmp", [128, 256], F32, pq)
                        cstg = sb("cstg", [128, 51], F32, pq)
                        cout = ring("cout", [51, 128], F32, 2, pq)
                        if not pre:
                            convT_all = sb("convT_all", [128, 32, 48], F32, pq)
                            with Scope() as s0:
                                cst = ring("cst", [48, 512], F32, 2, s0)
                                for q in range(8):
                                    cs_ = cst()
                                    P.dma("sp", cs_.t[:], sconv[:, q * 512:(q + 1) * 512], cs_.b, writes=[cs_.b])
                                    bk = nb()
                                    TR([(bk.t[:, j * 48:(j + 1) * 48], cs_.t[0:48, j * 128:(j + 1) * 128], identf[0:48, 0:48])
                                        for j in range(4)], [cs_.b, identF.b], [bk.b])
                                    CP("act", convT_all.t[:, 4 * q:4 * q + 4, :], bk.t[:, 0:192].rearrange("p (a c) -> p a c", a=4),
                                       [bk.b], [convT_all.b])
                            hnat = ring("hnat", [128, 2, 128], F32, 2, pq)
                            hTbi = ring("hTbi", [128, 256], BF16, 2, pq)
                            hout = ring("hout", [128, 2, 128], F32, 2, pq)
                            Cm_ = ring("Cm", [128, 64], BF16, 2, pq)
                            xpm = ring("xpm", [64, 256], BF16, 2, pq)
                            dta_e = sb("dta_e", [64, 2, 128], F32, pq)
                            dec = sb("dec", [128, 2, 16], F32, pq)

                        def xbc_evac(q):
                            def f(bk, c0, c1):
                                if pre or c1 <= 1024:
                                    CP("act", xbcT.t[:, q, 3 + c0:3 + c1], bk.t[:, 0:c1 - c0], [bk.b], [xbcT.b])
                                else:
                                    CP("act", xbcT.t[:, q, 1035:1147].rearrange("p (s w) -> p s w", w=7)[:, :, 3:7],
                                       bk.t[:, 0:64].rearrange("p (s w) -> p s w", w=4), [bk.b], [xbcT.b])
                                    CP("act", xbcT.t[:, q, 1027:1035], bk.t[:, 64:72], [bk.b], [xbcT.b])
                            return f

                        def z_evac(j):
                            def f(bk, c0, c1):
                                vs = zstage()
                                n = c1 - c0
                                ACT(vs.t[:, 0:n], bk.t[:, 0:n], AF.Silu, [bk.b], [vs.b])
                                tb = nb()
                                tbv = bf(tb)
                                if n == 512:
                                    TR([(tbv[:, a * 128:(a + 1) * 128], vs.t[:, a * 128:(a + 1) * 128], identb.t[:, :])
                                        for a in range(4)], [vs.b, identb.b], [tb.b])
                                    ci0 = c0 // 128
                                    CP("dve", zs_tm.t[:, ci0:ci0 + 4, j * 128:(j + 1) * 128],
                                       tbv[:, 0:512].rearrange("p (a c) -> p a c", a=4), [tb.b], [zs_tm.b])
                                else:
                                    TR([(tbv[0:64, 0:128], vs.t[:, 0:64], identb.t[:, :]),
                                        (tbv[0:8, 128:256], vs.t[:, 64:72], identb.t[:, :])], [vs.b, identb.b], [tb.b])
                                    CP("dve", zs_tm.t[0:64, 8, j * 128:(j + 1) * 128], tbv[0:64, 0:128], [tb.b], [zs_tm.b])
                                    CP("dve", zs_tm.t[0:8, 9, j * 128:(j + 1) * 128], tbv[0:8, 128:256], [tb.b], [zs_tm.b])
                            return f

                        def conv_tile(q, gt):
                            npr = NPRE
                            ncol = (3 + npr) if pre else NTX
                            if pre:
                                MEMSET(xbcT.t[:, q, 0:3], 0.0, [xbcT.b])
                            else:
                                CP("act", xbcT.t[:, q, 0:3], halo_all.t[:, gt, :], [halo_all.b], [xbcT.b])
                                CP("act", xbcT.t[:, q, 1035:1147].rearrange("p (s w) -> p s w", w=7)[:, :, 0:3],
                                   convT_all.t[:, gt, :].rearrange("p (s w) -> p s w", w=3), [convT_all.b], [xbcT.b])
                            m = ncol - 3
                            ACT(acc.t[:, 0:m], xbcT.t[:, q, 3:ncol], AF.Identity, [xbcT.b, pv.b], [acc.b],
                                scale=Pv("convw", c=slice(4 * gt + 3, 4 * gt + 4)), bias=Pv("convb", c=slice(gt, gt + 1)))
                            for jj in range(3):
                                STT(acc.t[:, 0:m], xbcT.t[:, q, jj:jj + m], Pv("convw", c=slice(4 * gt + jj, 4 * gt + jj + 1)),
                                    acc.t[:, 0:m], ALU.mult, ALU.add, [xbcT.b, pv.b, acc.b], [acc.b])
                            ACT(xcp.t[:, q, 0:npr], acc.t[:, 0:npr], AF.Silu, [acc.b], [xcp.b])
                            if not pre:
                                ACT(xcs.t[:, q, :].rearrange("p (s w) -> p s w", w=4),
                                    acc.t[:, 1032:1144].rearrange("p (s w) -> p s w", w=7)[:, :, 3:7], AF.Silu, [acc.b], [xcs.b])
                                CP("act", cstg.t[:, 0:48].rearrange("p (s w) -> p s w", w=3),
                                   xbcT.t[:, q, 1035:1147].rearrange("p (s w) -> p s w", w=7)[:, :, 4:7], [xbcT.b], [cstg.b])
                                CP("act", cstg.t[:, 48:51], xbcT.t[:, q, 3 + 1029:3 + 1032], [xbcT.b], [cstg.b])
                                bk = nb()
                                TR([(bk.t[0:51, 0:128], cstg.t[:, 0:51], identf)], [cstg.b, identF.b], [bk.b])
                                co = cout()
                                CP("dve", co.t[:, :], bk.t[0:51, 0:128], [bk.b], [co.b])
                                P.dma("sp", conv_s_o[:, gt * 128:(gt + 1) * 128], co.t[0:48, :], co.b, reads=[co.b], is_out=True)
                                P.dma("sp", conv_p_o[:, gt * 128:(gt + 1) * 128], co.t[48:51, :], co.b, reads=[co.b], is_out=True)
                            else:
                                TSM(halo_all.t[:, gt, :], xbcT.t[:, q, 3 + 1029:3 + 1032], Pv("flag"), [xbcT.b, pv.b], [halo_all.b])

                        def ssd_front(g, L, xsrc, c0, ci, blk):
                            dta_c = dta_tm.t[0:L, ci, 4 * g:4 * g + 4]
                            dt_c = dt_tm.t[0:L, ci, 4 * g:4 * g + 4]
                            Lm_ = None
                            if not pre:
                                R1_, R2_ = R1(), R2()
                                TTo(R1_.t[0:L, :, 0:L], dta_c.unsqueeze(2).broadcast_to([L, 4, L]),
                                    C("btri" if blk else "tri", slice(0, L), slice(0, L)).unsqueeze(1).broadcast_to([L, 4, L]),
                                    ALU.mult, [dta_tm.b, cm.b], [R1_.b])
                                CP("dve", R2_.t[0:L, :, 0:L], dta_c.unsqueeze(2).broadcast_to([L, 4, L]), [dta_tm.b], [R2_.b])
                                bD = nb()
                                bDv = bD.t[0:L, 0:4 * L].rearrange("p (a t) -> p a t", a=4)
                                nm = C("negbmask4" if blk else "negmask4", slice(0, L)).rearrange("p (a t) -> p a t", a=4)[:, :, 0:L]
                                MM([(bDv, C("ones", slice(0, L), slice(0, L)), R1_.t[0:L, :, 0:L], True, False),
                                    (bDv, C("negbtri" if blk else "negtri", slice(0, L), slice(0, L)), R2_.t[0:L, :, 0:L], False, False),
                                    (bDv, identf[0:L, 0:L], nm, False, True)], [R1_.b, R2_.b, cm.b, identF.b], [bD.b])
                                Lm_ = Lm()
                                ACT(Lm_.t[0:L, :, 0:L], bDv, AF.Exp, [bD.b], [Lm_.b])
                            bw = nb()
                            MM([(bw.t[0:L, 0:32], C("Rbs" if blk else "Rs", slice(0, L), slice(0, L)), dta_tm.t[0:L, ci, :], True, True)],
                               [dta_tm.b, cm.b], [bw.b])
                            we = wend()
                            ACT(we.t[0:L, :], bw.t[0:L, 4 * g:4 * g + 4], AF.Exp, [bw.b], [we.b])
                            tb = nb()
                            tbv = bf(tb)
                            TR([(tbv[0:L, a * 128:(a + 1) * 128], xsrc.t[:, a, c0:c0 + L], identb.t[:, :]) for a in range(3)],
                               [xsrc.b, identb.b], [tb.b])
                            xpr_, xpp_, Btm_ = xpr(), xpp(), Btm()
                            TTo(xpr_.t[0:L, :].rearrange("p (j c) -> p j c", j=4), tbv[0:L, 0:256].rearrange("p (j c) -> p j c", j=4),
                                dt_c.unsqueeze(2).broadcast_to([L, 4, 64]), ALU.mult, [tb.b, dt_tm.b], [xpr_.b])
                            CP("dve", Btm_.t[0:L, :], tbv[0:L, 256:384], [tb.b], [Btm_.b])
                            TTo(xpp_.t[0:L, :].rearrange("p (j c) -> p j c", j=4), xpr_.t[0:L, :].rearrange("p (j c) -> p j c", j=4),
                                we.t[0:L, :].unsqueeze(2).broadcast_to([L, 4, 64]), ALU.mult, [xpr_.b, we.b], [xpp_.b])
                            return Lm_, xpr_, xpp_, Btm_, dta_c

                        def ssd_y(g, L, xsrc, c0, Lm_, xpr_, blk):
                            bcb = nb()
                            MM([(bcb.t[0:L, 0:L], xsrc.t[:, 2, c0:c0 + L], xsrc.t[:, 3, c0:c0 + L], True, True)], [xsrc.b], [bcb.b])
                            MT_ = MT()
                            TTo(MT_.t[0:L, :, 0:L], bcb.t[0:L, 0:L].unsqueeze(1).broadcast_to([L, 4, L]), Lm_.t[0:L, :, 0:L], ALU.mult,
                                [bcb.b, Lm_.b], [MT_.b])
                            by = acc_bank if blk else nb()
                            lst = []
                            for j in range(4):
                                xt = 2 * g + j // 2
                                lst.append((by.t[0:L, 64 * j:64 * j + 64], MT_.t[0:L, j, 0:L], xpr_.t[0:L, 64 * j:64 * j + 64], True, False))
                                lst.append((by.t[0:L, 64 * j:64 * j + 64], xsrc.t[:, j // 2, c0:c0 + L],
                                            diagD.t[:, xt, 64 * (j % 2):64 * (j % 2) + 64], False, True))
                            MM(lst, [MT_.b, xpr_.b, xsrc.b, diagD.b], [by.b])
                            return by

                        def ssd_out(g, L, ci, by, dta_c, blk, tile, off):
                            bc = nb()
                            MM([(bc.t[0:L, 0:32], C("btri" if blk else "tri", slice(0, L), slice(0, L)), dta_tm.t[0:L, ci, :], True, True)],
                               [dta_tm.b, cm.b], [bc.b])
                            et = etm()
                            ACT(et.t[0:L, :], bc.t[0:L, 4 * g:4 * g + 4], AF.Exp, [bc.b], [et.b])
                            t0_, t1_ = t0(), t1()
                            TTo(t0_.t[0:L, :].rearrange("p (j c) -> p j c", j=4), by.t[0:L, 256:512].rearrange("p (j c) -> p j c", j=4),
                                et.t[0:L, :].unsqueeze(2).broadcast_to([L, 4, 64]), ALU.mult, [by.b, et.b], [t0_.b])
                            TTo(t1_.t[0:L, :], by.t[0:L, 0:256], t0_.t[0:L, :], ALU.add, [by.b, t0_.b], [t1_.b])
                            TTo(t1_.t[0:L, :], t1_.t[0:L, :], zs_tm.t[0:L, ci, :], ALU.mult, [t1_.b, zs_tm.b], [t1_.b])
                            s_, s2_, rs_ = ssd_ss(), ssd_s2(), ssd_rs()
                            ACT(junk2.t[0:L, :], t1_.t[0:L, :], AF.Square, [t1_.b], [junk2.b, s_.b], accum_out=s_.t[0:L, 0:1])
                            RSTD(s_, s2_, rs_, L, 1.0 / 256)
                            yn_ = yn()
                            ACT(yn_.t[0:L, :], t1_.t[0:L, :], AF.Identity, [t1_.b, rs_.b], [yn_.b], scale=rs_.t[0:L, 0:1])
                            tb = nb()
                            tbv = bf(tb)
                            TR([(tbv[:, a * 128:a * 128 + L], yn_.t[0:L, a * 128:(a + 1) * 128], identb.t[0:L, 0:L]) for a in range(2)],
                               [yn_.b, identb.b], [tb.b])
                            mx = mixs2()
                            TTo(mx.t[:, :, 0:L], tbv[:, 0:256].rearrange("p (a c) -> p a c", a=2)[:, :, 0:L],
                                Pv("snw", c=slice(2 * g, 2 * g + 2)).unsqueeze(2).broadcast_to([128, 2, L]), ALU.mult,
                                [tb.b, pv.b], [mx.b])
                            P.dma("sp", mix_scr[tile, :, 16 + 2 * g:16 + 2 * g + 2, off:off + L], mx.t[:, :, 0:L], mx.b,
                                  reads=[mx.b, mixb[tile]])

                        for g in range(8):
                            base = 49 + 6 * g
                            gts = [2 * g, 2 * g + 1, 16 + g, 24 + g]
                            for q in range(3):
                                gemm_tile(wl(w_in_t[base + q]), 128, xbufs, chunks, xbc_evac(q))
                                if DBG >= 1.7:
                                    conv_tile(q, gts[q])
                            if pre and DBG < 1.8:
                                MEMSET(hT.t[:], 0.0, [hT.b])
                            elif pre:
                                gemm_tile(wl(w_in_t[base + 3]), 128, xbufs, [(1024, 1032)],
                                          lambda bk, c0, c1: TSM(halo_all.t[:, 24 + g, :], bk.t[:, 5:8], Pv("flag"), [bk.b, pv.b], [halo_all.b]))
                                MEMSET(hT.t[:], 0.0, [hT.b])
                            else:
                                gemm_tile(wl(w_in_t[base + 3]), 128, xbufs, chunks, xbc_evac(3))
                                conv_tile(3, gts[3])
                                for j in range(2):
                                    gemm_tile(wl(w_in_t[base + 4 + j]), 128, xbufs, chunks, z_evac(j))
                                P.dma("sp", hT.t[:], hT_scr[g], hT.b, reads=[hscrb[g]], writes=[hT.b])
                                CP("act", hTb.t[:], hT.t[:], [hT.b], [hTb.b])
                            for (L, col, ci) in (scan if ((pre and DBG >= 1.9) or DBG >= 2.2) else []):
                                pc = 128 * ci if ci < 8 else 1024
                                Lm_, xpr_, xpp_, Btm_, dta_c = ssd_front(g, L, xcp, pc, ci, False)
                                if not pre:
                                    by = ssd_y(g, L, xcp, pc, Lm_, xpr_, False)
                                    MM([(by.t[0:L, 256:512], xcp.t[:, 3, pc:pc + L], hTb.t[:, :], True, True)], [xcp.b, hTb.b], [by.b])
                                    tile, off = (ci, 0) if ci < 8 else (8, 64)
                                    ssd_out(g, L, ci, by, dta_c, False, tile, off)
                                bcl = nb()
                                MM([(bcl.t[:, 0:32], C("ones", slice(0, L), slice(0, 128)), dta_tm.t[0:L, ci, :], True, True)], [dta_tm.b, cm.b], [bcl.b])
                                El_ = El()
                                ACT(El_.t[:, :], bcl.t[:, 4 * g:4 * g + 4], AF.Exp, [bcl.b], [El_.b])
                                bh = nb()
                                MM([(bh.t[:, 0:256], Btm_.t[0:L, :], xpp_.t[0:L, :], True, True)], [Btm_.b, xpp_.b], [bh.b])
                                TTo(htmp.t[:, :].rearrange("p (j c) -> p j c", j=4), hT.t[:, :].rearrange("p (j c) -> p j c", j=4),
                                    El_.t[:, :].unsqueeze(2).broadcast_to([128, 4, 64]), ALU.mult, [hT.b, El_.b], [htmp.b])
                                TTo(hT.t[:, :], htmp.t[:, :], bh.t[:, 0:256], ALU.add, [htmp.b, bh.b], [hT.b])
                                if not pre:
                                    CP("act", hTb.t[:], hT.t[:], [hT.b], [hTb.b])
                            if pre:
                                TSM(hT.t[:], hT.t[:], Pv("flag"), [hT.b, pv.b], [hT.b])
                                P.dma("sp", hT_scr[g], hT.t[:], hT.b, reads=[hT.b], writes=[hscrb[g]])
                            elif DBG >= 2.3:
                                bk = nb()
                                TR([(bk.t[:, a * 128:(a + 1) * 128], hT.t[:, a * 128:(a + 1) * 128], identf) for a in range(2)],
                                   [hT.b, identF.b], [bk.b])
                                ho = hout()
                                CP("dve", ho.t[:, :, :], bk.t[:, 0:256].rearrange("p (a c) -> p a c", a=2), [bk.b], [ho.b])
                                P.dma("sp", ssm_p_o[4 * g:4 * g + 4].rearrange("(t jj) p n -> (jj p) t n", jj=2), ho.t[:], ho.b,
                                      reads=[ho.b], is_out=True)
                                L, ci = 64, 8

                                def ldh(i, g=g):
                                    hn = hnat()
                                    P.dma("sp", hn.t[:], sssm[i, 4 * g:4 * g + 4].rearrange("(t jj) p n -> (jj p) t n", jj=2), hn.b,
                                          writes=[hn.b])
                                    return hn
                                lds = [ldh(0)]
                                Lm_, xpr_, xpp_, Btm_, dta_c = ssd_front(g, L, xcs, 0, ci, True)
                                by = ssd_y(g, L, xcs, 0, Lm_, xpr_, True)
                                CP("dve", dta_e.t[:, :, :].rearrange("p t (jj c) -> p t jj c", jj=2),
                                   dta_c.rearrange("p (t jj) -> p t jj", t=2).unsqueeze(3).broadcast_to([64, 2, 2, 64]),
                                   [dta_tm.b], [dta_e.b])
                                bdec = nb()
                                MM([(bdec.t[:, a * 16:(a + 1) * 16], dta_e.t[0:64, a, :], C("seqmask", slice(0, 64)), True, True)
                                    for a in range(2)], [dta_e.b, cm.b], [bdec.b])
                                ACT(dec.t[:, :, :], bdec.t[:, 0:32].rearrange("p (a s) -> p a s", a=2), AF.Exp, [bdec.b], [dec.b])
                                for i in range(16):
                                    if i + 1 < 16:
                                        lds.append(ldh(i + 1))
                                    hn = lds[i]
                                    bt = nb()
                                    TR([(bt.t[:, a * 128:(a + 1) * 128], hn.t[:, a, :], identf) for a in range(2)], [hn.b, identF.b], [bt.b])
                                    hb = hTbi()
                                    CP("act", hb.t[:, :], bt.t[:, 0:256], [bt.b], [hb.b])
                                    cmk, xm_ = Cm_(), xpm()
                                    TTo(cmk.t[:, :], xcs.t[:, 3, :], C("colmask", c=slice(64 * i, 64 * i + 64)), ALU.mult, [xcs.b, cm.b], [cmk.b])
                                    MM([(by.t[0:L, 256:512], cmk.t[:, :], hb.t[:, :], i == 0, i == 15)], [cmk.b, hb.b], [by.b])
                                    TSM(xm_.t[0:64, :], xpp_.t[0:64, :], C("seqmask", slice(0, 64), slice(i, i + 1)), [xpp_.b, cm.b], [xm_.b])
                                    bh = nb()
                                    MM([(bh.t[:, a * 128:(a + 1) * 128], xm_.t[0:64, a * 128:(a + 1) * 128], Btm_.t[0:64, :], True, True)
                                        for a in range(2)], [xm_.b, Btm_.b], [bh.b])
                                    ho = hout()
                                    for a in range(2):
                                        STT(ho.t[:, a, :], hn.t[:, a, :], dec.t[:, a, i:i + 1], bh.t[:, a * 128:(a + 1) * 128], ALU.mult, ALU.add,
                                            [hn.b, dec.b, bh.b], [ho.b])
                                    P.dma("sp", ssm_s_o[i, 4 * g:4 * g + 4].rearrange("(t jj) p n -> (jj p) t n", jj=2), ho.t[:], ho.b,
                                          reads=[ho.b], is_out=True)
                                ssd_out(g, L, ci, by, dta_c, True, 8, 0)

        if stage >= 3:
            groups = [([0, 1, 2, 3], 512), ([4, 5, 6, 7, 8], 584)]
            with Scope() as pc:
                facc = sb("facc", [128, 5, 4096], F32, pc)
                faccb = [Buf(f"facc{t}") for t in range(5)]
                pc.bufs.extend(faccb)
                junkL = sb("junkL", [128, 4096], BF16, pc)
                st_ = {k: ring(k, [128, 1], F32, 2, pc) for k in ["s1", "s2", "mean", "msq", "var", "sd", "rs", "nmr"]}
                for gi, (tiles, ntg) in enumerate(groups):
                    tn = [(128 if t < 8 else 72) for t in tiles]
                    tchunks = [(0, 512)] if ntg == 512 else [(0, 292), (292, 584)]
                    for ti, t in enumerate(tiles):
                        P.dma("sp", facc.t[0:tn[ti], ti, :], xm[t * 128:t * 128 + tn[ti], :], faccb[ti], writes=[faccb[ti]])
                    with Scope() as pcc:
                        mixT = sb("mixT", [128, 32, 584], BF16, pcc)
                        ostage = sb("ostage", [128, 4, 584], F32, pcc)
                        for ti, t in enumerate(tiles):
                            P.dma("sp", mixT.t[:, :, ti * 128:ti * 128 + tn[ti]], mix_scr[t, :, :, 0:tn[ti]], mixT.b,
                                  writes=[mixT.b, mixb[t]])
                        for nq in range(8):
                            for a in range(4):
                                slot = wl(w_out_t[4 * nq + a])
                                for (c0, c1) in tchunks:
                                    bk = nb()
                                    MM([(bk.t[:, 0:c1 - c0], slot.t[:, ft, :], mixT.t[:, ft, c0:c1], ft == 0, ft == 31)
                                        for ft in range(32)], [slot.b, mixT.b], [bk.b])
                                    CP("act", ostage.t[:, a, c0:c1], bk.t[:, 0:c1 - c0], [bk.b], [ostage.b])
                            for ti in range(len(tiles)):
                                n = tn[ti]
                                bk = nb()
                                TR([(bk.t[0:n, a * 128:(a + 1) * 128], ostage.t[:, a, ti * 128:ti * 128 + n], identf) for a in range(4)],
                                   [ostage.b, identF.b], [bk.b])
                                STT(facc.t[0:n, ti, nq * 512:(nq + 1) * 512], facc.t[0:n, ti, nq * 512:(nq + 1) * 512], ALPHA,
                                    bk.t[0:n, :], ALU.mult, ALU.add, [faccb[ti], bk.b], [faccb[ti]])

                    def layer_norm(ti, n, lnc, scale):
                        fa = facc.t[0:n, ti, :]
                        s1, s2, mean, msq, var, sd, rs, nmr = [st_[k]() for k in ["s1", "s2", "mean", "msq", "var", "sd", "rs", "nmr"]]
                        ACT(junkL.t[0:n, :], fa, AF.Identity, [faccb[ti]], [junkL.b, s1.b], accum_out=s1.t[0:n, :])
                        ACT(junkL.t[0:n, :], fa, AF.Square, [faccb[ti]], [junkL.b, s2.b], accum_out=s2.t[0:n, :])
                        TSM(mean.t[0:n, :], s1.t[0:n, :], 1.0 / 4096, [s1.b], [mean.b])
                        TTo(msq.t[0:n, :], mean.t[0:n, :], mean.t[0:n, :], ALU.mult, [mean.b], [msq.b])
                        STT(var.t[0:n, :], s2.t[0:n, :], 1.0 / 4096, msq.t[0:n, :], ALU.mult, ALU.subtract, [s2.b, msq.b], [var.b])
                        ACT(sd.t[0:n, :], var.t[0:n, :], AF.Sqrt, [var.b, epsc.b], [sd.b], bias=epsc.t[0:n, :])
                        P.op("dve", lambda e: e.reciprocal(out=rs.t[0:n, :], in_=sd.t[0:n, :]), [sd.b], [rs.b])
                        if scale != 1.0:
                            TSM(rs.t[0:n, :], rs.t[0:n, :], scale, [rs.b], [rs.b])
                        STT(nmr.t[0:n, :], mean.t[0:n, :], -1.0, rs.t[0:n, :], ALU.mult, ALU.mult, [mean.b, rs.b], [nmr.b])
                        ACT(fa, fa, AF.Identity, [faccb[ti], rs.b, nmr.b], [faccb[ti]], scale=rs.t[0:n, :], bias=nmr.t[0:n, :])
                        TTo(fa, fa, lnc.t[0:n, 0, :], ALU.mult, [faccb[ti], lnc.b], [faccb[ti]])
                        TTo(fa, fa, lnc.t[0:n, 1, :], ALU.add, [faccb[ti], lnc.b], [faccb[ti]])

                    with Scope() as pd:
                        hTg = sb("hTg", [128, 32, 584], BF16, pd)
                        with Scope() as pl:
                            lnc = sb("lnc", [128, 2, 4096], F32, pl)
                            P.dma("sp", lnc.t[:, 0, :], lnp[0].partition_broadcast(128), lnc.b, writes=[lnc.b])
                            P.dma("sp", lnc.t[:, 1, :], lnp[1].partition_broadcast(128), lnc.b, writes=[lnc.b])
                            TSM(lnc.t[:, 1, :], lnc.t[:, 1, :], ALPHA, [lnc.b], [lnc.b])
                            for ti in range(len(tiles)):
                                n = tn[ti]
                                layer_norm(ti, n, lnc, ALPHA)
                                for q in range(8):
                                    bk = nb()
                                    TR([(bk.t[:, a * 128:a * 128 + n], facc.t[0:n, ti, (4 * q + a) * 128:(4 * q + a + 1) * 128],
                                         identf[0:n, 0:n]) for a in range(4)], [faccb[ti], identF.b], [bk.b])
                                    src = bk.t[:].rearrange("p (a t) -> p a t", a=4)[:, :, 0:n]
                                    dst = hTg.t[:, 4 * q:4 * q + 4, ti * 128:ti * 128 + n]
                                    if q % 2 == 0:
                                        ACT(dst, src, AF.Identity, [bk.b], [hTg.b], scale=1.0 / ALPHA)
                                    else:
                                        TSM(dst, src, 1.0 / ALPHA, [bk.b], [hTg.b])
                        with Scope() as pm:
                            h1T = ring("h1T", [128, 8, 584], BF16, 2, pm)
                            wdn = ring("wdn", [128, 8, 512], BF16, 2, pm)
                            rl = ring("rl", [128, 512], F32, 2, pm)
                            for sc in range(16):
                                h1 = h1T()
                                for hk in range(8):
                                    slot = wl(w_up_t[sc * 8 + hk])
                                    for (c0, c1) in tchunks:
                                        bk = nb()
                                        MM([(bk.t[:, 0:c1 - c0], slot.t[:, kt, :], hTg.t[:, kt, c0:c1], kt == 0, kt == 31)
                                            for kt in range(32)], [slot.b, hTg.b], [bk.b])
                                        r_ = rl()
                                        ACT(r_.t[:, 0:c1 - c0], bk.t[:, 0:c1 - c0], AF.Relu, [bk.b], [r_.b])
                                        TTo(h1.t[:, hk, c0:c1], r_.t[:, 0:c1 - c0], r_.t[:, 0:c1 - c0], ALU.mult, [r_.b], [h1.b])
                                for ng in range(8):
                                    wd = wdn()
                                    P.dma("pool", wd.t[:], w_down[sc * 1024:(sc + 1) * 1024, ng * 512:(ng + 1) * 512]
                                          .rearrange("(hk p) c -> p hk c", p=128), wd.b, writes=[wd.b])
                                    for ti in range(len(tiles)):
                                        n = tn[ti]
                                        bk = nb()
                                        MM([(bk.t[0:n, :], h1.t[:, hk, ti * 128:ti * 128 + n], wd.t[:, hk, :], hk == 0, hk == 7)
                                            for hk in range(8)], [h1.b, wd.b], [bk.b])
                                        TTo(facc.t[0:n, ti, ng * 512:(ng + 1) * 512], facc.t[0:n, ti, ng * 512:(ng + 1) * 512],
                                            bk.t[0:n, :], ALU.add, [faccb[ti], bk.b], [faccb[ti]])
                    with Scope() as pl2:
                        lnc2 = sb("lnc2", [128, 2, 4096], F32, pl2)
                        P.dma("sp", lnc2.t[:, 0, :], lnp[2].partition_broadcast(128), lnc2.b, writes=[lnc2.b])
                        P.dma("sp", lnc2.t[:, 1, :], lnp[3].partition_broadcast(128), lnc2.b, writes=[lnc2.b])
                        for ti, t in enumerate(tiles):
                            n = tn[ti]
                            layer_norm(ti, n, lnc2, 1.0)
                            P.dma("sp", y_o[t * 128:t * 128 + n, :], facc.t[0:n, ti, :], faccb[ti], reads=[faccb[ti]], is_out=True)

        P.finish()
        P.emit()
    return nc


def _cmask():
    m = np.zeros((128, NCM), np.float32)
    s = np.arange(128)[:, None]
    t = np.arange(128)[None, :]
    tri = (s <= t).astype(np.float32)
    R = (s > t).astype(np.float32)

    def put(n, a):
        a0, a1 = CM[n]
        m[:a.shape[0], a0:a0 + a.shape[1]] = a
    put("ident", np.eye(128, dtype=np.float32))
    put("tri", tri)
    put("ntri16", -tri / 16.0)
    put("nR16", -R / 16.0)
    put("negtri", -tri)
    put("ones", np.ones((128, 128), np.float32))
    put("negmask4", np.tile(NEG * R, (1, 4)))
    put("Rs", R)
    s6 = np.arange(64)[:, None]
    t6 = np.arange(64)[None, :]
    same = (s6 // 4 == t6 // 4)
    btri = (same & (s6 <= t6)).astype(np.float32)
    Rb = (same & (s6 > t6)).astype(np.float32)
    put("btri", btri)
    put("nbtri16", -btri / 16.0)
    put("nRb16", -Rb / 16.0)
    put("negbtri", -btri)
    put("negbmask4", np.tile(NEG * (1.0 - btri), (1, 4)))
    put("Rbs", Rb)
    put("seqmask", (s6 // 4 == np.arange(16)[None, :]).astype(np.float32))
    col = (np.arange(16)[:, None] == (np.arange(64)[None, :] // 4)).astype(np.float32).reshape(1, 1024)
    put("colmask", np.broadcast_to(col, (128, 1024)))
    return m


def _pvec(inp, flag):
    m = np.zeros((128, NPV), np.float32)

    def put(n, a):
        a0, a1 = PV[n]
        m[:a.shape[0], a0:a0 + a.shape[1]] = a
    put("dtb", np.broadcast_to(inp["dt_bias"][0][None, :], (128, 32)))
    put("alog", np.broadcast_to(inp["a_log"][0][None, :], (128, 32)))
    put("bgk", inp["b_gk"][0][None, :])
    put("wgk", inp["w_gk_up"][0])
    put("gnw", inp["gla_norm_w"][0].reshape(4, 128).T)
    put("convw", inp["conv_w"][0].reshape(4, 32, 128).transpose(2, 1, 0).reshape(128, 128))
    put("convb", inp["conv_b"][0].reshape(32, 128).T)
    put("dsk", np.repeat(inp["d_skip"][0].reshape(16, 2), 64, axis=1).T)
    put("snw", inp["ssd_norm_w"][0].reshape(16, 128).T)
    put("flag", np.full((128, 1), flag, np.float32))
    return m


def _win_cols():
    cols = -np.ones((97, 128), np.int64)
    cols[0, 0:16] = np.arange(6144, 6160)
    cols[0, 32:64] = np.arange(12304, 12336)
    r = np.arange(128)
    for h in range(4):
        b = 1 + 12 * h
        for j in range(2):
            cols[b + j] = 1024 + h * 256 + j * 128 + r
            cols[b + 6 + j] = h * 256 + j * 128 + r
        for j in range(4):
            cols[b + 2 + j] = 2048 + h * 512 + j * 128 + r
            cols[b + 8 + j] = 4096 + h * 512 + j * 128 + r
    for g in range(8):
        b = 49 + 6 * g
        for j in range(2):
            cols[b + j] = 8208 + g * 256 + j * 128 + r
            cols[b + 4 + j] = 6160 + g * 256 + j * 128 + r
        cols[b + 2] = 10256 + g * 128 + r
        cols[b + 3] = 11280 + g * 128 + r
    return cols


_NC_CACHE = {}


def _prep_shared(inp):
    w_in = np.asarray(inp["w_in"][0])
    cols = _win_cols().reshape(-1)
    wz = np.concatenate([w_in, np.zeros((4096, 1), np.float32)], axis=1)
    g = wz[:, np.where(cols < 0, w_in.shape[1], cols)]
    w_in_t = np.ascontiguousarray(g.reshape(32, 128, 97, 128).transpose(2, 1, 0, 3))
    w_out_t = np.ascontiguousarray(np.asarray(inp["w_out"][0]).reshape(32, 128, 32, 128).transpose(2, 1, 0, 3))
    w_up_t = np.ascontiguousarray(np.asarray(inp["w_up"][0]).reshape(32, 128, 128, 128).transpose(2, 1, 0, 3))
    w_down = np.ascontiguousarray(np.asarray(inp["w_down"][0]))
    lnp = np.ascontiguousarray(np.stack([inp["ln1_g"][0], inp["ln1_b"][0], inp["ln2_g"][0], inp["ln2_b"][0]]).astype(np.float32))
    return dict(w_in_t=w_in_t, w_out_t=w_out_t, w_up_t=w_up_t, w_down=w_down, lnp=lnp, cmask=_cmask())


def _make_in_maps(inp):
    inp = {k: np.asarray(v) for k, v in inp.items()}
    shared = _prep_shared(inp)
    xfull = np.concatenate([np.broadcast_to(inp["meta_tokens"][None], (4, 16, 4096)), inp["x_prompt"]], axis=1)
    xs = inp["x_sample"]
    in_maps = []
    for c in range(8):
        b, hh = c // 2, c % 2
        main = xfull[b, hh * 1032:(hh + 1) * 1032]
        samp = xs[16 * c:16 * c + 16].reshape(64, 4096)
        xm = np.ascontiguousarray(np.concatenate([main[:1024], samp, main[1024:]], axis=0))
        xp = np.ascontiguousarray(xfull[b, 0:1032]) if hh == 1 else np.zeros((NPRE, 4096), np.float32)
        m = dict(shared)
        m.update(xm=xm, xp=xp, pvec=_pvec(inp, float(hh)),
                 sgla=np.ascontiguousarray(inp["state_gla"][0, 16 * c:16 * c + 16]),
                 sssm=np.ascontiguousarray(inp["state_ssm"][0, 16 * c:16 * c + 16]),
                 sconv=np.ascontiguousarray(inp["state_conv"][0, 16 * c:16 * c + 16].reshape(48, 4096)))
        in_maps.append(m)
    return in_maps


def _assemble(R, cores=range(8)):
    y_prompt = np.zeros((4, 2048, 4096), np.float32)
    y_sample = np.zeros((128, 4, 4096), np.float32)
    gla_p = np.zeros((1, 4, 4, 256, 512), np.float32)
    ssm_p = np.zeros((1, 4, 32, 64, 128), np.float32)
    conv_p = np.zeros((1, 4, 3, 4096), np.float32)
    gla_s = np.zeros((1, 128, 4, 256, 512), np.float32)
    ssm_s = np.zeros((1, 128, 32, 64, 128), np.float32)
    conv_s = np.zeros((1, 128, 3, 4096), np.float32)
    for k, c in enumerate(cores):
        b, hh = c // 2, c % 2
        r = R[k]
        y = np.asarray(r["y"])
        main = np.concatenate([y[:1024], y[1088:1096]], axis=0)
        if hh == 0:
            y_prompt[b, 0:1016] = main[16:]
        else:
            y_prompt[b, 1016:2048] = main
            gla_p[0, b] = np.asarray(r["gla_p"])
            ssm_p[0, b] = np.asarray(r["ssm_p"])
            conv_p[0, b] = np.asarray(r["conv_p"])
        y_sample[16 * c:16 * c + 16] = y[1024:1088].reshape(16, 4, 4096)
        gla_s[0, 16 * c:16 * c + 16] = np.asarray(r["gla_s"])
        ssm_s[0, 16 * c:16 * c + 16] = np.asarray(r["ssm_s"])
        conv_s[0, 16 * c:16 * c + 16] = np.asarray(r["conv_s"]).reshape(16, 3, 4096)
    return (y_prompt, y_sample, gla_p, ssm_p, conv_p, gla_s, ssm_s, conv_s)


def kernel(**inp):
    in_maps = _make_in_maps(inp)
    if 99 not in _NC_CACHE:
        _NC_CACHE[99] = build_nc(99)
    res = run_bass_kernel_spmd(_NC_CACHE[99], in_maps, core_ids=list(range(8)))
    return _assemble(res.results)
```

```python
import contextlib
import threading
import numpy as np
import concourse.bass as bass
import concourse.mybir as mybir
from concourse.bass_utils import run_bass_kernel_spmd

F32 = mybir.dt.float32
BF16 = mybir.dt.bfloat16
AF = mybir.ActivationFunctionType
ALU = mybir.AluOpType

NT = 1096
NPRE = 1032
NTX = 1147
ALPHA = 2.0 ** 0.25
EPS = 1e-5
NEG = -30000.0

CM = {}
_o = 0
for _n, _w in [("ident", 128), ("tri", 128), ("ntri16", 128), ("nR16", 128), ("negtri", 128), ("ones", 128),
               ("negmask4", 512), ("Rs", 128), ("btri", 64), ("nbtri16", 64), ("nRb16", 64), ("negbtri", 64),
               ("negbmask4", 256), ("Rbs", 64), ("seqmask", 16), ("colmask", 1024)]:
    CM[_n] = (_o, _o + _w)
    _o += _w
NCM = _o
PV = {}
_o = 0
for _n, _w in [("dtb", 32), ("alog", 32), ("bgk", 1024), ("wgk", 1024), ("gnw", 4), ("convw", 128), ("convb", 32),
               ("dsk", 16), ("snw", 16), ("flag", 1)]:
    PV[_n] = (_o, _o + _w)
    _o += _w
NPV = _o


class Ev:
    __slots__ = ("sem", "val", "eng")

    def __init__(self, sem, val, eng):
        self.sem, self.val, self.eng = sem, val, eng


class Buf:
    def __init__(self, name):
        self.name = name
        self.last_write = None
        self.readers = []
        self.dsem = None
        self.dcount = 0


class TT_:
    def __init__(self, t, name):
        self.t = t
        self.b = Buf(name)


class Prog:
    ENGS = ("pe", "act", "dve", "pool", "sp")

    def __init__(self, nc, stack):
        self.nc = nc
        self.stack = stack
        self.ops = {e: [] for e in self.ENGS}
        self.esem = {e: stack.enter_context(nc.semaphore("es_" + e)) for e in self.ENGS}
        self.ecount = {e: 0 for e in self.ENGS}
        self.known = {e: {} for e in self.ENGS}
        self.out_events = []
        self.dma_owners = []

    def _need(self, eng, ev, raw):
        if ev is None:
            return None
        if ev.eng == eng and ev.sem is self.esem[eng]:
            if eng == "pe":
                return None
        if self.known[eng].get(id(ev.sem), 0) >= ev.val:
            return None
        return ev

    def _collect(self, eng, reads, writes):
        waits = {}

        def add(ev, raw):
            ev = self._need(eng, ev, raw)
            if ev is None:
                return
            cur = waits.get(id(ev.sem))
            if cur is None or cur.val < ev.val:
                waits[id(ev.sem)] = ev

        for b in reads:
            add(b.last_write, True)
        for b in writes:
            add(b.last_write, False)
            for r in b.readers:
                add(r, False)
        out = []
        for ev in waits.values():
            self.known[eng][id(ev.sem)] = ev.val
            out.append((ev.sem, ev.val))
        return out

    def _commit(self, ev, reads, writes):
        for b in reads:
            b.readers.append(ev)
            if len(b.readers) > 16:
                best = {}
                for r in b.readers:
                    c = best.get(id(r.sem))
                    if c is None or c.val < r.val:
                        best[id(r.sem)] = r
                b.readers = list(best.values())
        for b in writes:
            b.last_write = ev
            b.readers = []

    def op(self, eng, fn, reads=(), writes=()):
        waits = self._collect(eng, reads, writes)
        self.ecount[eng] += 1
        ev = Ev(self.esem[eng], self.ecount[eng], eng)
        self._commit(ev, reads, writes)
        self.ops[eng].append((waits, fn, (self.esem[eng], 1)))
        INTER.switch()
        return ev

    def dma(self, eng, out_ap, in_ap, owner, reads=(), writes=(), is_out=False):
        if owner.dsem is None:
            owner.dsem = self.stack.enter_context(self.nc.semaphore("ds_" + owner.name))
            self.dma_owners.append(owner)
        waits = self._collect(eng, reads, writes)
        owner.dcount += 16
        ev = Ev(owner.dsem, owner.dcount, "dma")
        self._commit(ev, reads, writes)
        self.ops[eng].append((waits, lambda e: e.dma_start(out=out_ap, in_=in_ap), (owner.dsem, 16)))
        if is_out:
            self.out_events.append(ev)
        INTER.switch()
        return ev

    def finish(self):
        self.ops["sp"].append(([(o.dsem, o.dcount) for o in self.dma_owners], None, None))

    def emit(self):
        nc = self.nc
        with nc.Block() as block:
            def run(e, lst):
                for waits, fn, inc in lst:
                    for sem, val in waits:
                        e.wait_ge(sem, val)
                    if fn is not None:
                        ins = fn(e)
                        if inc is not None:
                            ins.then_inc(inc[0], inc[1])

            @block.sync
            def _(e):
                run(e, self.ops["sp"])

            @block.tensor
            def _(e):
                run(e, self.ops["pe"])

            @block.scalar
            def _(e):
                run(e, self.ops["act"])

            @block.vector
            def _(e):
                run(e, self.ops["dve"])

            @block.gpsimd
            def _(e):
                run(e, self.ops["pool"])


class Inter:
    def __init__(self):
        self.cv = threading.Condition()
        self.active = None
        self.alive = {}
        self.exc = None

    def switch(self):
        me = threading.get_ident()
        if me not in self.alive:
            return
        with self.cv:
            other = [t for t in self.alive if t != me and self.alive[t]]
            if not other:
                return
            self.active = other[0]
            self.cv.notify_all()
            while self.active != me:
                self.cv.wait()

    def run(self, fa, fb):
        if fb is None:
            return fa()
        if fa is None:
            return fb()
        self.alive = {}
        self.active = None
        self.exc = None

        def w(f):
            me = threading.get_ident()
            with self.cv:
                while self.active != me:
                    self.cv.wait()
            try:
                f()
            except BaseException as e:
                self.exc = e
            finally:
                with self.cv:
                    self.alive[me] = False
                    other = [t for t in self.alive if self.alive[t]]
                    self.active = other[0] if other else -1
                    self.cv.notify_all()
        ta = threading.Thread(target=w, args=(fa,))
        tb = threading.Thread(target=w, args=(fb,))
        ta.start()
        tb.start()
        with self.cv:
            self.alive = {ta.ident: True, tb.ident: True}
            self.active = ta.ident
            self.cv.notify_all()
        ta.join()
        tb.join()
        self.alive = {}
        if self.exc is not None:
            raise self.exc


INTER = Inter()


class Ring:
    def __init__(self, items):
        self.items = items
        self.i = 0

    def __call__(self):
        x = self.items[self.i % len(self.items)]
        self.i += 1
        return x


def build_nc(stage=99):
    nc = bass.Bass("TRN2", target_bir_lowering=False)

    def D(name, shape, dt=F32, kind="ExternalInput"):
        return nc.dram_tensor(name, shape, dt, kind=kind).ap()

    xm = D("xm", [NT, 4096])
    xp = D("xp", [NPRE, 4096])
    cmask_d = D("cmask", [128, NCM])
    pvec_d = D("pvec", [128, NPV])
    w_in_t = D("w_in_t", [97, 128, 32, 128])
    w_out_t = D("w_out_t", [32, 128, 32, 128])
    w_up_t = D("w_up_t", [128, 128, 32, 128])
    w_down = D("w_down", [16384, 4096])
    lnp = D("lnp", [4, 4096])
    sgla = D("sgla", [16, 4, 256, 512])
    sssm = D("sssm", [16, 32, 64, 128])
    sconv = D("sconv", [48, 4096])
    y_o = D("y", [NT, 4096], kind="ExternalOutput")
    gla_p_o = D("gla_p", [4, 256, 512], kind="ExternalOutput")
    ssm_p_o = D("ssm_p", [32, 64, 128], kind="ExternalOutput")
    conv_p_o = D("conv_p", [3, 4096], kind="ExternalOutput")
    gla_s_o = D("gla_s", [16, 4, 256, 512], kind="ExternalOutput")
    ssm_s_o = D("ssm_s", [16, 32, 64, 128], kind="ExternalOutput")
    conv_s_o = D("conv_s", [48, 4096], kind="ExternalOutput")
    mix_scr = D("mix_scr", [9, 128, 32, 128], BF16, kind="Internal")
    S_scr = D("S_scr", [4, 128, 2, 512], F32, kind="Internal")
    hT_scr = D("hT_scr", [8, 128, 256], F32, kind="Internal")
    mixb = [Buf(f"mixb{t}") for t in range(9)]
    Sscrb = [Buf(f"Sscrb{h}") for h in range(4)]
    hscrb = [Buf(f"hscrb{g}") for g in range(8)]

    with contextlib.ExitStack() as st:
        P = Prog(nc, st)
        P.fence = {}

        class Scope:
            def __init__(self):
                self.stack = contextlib.ExitStack()
                self.bufs = []

            def __enter__(self):
                self.stack.__enter__()
                return self

            def __exit__(self, *a):
                for b in self.bufs:
                    for ev in ([b.last_write] if b.last_write else []) + b.readers:
                        c = P.fence.get(id(ev.sem))
                        if c is None or c.val < ev.val:
                            P.fence[id(ev.sem)] = ev
                return self.stack.__exit__(*a)

        root = Scope()
        root.stack = st
        uid = [0]

        def sb(name, shape, dt=F32, sc=None):
            sc = sc or root
            uid[0] += 1
            nm = f"{name}_{uid[0]}"
            t = TT_(sc.stack.enter_context(nc.sbuf_tensor(nm, shape, dt)), nm)
            t.b.readers = list(P.fence.values())
            sc.bufs.append(t.b)
            return t

        def ring(name, shape, dt, n, sc=None):
            return Ring([sb(f"{name}{i}", shape, dt, sc) for i in range(n)])

        def ACT(out, in_, func, reads, writes, **kw):
            return P.op("act", lambda e: e.activation(out=out, in_=in_, func=func, **kw), reads, writes)

        def TTo(out, in0, in1, op, reads, writes, eng="dve"):
            return P.op(eng, lambda e: e.tensor_tensor(out=out, in0=in0, in1=in1, op=op), reads, writes)

        def TSM(out, in0, s1, reads, writes, eng="dve"):
            return P.op(eng, lambda e: e.tensor_scalar_mul(out=out, in0=in0, scalar1=s1), reads, writes)

        def STT(out, in0, scalar, in1, op0, op1, reads, writes, eng="dve"):
            return P.op(eng, lambda e: e.scalar_tensor_tensor(out=out, in0=in0, scalar=scalar, in1=in1, op0=op0,
                                                              op1=op1), reads, writes)

        def CP(eng, out, in_, reads, writes):
            if eng == "act":
                return P.op("act", lambda e: e.copy(out=out, in_=in_), reads, writes)
            return P.op(eng, lambda e: e.tensor_copy(out=out, in_=in_), reads, writes)

        def MM(lst, reads, writes):
            def fn(e):
                ins = None
                for (o, l, r, s0, s1) in lst:
                    ins = e.matmul(o, lhsT=l, rhs=r, start=s0, stop=s1)
                return ins
            return P.op("pe", fn, reads, writes)

        def TR(lst, reads, writes):
            def fn(e):
                ins = None
                for (o, i, idn) in lst:
                    ins = e.transpose(o, i, idn)
                return ins
            return P.op("pe", fn, reads, writes)

        def MEMSET(ap, val, writes):
            return P.op("dve", lambda e: e.memset(ap, val), [], writes)

        def RSTD(ss_, tmp, rstd_, L, scale):
            ACT(tmp.t[0:L, :], ss_.t[0:L, :], AF.Sqrt, [ss_.b, epsc.b], [tmp.b], scale=scale, bias=epsc.t[0:L, :])
            P.op("dve", lambda e: e.reciprocal(out=rstd_.t[0:L, :], in_=tmp.t[0:L, :]), [tmp.b], [rstd_.b])

        banks = [TT_(st.enter_context(nc.psum_tensor(f"pb{i}", [128, 512], F32)), f"pb{i}") for i in range(8)]
        nb = Ring(banks[0:5])
        byring = Ring([banks[5], banks[6]])
        acc_bank = banks[7]

        def bf(bank):
            return bank.t[:].bitcast(BF16)

        identF = sb("identF", [128, 128])
        identb = sb("identb", [128, 128], BF16)
        epsc = sb("epsc", [128, 1])
        P.dma("sp", identF.t[:], cmask_d[:, CM["ident"][0]:CM["ident"][1]], identF.b, writes=[identF.b])
        CP("act", identb.t[:], identF.t[:], [identF.b], [identb.b])
        MEMSET(epsc.t[:], EPS, [epsc.b])
        identf = identF.t[:]
        wring = ring("wsl", [128, 32, 128], BF16, 3)

        def wl(dram_ap):
            s = wring()
            P.dma("pool", s.t[:], dram_ap, s.b, writes=[s.b])
            return s

        with Scope() as phAB:
            cm = sb("cm", [128, NCM], F32, phAB)
            pv = sb("pv", [128, NPV], F32, phAB)
            a_bc = sb("a_bc", [128, 32], F32, phAB)
            diagD = sb("diagD", [128, 16, 128], BF16, phAB)
            halo_all = sb("halo_all", [128, 32, 3], F32, phAB)
            P.dma("sp", cm.t[:], cmask_d, cm.b, writes=[cm.b])
            P.dma("sp", pv.t[:], pvec_d, pv.b, writes=[pv.b])

            def C(name, r=slice(0, 128), c=None):
                a, b_ = CM[name]
                if c is None:
                    return cm.t[r, a:b_]
                return cm.t[r, a + c.start:a + c.stop]

            def Pv(name, r=slice(0, 128), c=None):
                a, b_ = PV[name]
                if c is None:
                    return pv.t[r, a:b_]
                return pv.t[r, a + c.start:a + c.stop]

            ACT(a_bc.t[:], Pv("alog"), AF.Exp, [pv.b], [a_bc.b])
            TSM(a_bc.t[:], a_bc.t[:], -1.0, [a_bc.b], [a_bc.b])
            for xt in range(16):
                TSM(diagD.t[:, xt, :], identf, Pv("dsk", c=slice(xt, xt + 1)), [identF.b, pv.b], [diagD.b])
            MEMSET(halo_all.t[:], 0.0, [halo_all.b])
            xT = sb("xT", [128, 32, NT], BF16, phAB)
            xTb = [[Buf(f"xT{t}_{k}") for k in range(2)] for t in range(9)]
            for l_ in xTb:
                phAB.bufs.extend(l_)

            def build_xT(x_d, ntok):
                bufs = []
                with Scope() as sx:
                    xst = ring("xst", [128, 4096], F32, 2, sx)
                    for t in range((ntok + 127) // 128):
                        r0 = t * 128
                        n = min(128, ntok - r0)
                        xs_ = xst()
                        P.dma("sp", xs_.t[0:n, :], x_d[r0:r0 + n, :], xs_.b, writes=[xs_.b])
                        for q in range(8):
                            bk = nb()
                            TR([(bk.t[:, j * 128:j * 128 + n], xs_.t[0:n, (4 * q + j) * 128:(4 * q + j + 1) * 128],
                                 identf[0:n, 0:n]) for j in range(4)], [xs_.b, identF.b], [bk.b])
                            src = bk.t[:].rearrange("p (a t) -> p a t", a=4)[:, :, 0:n]
                            CP("act" if q % 2 == 0 else "dve", xT.t[:, 4 * q:4 * q + 4, r0:r0 + n], src, [bk.b],
                               [xTb[t][q % 2]])
                        bufs += xTb[t]
                return bufs

            def gemm_tile(slot, M, xbufs, chunks, evac, mcol0=0):
                for (c0, c1) in chunks:
                    bk = nb()
                    MM([(bk.t[0:M, 0:c1 - c0], slot.t[:, kt, mcol0:mcol0 + M], xT.t[:, kt, c0:c1], kt == 0, kt == 31)
                        for kt in range(32)], [slot.b] + xbufs, [bk.b])
                    evac(bk, c0, c1)

            for pre in (True, False):
                ntok = NPRE if pre else NT
                xbufs = build_xT(xp if pre else xm, ntok)
                if pre:
                    chunks = [(0, 512), (512, 1024), (1024, 1032)]
                    scan = [(128, 128 * c, c) for c in range(8)] + [(8, 1024, 9)]
                else:
                    chunks = [(0, 512), (512, 1024), (1024, 1096)]
                    scan = [(128, 128 * c, c) for c in range(8)] + [(8, 1088, 9)]
                with Scope() as ph:
                    gklT = sb("gklT", [16, NT], F32, ph)
                    dt_tm = sb("dt_tm", [128, 10, 32], F32, ph)
                    dta_tm = sb("dta_tm", [128, 10, 32], F32, ph)
                    sdt = Scope()
                    sdt.__enter__()
                    dtT = sb("dtT", [32, NT], F32, sdt)
                    slot = wl(w_in_t[0])
                    gemm_tile(slot, 16, xbufs, chunks,
                              lambda bk, c0, c1: CP("act", gklT.t[:, c0:c1], bk.t[0:16, 0:c1 - c0], [bk.b], [gklT.b]))
                    gemm_tile(slot, 32, xbufs, chunks,
                              lambda bk, c0, c1: CP("act", dtT.t[:, c0:c1], bk.t[0:32, 0:c1 - c0], [bk.b], [dtT.b]),
                              mcol0=32)
                    dtl = list(scan) + ([] if pre else [(64, 1024, 8)])
                    dtmp = sb("dtmp", [128, 32], F32, sdt)
                    for (L, col, ci) in dtl:
                        bk = nb()
                        TR([(bk.t[0:L, 0:32], dtT.t[0:32, col:col + L], identf[0:32, 0:32])], [dtT.b, identF.b], [bk.b])
                        TTo(dtmp.t[0:L, :], bk.t[0:L, 0:32], Pv("dtb", slice(0, L)), ALU.add, [bk.b, pv.b], [dtmp.b])
                        ACT(dtmp.t[0:L, :], dtmp.t[0:L, :], AF.Exp, [dtmp.b], [dtmp.b])
                        ACT(dt_tm.t[0:L, ci, :], dtmp.t[0:L, :], AF.Ln, [dtmp.b], [dt_tm.b], bias=1.0)
                        TTo(dta_tm.t[0:L, ci, :], dt_tm.t[0:L, ci, :], a_bc.t[0:L, :], ALU.mult, [dt_tm.b, a_bc.b],
                            [dta_tm.b])
                    sdt.__exit__(None, None, None)

                    if stage >= 1:
                      with Scope() as pg:
                        kT = sb("kT", [128, 2, NT], F32, pg)
                        qT = sb("qT", [128, 2, NT], F32, pg)
                        v_tm = sb("v_tm", [128, 10, 512], BF16, pg)
                        rgT = sb("rgT", [128, 4, NT], BF16, pg)
                        vstage = ring("vstage", [128, 512], BF16, 1, pg)
                        rtmp = ring("rtmp", [128, 512], F32, 1, pg)
                        e1 = ring("e1", [128, 256], F32, 1, pg)
                        spb = ring("spb", [128, 256], F32, 1, pg)
                        E4 = ring("E4", [128, 4, 128], F32, 2, pg)
                        Ekn = ring("Ekn", [128, 2, 128], F32, 1, pg)
                        qp = ring("qp", [128, 2, 128], BF16, 2, pg)
                        kp = ring("kp", [128, 2, 128], BF16, 2, pg)
                        kpp = ring("kpp", [128, 2, 128], BF16, 2, pg)
                        kpptm = ring("kpptm", [128, 256], BF16, 2, pg)
                        AT = ring("AT", [128, 128], BF16, 2, pg)
                        ss = ring("ss", [128, 1], F32, 2, pg)
                        ss2 = ring("ss2", [128, 1], F32, 2, pg)
                        rstd = ring("rstd", [128, 1], F32, 2, pg)
                        on = ring("on", [128, 512], BF16, 2, pg)
                        mixst = ring("mixst", [128, 4, 128], BF16, 2, pg)
                        S = sb("S", [128, 2, 512], F32, pg)
                        Sbf = sb("Sbf", [128, 2, 512], BF16, pg)
                        if not pre:
                            kppm = ring("kppm", [64, 256], BF16, 2, pg)
                            qpm = ring("qpm", [128, 2, 64], BF16, 2, pg)
                            Sin = ring("Sin", [128, 2, 512], F32, 2, pg)
                            Sinb = ring("Sinb", [128, 2, 512], BF16, 2, pg)

                        def v_evac(j):
                            def f(bk, c0, c1):
                                vs = vstage()
                                n = c1 - c0
                                CP("act", vs.t[:, 0:n], bk.t[:, 0:n], [bk.b], [vs.b])
                                tb = nb()
                                tbv = bf(tb)
                                if n == 512:
                                    TR([(tbv[:, a * 128:(a + 1) * 128], vs.t[:, a * 128:(a + 1) * 128], identb.t[:, :])
                                        for a in range(4)], [vs.b, identb.b], [tb.b])
                                    ci0 = c0 // 128
                                    CP("dve", v_tm.t[:, ci0:ci0 + 4, j * 128:(j + 1) * 128],
                                       tbv[:, 0:512].rearrange("p (a c) -> p a c", a=4), [tb.b], [v_tm.b])
                                elif pre:
                                    TR([(tbv[0:8, 0:128], vs.t[:, 0:8], identb.t[:, :])], [vs.b, identb.b], [tb.b])
                                    CP("dve", v_tm.t[0:8, 9, j * 128:(j + 1) * 128], tbv[0:8, 0:128], [tb.b], [v_tm.b])
                                else:
                                    TR([(tbv[0:64, 0:128], vs.t[:, 0:64], identb.t[:, :]),
                                        (tbv[0:8, 128:256], vs.t[:, 64:72], identb.t[:, :])], [vs.b, identb.b], [tb.b])
                                    CP("dve", v_tm.t[0:64, 8, j * 128:(j + 1) * 128], tbv[0:64, 0:128], [tb.b], [v_tm.b])
                                    CP("dve", v_tm.t[0:8, 9, j * 128:(j + 1) * 128], tbv[0:8, 128:256], [tb.b], [v_tm.b])
                            return f

                        def r_evac(j):
                            def f(bk, c0, c1):
                                rt = rtmp()
                                n = c1 - c0
                                ACT(rt.t[:, 0:n], bk.t[:, 0:n], AF.Silu, [bk.b], [rt.b])
                                TSM(rgT.t[:, j, c0:c1], rt.t[:, 0:n], Pv("gnw", c=slice(j, j + 1)), [rt.b, pv.b], [rgT.b])
                            return f

                        def gla_front(h, L, col, blk):
                            tri16 = C("nbtri16" if blk else "ntri16", slice(0, L), slice(0, L))
                            r16 = C("nRb16" if blk else "nR16", slice(0, L), slice(0, L))
                            bk = nb()
                            MM([(bk.t[0:L, 0:256], gklT.t[0:16, col:col + L],
                                 Pv("wgk", slice(0, 16), slice(h * 256, h * 256 + 256)), True, False),
                                (bk.t[0:L, 0:256], C("ones", slice(0, 1), slice(0, L)),
                                 Pv("bgk", slice(0, 1), slice(h * 256, h * 256 + 256)), False, True)],
                               [gklT.b, pv.b, cm.b], [bk.b])
                            e1_, sp_ = e1(), spb()
                            ACT(e1_.t[0:L, :], bk.t[0:L, 0:256], AF.Exp, [bk.b], [e1_.b], scale=-1.0)
                            ACT(sp_.t[0:L, :], e1_.t[0:L, :], AF.Ln, [e1_.b], [sp_.b], bias=1.0)
                            b5 = nb()
                            b5v = b5.t[:].rearrange("p (a t) -> p a t", a=4)
                            MM([(b5v[:, kt, 0:L], sp_.t[0:L, kt * 128:(kt + 1) * 128], tri16, True, True) for kt in range(2)] +
                               [(b5v[:, 2 + kt, 0:L], sp_.t[0:L, kt * 128:(kt + 1) * 128], r16, True, True) for kt in range(2)],
                               [sp_.b, cm.b], [b5.b])
                            E4_ = E4()
                            ACT(E4_.t[:, :, 0:L], b5v[:, :, 0:L], AF.Exp, [b5.b], [E4_.b])
                            qp_ = kp_ = None
                            if not pre:
                                Ekn_ = Ekn()
                                ACT(Ekn_.t[:, :, 0:L], b5v[:, 0:2, 0:L], AF.Exp, [b5.b], [Ekn_.b], scale=-1.0)
                                qp_, kp_ = qp(), kp()
                                STT(qp_.t[:, :, 0:L], qT.t[:, :, col:col + L], 0.0625, E4_.t[:, 0:2, 0:L], ALU.mult, ALU.mult,
                                    [qT.b, E4_.b], [qp_.b])
                                TTo(kp_.t[:, :, 0:L], kT.t[:, :, col:col + L], Ekn_.t[:, :, 0:L], ALU.mult, [kT.b, Ekn_.b], [kp_.b])
                            kpp_ = kpp()
                            TTo(kpp_.t[:, :, 0:L], kT.t[:, :, col:col + L], E4_.t[:, 2:4, 0:L], ALU.mult, [kT.b, E4_.b], [kpp_.b])
                            tb = nb()
                            tbv = bf(tb)
                            TR([(tbv[0:L, kt * 128:(kt + 1) * 128], kpp_.t[:, kt, 0:L], identb.t[:, :]) for kt in range(2)],
                               [kpp_.b, identb.b], [tb.b])
                            kt_ = kpptm()
                            CP("act", kt_.t[0:L, :], tbv[0:L, 0:256], [tb.b], [kt_.b])
                            return E4_, qp_, kp_, kt_

                        def gla_out(h, L, col, ob, tile, off):
                            ss_, ss2_, rstd_, on_, mx = ss(), ss2(), rstd(), on(), mixst()
                            ACT(on_.t[0:L, :], ob.t[0:L, :], AF.Square, [ob.b], [on_.b, ss_.b], accum_out=ss_.t[0:L, 0:1])
                            RSTD(ss_, ss2_, rstd_, L, 1.0 / 512)
                            ACT(on_.t[0:L, :], ob.t[0:L, :], AF.Identity, [ob.b, rstd_.b], [on_.b], scale=rstd_.t[0:L, 0:1])
                            tb = nb()
                            tbv = bf(tb)
                            TR([(tbv[:, j * 128:j * 128 + L], on_.t[0:L, j * 128:(j + 1) * 128], identb.t[0:L, 0:L])
                                for j in range(4)], [on_.b, identb.b], [tb.b])
                            TTo(mx.t[:, :, 0:L], tbv[:, 0:512].rearrange("p (a c) -> p a c", a=4)[:, :, 0:L],
                                rgT.t[:, :, col:col + L], ALU.mult, [tb.b, rgT.b], [mx.b])
                            P.dma("sp", mix_scr[tile, :, 4 * h:4 * h + 4, off:off + L], mx.t[:, :, 0:L], mx.b,
                                  reads=[mx.b, mixb[tile]])

                        for h in range(4):
                            base = 1 + 12 * h
                            for j in range(2):
                                gemm_tile(wl(w_in_t[base + j]), 128, xbufs, chunks,
                                          lambda bk, c0, c1, j=j: CP("act", kT.t[:, j, c0:c1], bk.t[:, 0:c1 - c0], [bk.b], [kT.b]))
                            for j in range(4):
                                gemm_tile(wl(w_in_t[base + 2 + j]), 128, xbufs, chunks, v_evac(j))
                            if pre:
                                MEMSET(S.t[:], 0.0, [S.b])
                            else:
                                for j in range(2):
                                    gemm_tile(wl(w_in_t[base + 6 + j]), 128, xbufs, chunks,
                                              lambda bk, c0, c1, j=j: CP("act", qT.t[:, j, c0:c1], bk.t[:, 0:c1 - c0], [bk.b], [qT.b]))
                                for j in range(4):
                                    gemm_tile(wl(w_in_t[base + 8 + j]), 128, xbufs, chunks, r_evac(j))
                                P.dma("sp", S.t[:], S_scr[h], S.b, reads=[Sscrb[h]], writes=[S.b])
                                CP("act", Sbf.t[:], S.t[:], [S.b], [Sbf.b])
                            HG = {}

                            def gF(k, h=h):
                                L, col, ci = scan[k]
                                E4_, qp_, kp_, kt_ = gla_front(h, L, col, False)
                                AT_ = None
                                if not pre:
                                    sbk = nb()
                                    MM([(sbk.t[0:L, 0:L], kp_.t[:, kt, 0:L], qp_.t[:, kt, 0:L], kt == 0, kt == 1) for kt in range(2)],
                                       [kp_.b, qp_.b], [sbk.b])
                                    AT_ = AT()
                                    TTo(AT_.t[0:L, 0:L], sbk.t[0:L, 0:L], C("tri", slice(0, L), slice(0, L)), ALU.mult,
                                        [sbk.b, cm.b], [AT_.b])
                                HG[k] = (E4_, qp_, kp_, kt_, AT_)

                            def gT(k, h=h):
                                L, col, ci = scan[k]
                                E4_, qp_, kp_, kt_, AT_ = HG.pop(k)
                                if not pre:
                                    ob = byring()
                                    MM([(ob.t[0:L, :], AT_.t[0:L, 0:L], v_tm.t[0:L, ci, :], True, False)] +
                                       [(ob.t[0:L, :], qp_.t[:, kt, 0:L], Sbf.t[:, kt, :], False, kt == 1) for kt in range(2)],
                                       [AT_.b, v_tm.b, qp_.b, Sbf.b], [ob.b])
                                for kt in range(2):
                                    ub = nb()
                                    MM([(ub.t[:, :], kt_.t[0:L, kt * 128:(kt + 1) * 128], v_tm.t[0:L, ci, :], True, True)],
                                       [kt_.b, v_tm.b], [ub.b])
                                    STT(S.t[:, kt, :], S.t[:, kt, :], E4_.t[:, kt, L - 1:L], ub.t[:, :], ALU.mult, ALU.add,
                                        [S.b, E4_.b, ub.b], [S.b])
                                if not pre:
                                    CP("act", Sbf.t[:], S.t[:], [S.b], [Sbf.b])
                                    tile, off = (ci, 0) if ci < 8 else (8, 64)
                                    gla_out(h, L, col, ob, tile, off)

                            gF(0)
                            for k in range(len(scan)):
                                INTER.run(lambda k=k: gT(k), (lambda k=k: gF(k + 1)) if k + 1 < len(scan) else None)
                            if pre:
                                TSM(S.t[:], S.t[:], Pv("flag"), [S.b, pv.b], [S.b])
                                P.dma("sp", S_scr[h], S.t[:], S.b, reads=[S.b], writes=[Sscrb[h]])
                            else:
                                P.dma("sp", gla_p_o[h].rearrange("(kt p) v -> p kt v", p=128), S.t[:], S.b, reads=[S.b], is_out=True)
                                L, col, ci = 64, 1024, 8

                                def ld(i, h=h):
                                    s1, s2 = Sin(), Sinb()
                                    src = sgla[i, h].rearrange("(kt p) v -> p kt v", p=128)
                                    P.dma("sp", s1.t[:], src, s1.b, writes=[s1.b])
                                    P.dma("pool", s2.t[:], src, s2.b, writes=[s2.b])
                                    return s1, s2
                                lds = [ld(0)]
                                E4_, qp_, kp_, kt_ = gla_front(h, L, col, True)
                                sbk = nb()
                                MM([(sbk.t[0:L, 0:L], kp_.t[:, kt, 0:L], qp_.t[:, kt, 0:L], kt == 0, kt == 1) for kt in range(2)],
                                   [kp_.b, qp_.b], [sbk.b])
                                AT_ = AT()
                                TTo(AT_.t[0:L, 0:L], sbk.t[0:L, 0:L], C("btri", slice(0, L)), ALU.mult, [sbk.b, cm.b], [AT_.b])
                                ob = acc_bank
                                MM([(ob.t[0:L, :], AT_.t[0:L, 0:L], v_tm.t[0:L, ci, :], True, False)], [AT_.b, v_tm.b], [ob.b])
                                for i in range(16):
                                    if i + 1 < 16:
                                        lds.append(ld(i + 1))
                                    s1, s2 = lds[i]
                                    qm, km = qpm(), kppm()
                                    TTo(qm.t[:, :, :], qp_.t[:, :, 0:64],
                                        C("colmask", c=slice(64 * i, 64 * i + 64)).unsqueeze(1).broadcast_to([128, 2, 64]),
                                        ALU.mult, [qp_.b, cm.b], [qm.b])
                                    TSM(km.t[0:64, :], kt_.t[0:64, :], C("seqmask", slice(0, 64), slice(i, i + 1)), [kt_.b, cm.b], [km.b])
                                    MM([(ob.t[0:L, :], qm.t[:, kt, :], s2.t[:, kt, :], False, (i == 15 and kt == 1))
                                        for kt in range(2)], [qm.b, s2.b], [ob.b])
                                    so = s1
                                    for kt in range(2):
                                        ub = nb()
                                        MM([(ub.t[:, :], km.t[0:64, kt * 128:(kt + 1) * 128], v_tm.t[0:64, ci, :], True, True)],
                                           [km.b, v_tm.b], [ub.b])
                                        STT(so.t[:, kt, :], s1.t[:, kt, :], E4_.t[:, kt, 4 * i + 3:4 * i + 4], ub.t[:, :], ALU.mult,
                                            ALU.add, [E4_.b, ub.b], [so.b])
                                    P.dma("sp", gla_s_o[i, h].rearrange("(kt p) v -> p kt v", p=128), so.t[:], so.b,
                                          reads=[so.b], is_out=True)
                                gla_out(h, L, col, ob, 8, 0)

                    DBG = 9.0
                    if stage >= 2 and (pre or DBG >= 2.1):
                      with Scope() as pq:
                        xbcT = sb("xbcT", [128, 4, NTX], F32, pq)
                        acc = sb("acc", [128, NTX], F32, pq)
                        xcp = sb("xcp", [128, 4, NPRE], BF16, pq)
                        xcs = sb("xcs", [128, 4, 64], BF16, pq)
                        zs_tm = sb("zs_tm", [128, 10, 256], BF16, pq)
                        zstage = ring("zstage", [128, 512], BF16, 2, pq)
                        R1 = ring("R1", [128, 4, 128], F32, 1, pq)
                        R2 = ring("R2", [128, 4, 128], F32, 1, pq)
                        Lm = ring("Lm", [128, 4, 128], F32, 2, pq)
                        MT = ring("MT", [128, 4, 128], BF16, 2, pq)
                        xpr = ring("xpr", [128, 256], BF16, 2, pq)
                        xpp = ring("xpp", [128, 256], BF16, 2, pq)
                        Btm = ring("Btm", [128, 128], BF16, 2, pq)
                        wend = ring("wend", [128, 4], F32, 2, pq)
                        etm = ring("etm", [128, 4], F32, 2, pq)
                        El = ring("El", [128, 4], F32, 2, pq)
                        t0 = ring("t0", [128, 256], F32, 2, pq)
                        t1 = ring("t1", [128, 256], F32, 2, pq)
                        ssd_ss = ring("sss", [128, 1], F32, 2, pq)
                        ssd_s2 = ring("sss2", [128, 1], F32, 2, pq)
                        ssd_rs = ring("ssrs", [128, 1], F32, 2, pq)
                        junk2 = sb("junk2", [128, 256], BF16, pq)
                        yn = ring("yn", [128, 256], BF16, 2, pq)
                        mixs2 = ring("mixs2", [128, 2, 128], BF16, 2, pq)
                        hT = sb("hTw", [128, 256], F32, pq)
                        hTb = sb("hTb", [128, 256], BF16, pq)
                        htmp = sb("htmp", [128, 256], F32, pq)
                        cstg = sb("cstg", [128, 51], F32, pq)
                        cout = ring("cout", [51, 128], F32, 2, pq)
                        if not pre:
                            convT_all = sb("convT_all", [128, 32, 48], F32, pq)
                            with Scope() as s0:
                                cst = ring("cst", [48, 512], F32, 2, s0)
                                for q in range(8):
                                    cs_ = cst()
                                    P.dma("sp", cs_.t[:], sconv[:, q * 512:(q + 1) * 512], cs_.b, writes=[cs_.b])
                                    bk = nb()
                                    TR([(bk.t[:, j * 48:(j + 1) * 48], cs_.t[0:48, j * 128:(j + 1) * 128], identf[0:48, 0:48])
                                        for j in range(4)], [cs_.b, identF.b], [bk.b])
                                    CP("act", convT_all.t[:, 4 * q:4 * q + 4, :], bk.t[:, 0:192].rearrange("p (a c) -> p a c", a=4),
                                       [bk.b], [convT_all.b])
                            hnat = ring("hnat", [128, 2, 128], F32, 2, pq)
                            hTbi = ring("hTbi", [128, 256], BF16, 2, pq)
                            hout = ring("hout", [128, 2, 128], F32, 2, pq)
                            Cm_ = ring("Cm", [128, 64], BF16, 2, pq)
                            xpm = ring("xpm", [64, 256], BF16, 2, pq)
                            dta_e = sb("dta_e", [64, 2, 128], F32, pq)
                            dec = sb("dec", [128, 2, 16], F32, pq)

                        def xbc_evac(q):
                            def f(bk, c0, c1):
                                if pre or c1 <= 1024:
                                    CP("act", xbcT.t[:, q, 3 + c0:3 + c1], bk.t[:, 0:c1 - c0], [bk.b], [xbcT.b])
                                else:
                                    CP("act", xbcT.t[:, q, 1035:1147].rearrange("p (s w) -> p s w", w=7)[:, :, 3:7],
                                       bk.t[:, 0:64].rearrange("p (s w) -> p s w", w=4), [bk.b], [xbcT.b])
                                    CP("act", xbcT.t[:, q, 1027:1035], bk.t[:, 64:72], [bk.b], [xbcT.b])
                            return f

                        def z_evac(j):
                            def f(bk, c0, c1):
                                vs = zstage()
                                n = c1 - c0
                                ACT(vs.t[:, 0:n], bk.t[:, 0:n], AF.Silu, [bk.b], [vs.b])
                                tb = nb()
                                tbv = bf(tb)
                                if n == 512:
                                    TR([(tbv[:, a * 128:(a + 1) * 128], vs.t[:, a * 128:(a + 1) * 128], identb.t[:, :])
                                        for a in range(4)], [vs.b, identb.b], [tb.b])
                                    ci0 = c0 // 128
                                    CP("dve", zs_tm.t[:, ci0:ci0 + 4, j * 128:(j + 1) * 128],
                                       tbv[:, 0:512].rearrange("p (a c) -> p a c", a=4), [tb.b], [zs_tm.b])
                                else:
                                    TR([(tbv[0:64, 0:128], vs.t[:, 0:64], identb.t[:, :]),
                                        (tbv[0:8, 128:256], vs.t[:, 64:72], identb.t[:, :])], [vs.b, identb.b], [tb.b])
                                    CP("dve", zs_tm.t[0:64, 8, j * 128:(j + 1) * 128], tbv[0:64, 0:128], [tb.b], [zs_tm.b])
                                    CP("dve", zs_tm.t[0:8, 9, j * 128:(j + 1) * 128], tbv[0:8, 128:256], [tb.b], [zs_tm.b])
                            return f

                        def conv_tile(q, gt):
                            npr = NPRE
                            ncol = (3 + npr) if pre else NTX
                            if pre:
                                MEMSET(xbcT.t[:, q, 0:3], 0.0, [xbcT.b])
                            else:
                                CP("act", xbcT.t[:, q, 0:3], halo_all.t[:, gt, :], [halo_all.b], [xbcT.b])
                                CP("act", xbcT.t[:, q, 1035:1147].rearrange("p (s w) -> p s w", w=7)[:, :, 0:3],
                                   convT_all.t[:, gt, :].rearrange("p (s w) -> p s w", w=3), [convT_all.b], [xbcT.b])
                            m = ncol - 3
                            ACT(acc.t[:, 0:m], xbcT.t[:, q, 3:ncol], AF.Identity, [xbcT.b, pv.b], [acc.b],
                                scale=Pv("convw", c=slice(4 * gt + 3, 4 * gt + 4)), bias=Pv("convb", c=slice(gt, gt + 1)))
                            for jj in range(3):
                                STT(acc.t[:, 0:m], xbcT.t[:, q, jj:jj + m], Pv("convw", c=slice(4 * gt + jj, 4 * gt + jj + 1)),
                                    acc.t[:, 0:m], ALU.mult, ALU.add, [xbcT.b, pv.b, acc.b], [acc.b])
                            ACT(xcp.t[:, q, 0:npr], acc.t[:, 0:npr], AF.Silu, [acc.b], [xcp.b])
                            if not pre:
                                ACT(xcs.t[:, q, :].rearrange("p (s w) -> p s w", w=4),
                                    acc.t[:, 1032:1144].rearrange("p (s w) -> p s w", w=7)[:, :, 3:7], AF.Silu, [acc.b], [xcs.b])
                                CP("act", cstg.t[:, 0:48].rearrange("p (s w) -> p s w", w=3),
                                   xbcT.t[:, q, 1035:1147].rearrange("p (s w) -> p s w", w=7)[:, :, 4:7], [xbcT.b], [cstg.b])
                                CP("act", cstg.t[:, 48:51], xbcT.t[:, q, 3 + 1029:3 + 1032], [xbcT.b], [cstg.b])
                                bk = nb()
                                TR([(bk.t[0:51, 0:128], cstg.t[:, 0:51], identf)], [cstg.b, identF.b], [bk.b])
                                co = cout()
                                CP("dve", co.t[:, :], bk.t[0:51, 0:128], [bk.b], [co.b])
                                P.dma("sp", conv_s_o[:, gt * 128:(gt + 1) * 128], co.t[0:48, :], co.b, reads=[co.b], is_out=True)
                                P.dma("sp", conv_p_o[:, gt * 128:(gt + 1) * 128], co.t[48:51, :], co.b, reads=[co.b], is_out=True)
                            else:
                                TSM(halo_all.t[:, gt, :], xbcT.t[:, q, 3 + 1029:3 + 1032], Pv("flag"), [xbcT.b, pv.b], [halo_all.b])

                        def ssd_front(g, L, xsrc, c0, ci, blk):
                            dta_c = dta_tm.t[0:L, ci, 4 * g:4 * g + 4]
                            dt_c = dt_tm.t[0:L, ci, 4 * g:4 * g + 4]
                            Lm_ = None
                            if not pre:
                                R1_, R2_ = R1(), R2()
                                TTo(R1_.t[0:L, :, 0:L], dta_c.unsqueeze(2).broadcast_to([L, 4, L]),
                                    C("btri" if blk else "tri", slice(0, L), slice(0, L)).unsqueeze(1).broadcast_to([L, 4, L]),
                                    ALU.mult, [dta_tm.b, cm.b], [R1_.b])
                                CP("dve", R2_.t[0:L, :, 0:L], dta_c.unsqueeze(2).broadcast_to([L, 4, L]), [dta_tm.b], [R2_.b])
                                bD = nb()
                                bDv = bD.t[0:L, 0:4 * L].rearrange("p (a t) -> p a t", a=4)
                                nm = C("negbmask4" if blk else "negmask4", slice(0, L)).rearrange("p (a t) -> p a t", a=4)[:, :, 0:L]
                                MM([(bDv, C("ones", slice(0, L), slice(0, L)), R1_.t[0:L, :, 0:L], True, False),
                                    (bDv, C("negbtri" if blk else "negtri", slice(0, L), slice(0, L)), R2_.t[0:L, :, 0:L], False, False),
                                    (bDv, identf[0:L, 0:L], nm, False, True)], [R1_.b, R2_.b, cm.b, identF.b], [bD.b])
                                Lm_ = Lm()
                                ACT(Lm_.t[0:L, :, 0:L], bDv, AF.Exp, [bD.b], [Lm_.b])
                            bw = nb()
                            MM([(bw.t[0:L, 0:32], C("Rbs" if blk else "Rs", slice(0, L), slice(0, L)), dta_tm.t[0:L, ci, :], True, True)],
                               [dta_tm.b, cm.b], [bw.b])
                            we = wend()
                            ACT(we.t[0:L, :], bw.t[0:L, 4 * g:4 * g + 4], AF.Exp, [bw.b], [we.b])
                            tb = nb()
                            tbv = bf(tb)
                            TR([(tbv[0:L, a * 128:(a + 1) * 128], xsrc.t[:, a, c0:c0 + L], identb.t[:, :]) for a in range(3)],
                               [xsrc.b, identb.b], [tb.b])
                            xpr_, xpp_, Btm_ = xpr(), xpp(), Btm()
                            TTo(xpr_.t[0:L, :].rearrange("p (j c) -> p j c", j=4), tbv[0:L, 0:256].rearrange("p (j c) -> p j c", j=4),
                                dt_c.unsqueeze(2).broadcast_to([L, 4, 64]), ALU.mult, [tb.b, dt_tm.b], [xpr_.b])
                            CP("dve", Btm_.t[0:L, :], tbv[0:L, 256:384], [tb.b], [Btm_.b])
                            TTo(xpp_.t[0:L, :].rearrange("p (j c) -> p j c", j=4), xpr_.t[0:L, :].rearrange("p (j c) -> p j c", j=4),
                                we.t[0:L, :].unsqueeze(2).broadcast_to([L, 4, 64]), ALU.mult, [xpr_.b, we.b], [xpp_.b])
                            return Lm_, xpr_, xpp_, Btm_, dta_c

                        def ssd_y(g, L, xsrc, c0, Lm_, xpr_, blk):
                            bcb = nb()
                            MM([(bcb.t[0:L, 0:L], xsrc.t[:, 2, c0:c0 + L], xsrc.t[:, 3, c0:c0 + L], True, True)], [xsrc.b], [bcb.b])
                            MT_ = MT()
                            TTo(MT_.t[0:L, :, 0:L], bcb.t[0:L, 0:L].unsqueeze(1).broadcast_to([L, 4, L]), Lm_.t[0:L, :, 0:L], ALU.mult,
                                [bcb.b, Lm_.b], [MT_.b])
                            by = acc_bank if blk else byring()
                            lst = []
                            for j in range(4):
                                xt = 2 * g + j // 2
                                lst.append((by.t[0:L, 64 * j:64 * j + 64], MT_.t[0:L, j, 0:L], xpr_.t[0:L, 64 * j:64 * j + 64], True, False))
                                lst.append((by.t[0:L, 64 * j:64 * j + 64], xsrc.t[:, j // 2, c0:c0 + L],
                                            diagD.t[:, xt, 64 * (j % 2):64 * (j % 2) + 64], False, True))
                            MM(lst, [MT_.b, xpr_.b, xsrc.b, diagD.b], [by.b])
                            return by

                        def ssd_et(g, L, ci, blk):
                            bc = nb()
                            MM([(bc.t[0:L, 0:32], C("btri" if blk else "tri", slice(0, L), slice(0, L)), dta_tm.t[0:L, ci, :], True, True)],
                               [dta_tm.b, cm.b], [bc.b])
                            et = etm()
                            ACT(et.t[0:L, :], bc.t[0:L, 4 * g:4 * g + 4], AF.Exp, [bc.b], [et.b])
                            return et

                        def ssd_out(g, L, ci, by, dta_c, blk, tile, off, et=None):
                            if et is None:
                                et = ssd_et(g, L, ci, blk)
                            t0_, t1_ = t0(), t1()
                            TTo(t0_.t[0:L, :].rearrange("p (j c) -> p j c", j=4), by.t[0:L, 256:512].rearrange("p (j c) -> p j c", j=4),
                                et.t[0:L, :].unsqueeze(2).broadcast_to([L, 4, 64]), ALU.mult, [by.b, et.b], [t0_.b])
                            TTo(t1_.t[0:L, :], by.t[0:L, 0:256], t0_.t[0:L, :], ALU.add, [by.b, t0_.b], [t1_.b])
                            TTo(t1_.t[0:L, :], t1_.t[0:L, :], zs_tm.t[0:L, ci, :], ALU.mult, [t1_.b, zs_tm.b], [t1_.b])
                            s_, s2_, rs_ = ssd_ss(), ssd_s2(), ssd_rs()
                            ACT(junk2.t[0:L, :], t1_.t[0:L, :], AF.Square, [t1_.b], [junk2.b, s_.b], accum_out=s_.t[0:L, 0:1])
                            RSTD(s_, s2_, rs_, L, 1.0 / 256)
                            yn_ = yn()
                            ACT(yn_.t[0:L, :], t1_.t[0:L, :], AF.Identity, [t1_.b, rs_.b], [yn_.b], scale=rs_.t[0:L, 0:1])
                            tb = nb()
                            tbv = bf(tb)
                            TR([(tbv[:, a * 128:a * 128 + L], yn_.t[0:L, a * 128:(a + 1) * 128], identb.t[0:L, 0:L]) for a in range(2)],
                               [yn_.b, identb.b], [tb.b])
                            mx = mixs2()
                            TTo(mx.t[:, :, 0:L], tbv[:, 0:256].rearrange("p (a c) -> p a c", a=2)[:, :, 0:L],
                                Pv("snw", c=slice(2 * g, 2 * g + 2)).unsqueeze(2).broadcast_to([128, 2, L]), ALU.mult,
                                [tb.b, pv.b], [mx.b])
                            P.dma("sp", mix_scr[tile, :, 16 + 2 * g:16 + 2 * g + 2, off:off + L], mx.t[:, :, 0:L], mx.b,
                                  reads=[mx.b, mixb[tile]])

                        for g in range(8):
                            base = 49 + 6 * g
                            gts = [2 * g, 2 * g + 1, 16 + g, 24 + g]
                            for q in range(3):
                                gemm_tile(wl(w_in_t[base + q]), 128, xbufs, chunks, xbc_evac(q))
                                if DBG >= 1.7:
                                    conv_tile(q, gts[q])
                            if pre and DBG < 1.8:
                                MEMSET(hT.t[:], 0.0, [hT.b])
                            elif pre:
                                gemm_tile(wl(w_in_t[base + 3]), 128, xbufs, [(1024, 1032)],
                                          lambda bk, c0, c1: TSM(halo_all.t[:, 24 + g, :], bk.t[:, 5:8], Pv("flag"), [bk.b, pv.b], [halo_all.b]))
                                MEMSET(hT.t[:], 0.0, [hT.b])
                            else:
                                gemm_tile(wl(w_in_t[base + 3]), 128, xbufs, chunks, xbc_evac(3))
                                conv_tile(3, gts[3])
                                for j in range(2):
                                    gemm_tile(wl(w_in_t[base + 4 + j]), 128, xbufs, chunks, z_evac(j))
                                P.dma("sp", hT.t[:], hT_scr[g], hT.b, reads=[hscrb[g]], writes=[hT.b])
                                CP("act", hTb.t[:], hT.t[:], [hT.b], [hTb.b])
                            HS = {}

                            def sF(k, g=g):
                                L, col, ci = scan[k]
                                pc = 128 * ci if ci < 8 else 1024
                                Lm_, xpr_, xpp_, Btm_, dta_c = ssd_front(g, L, xcp, pc, ci, False)
                                by = et = None
                                if not pre:
                                    by = ssd_y(g, L, xcp, pc, Lm_, xpr_, False)
                                    et = ssd_et(g, L, ci, False)
                                bcl = nb()
                                MM([(bcl.t[:, 0:32], C("ones", slice(0, L), slice(0, 128)), dta_tm.t[0:L, ci, :], True, True)], [dta_tm.b, cm.b], [bcl.b])
                                El_ = El()
                                ACT(El_.t[:, :], bcl.t[:, 4 * g:4 * g + 4], AF.Exp, [bcl.b], [El_.b])
                                HS[k] = (xpp_, Btm_, dta_c, by, et, El_)

                            def sT(k, g=g):
                                L, col, ci = scan[k]
                                pc = 128 * ci if ci < 8 else 1024
                                xpp_, Btm_, dta_c, by, et, El_ = HS.pop(k)
                                if not pre:
                                    MM([(by.t[0:L, 256:512], xcp.t[:, 3, pc:pc + L], hTb.t[:, :], True, True)], [xcp.b, hTb.b], [by.b])
                                bh = nb()
                                MM([(bh.t[:, 0:256], Btm_.t[0:L, :], xpp_.t[0:L, :], True, True)], [Btm_.b, xpp_.b], [bh.b])
                                TTo(htmp.t[:, :].rearrange("p (j c) -> p j c", j=4), hT.t[:, :].rearrange("p (j c) -> p j c", j=4),
                                    El_.t[:, :].unsqueeze(2).broadcast_to([128, 4, 64]), ALU.mult, [hT.b, El_.b], [htmp.b])
                                TTo(hT.t[:, :], htmp.t[:, :], bh.t[:, 0:256], ALU.add, [htmp.b, bh.b], [hT.b])
                                if not pre:
                                    CP("act", hTb.t[:], hT.t[:], [hT.b], [hTb.b])
                                    tile, off = (ci, 0) if ci < 8 else (8, 64)
                                    ssd_out(g, L, ci, by, dta_c, False, tile, off, et)

                            sF(0)
                            for k in range(len(scan)):
                                INTER.run(lambda k=k: sT(k), (lambda k=k: sF(k + 1)) if k + 1 < len(scan) else None)
                            if pre:
                                TSM(hT.t[:], hT.t[:], Pv("flag"), [hT.b, pv.b], [hT.b])
                                P.dma("sp", hT_scr[g], hT.t[:], hT.b, reads=[hT.b], writes=[hscrb[g]])
                            else:
                                bk = nb()
                                TR([(bk.t[:, a * 128:(a + 1) * 128], hT.t[:, a * 128:(a + 1) * 128], identf) for a in range(2)],
                                   [hT.b, identF.b], [bk.b])
                                ho = hout()
                                CP("dve", ho.t[:, :, :], bk.t[:, 0:256].rearrange("p (a c) -> p a c", a=2), [bk.b], [ho.b])
                                P.dma("sp", ssm_p_o[4 * g:4 * g + 4].rearrange("(t jj) p n -> (jj p) t n", jj=2), ho.t[:], ho.b,
                                      reads=[ho.b], is_out=True)
                                L, ci = 64, 8

                                def ldh(i, g=g):
                                    hn = hnat()
                                    P.dma("sp", hn.t[:], sssm[i, 4 * g:4 * g + 4].rearrange("(t jj) p n -> (jj p) t n", jj=2), hn.b,
                                          writes=[hn.b])
                                    return hn
                                lds = [ldh(0)]
                                Lm_, xpr_, xpp_, Btm_, dta_c = ssd_front(g, L, xcs, 0, ci, True)
                                by = ssd_y(g, L, xcs, 0, Lm_, xpr_, True)
                                CP("dve", dta_e.t[:, :, :].rearrange("p t (jj c) -> p t jj c", jj=2),
                                   dta_c.rearrange("p (t jj) -> p t jj", t=2).unsqueeze(3).broadcast_to([64, 2, 2, 64]),
                                   [dta_tm.b], [dta_e.b])
                                bdec = nb()
                                MM([(bdec.t[:, a * 16:(a + 1) * 16], dta_e.t[0:64, a, :], C("seqmask", slice(0, 64)), True, True)
                                    for a in range(2)], [dta_e.b, cm.b], [bdec.b])
                                ACT(dec.t[:, :, :], bdec.t[:, 0:32].rearrange("p (a s) -> p a s", a=2), AF.Exp, [bdec.b], [dec.b])
                                for i in range(16):
                                    if i + 1 < 16:
                                        lds.append(ldh(i + 1))
                                    hn = lds[i]
                                    bt = nb()
                                    TR([(bt.t[:, a * 128:(a + 1) * 128], hn.t[:, a, :], identf) for a in range(2)], [hn.b, identF.b], [bt.b])
                                    hb = hTbi()
                                    CP("act", hb.t[:, :], bt.t[:, 0:256], [bt.b], [hb.b])
                                    cmk, xm_ = Cm_(), xpm()
                                    TTo(cmk.t[:, :], xcs.t[:, 3, :], C("colmask", c=slice(64 * i, 64 * i + 64)), ALU.mult, [xcs.b, cm.b], [cmk.b])
                                    MM([(by.t[0:L, 256:512], cmk.t[:, :], hb.t[:, :], i == 0, i == 15)], [cmk.b, hb.b], [by.b])
                                    TSM(xm_.t[0:64, :], xpp_.t[0:64, :], C("seqmask", slice(0, 64), slice(i, i + 1)), [xpp_.b, cm.b], [xm_.b])
                                    bh = nb()
                                    MM([(bh.t[:, a * 128:(a + 1) * 128], xm_.t[0:64, a * 128:(a + 1) * 128], Btm_.t[0:64, :], True, True)
                                        for a in range(2)], [xm_.b, Btm_.b], [bh.b])
                                    ho = hout()
                                    for a in range(2):
                                        STT(ho.t[:, a, :], hn.t[:, a, :], dec.t[:, a, i:i + 1], bh.t[:, a * 128:(a + 1) * 128], ALU.mult, ALU.add,
                                            [hn.b, dec.b, bh.b], [ho.b])
                                    P.dma("sp", ssm_s_o[i, 4 * g:4 * g + 4].rearrange("(t jj) p n -> (jj p) t n", jj=2), ho.t[:], ho.b,
                                          reads=[ho.b], is_out=True)
                                ssd_out(g, L, ci, by, dta_c, True, 8, 0)

        if stage >= 3:
            groups = [([0, 1, 2, 3], 512), ([4, 5, 6, 7, 8], 584)]
            with Scope() as pc:
                facc = sb("facc", [128, 5, 4096], F32, pc)
                faccb = [Buf(f"facc{t}") for t in range(5)]
                pc.bufs.extend(faccb)
                junkL = sb("junkL", [128, 4096], BF16, pc)
                st_ = {k: ring(k, [128, 1], F32, 2, pc) for k in ["s1", "s2", "mean", "msq", "var", "sd", "rs", "nmr"]}
                for gi, (tiles, ntg) in enumerate(groups):
                    tn = [(128 if t < 8 else 72) for t in tiles]
                    tchunks = [(0, 512)] if ntg == 512 else [(0, 292), (292, 584)]
                    for ti, t in enumerate(tiles):
                        P.dma("sp", facc.t[0:tn[ti], ti, :], xm[t * 128:t * 128 + tn[ti], :], faccb[ti], writes=[faccb[ti]])
                    with Scope() as pcc:
                        mixT = sb("mixT", [128, 32, 584], BF16, pcc)
                        ostage = sb("ostage", [128, 4, 584], F32, pcc)
                        for ti, t in enumerate(tiles):
                            P.dma("sp", mixT.t[:, :, ti * 128:ti * 128 + tn[ti]], mix_scr[t, :, :, 0:tn[ti]], mixT.b,
                                  writes=[mixT.b, mixb[t]])
                        for nq in range(8):
                            for a in range(4):
                                slot = wl(w_out_t[4 * nq + a])
                                for (c0, c1) in tchunks:
                                    bk = nb()
                                    MM([(bk.t[:, 0:c1 - c0], slot.t[:, ft, :], mixT.t[:, ft, c0:c1], ft == 0, ft == 31)
                                        for ft in range(32)], [slot.b, mixT.b], [bk.b])
                                    CP("act", ostage.t[:, a, c0:c1], bk.t[:, 0:c1 - c0], [bk.b], [ostage.b])
                            for ti in range(len(tiles)):
                                n = tn[ti]
                                bk = nb()
                                TR([(bk.t[0:n, a * 128:(a + 1) * 128], ostage.t[:, a, ti * 128:ti * 128 + n], identf) for a in range(4)],
                                   [ostage.b, identF.b], [bk.b])
                                STT(facc.t[0:n, ti, nq * 512:(nq + 1) * 512], facc.t[0:n, ti, nq * 512:(nq + 1) * 512], ALPHA,
                                    bk.t[0:n, :], ALU.mult, ALU.add, [faccb[ti], bk.b], [faccb[ti]])

                    def layer_norm(ti, n, lnc, scale):
                        fa = facc.t[0:n, ti, :]
                        s1, s2, mean, msq, var, sd, rs, nmr = [st_[k]() for k in ["s1", "s2", "mean", "msq", "var", "sd", "rs", "nmr"]]
                        ACT(junkL.t[0:n, :], fa, AF.Identity, [faccb[ti]], [junkL.b, s1.b], accum_out=s1.t[0:n, :])
                        ACT(junkL.t[0:n, :], fa, AF.Square, [faccb[ti]], [junkL.b, s2.b], accum_out=s2.t[0:n, :])
                        TSM(mean.t[0:n, :], s1.t[0:n, :], 1.0 / 4096, [s1.b], [mean.b])
                        TTo(msq.t[0:n, :], mean.t[0:n, :], mean.t[0:n, :], ALU.mult, [mean.b], [msq.b])
                        STT(var.t[0:n, :], s2.t[0:n, :], 1.0 / 4096, msq.t[0:n, :], ALU.mult, ALU.subtract, [s2.b, msq.b], [var.b])
                        ACT(sd.t[0:n, :], var.t[0:n, :], AF.Sqrt, [var.b, epsc.b], [sd.b], bias=epsc.t[0:n, :])
                        P.op("dve", lambda e: e.reciprocal(out=rs.t[0:n, :], in_=sd.t[0:n, :]), [sd.b], [rs.b])
                        if scale != 1.0:
                            TSM(rs.t[0:n, :], rs.t[0:n, :], scale, [rs.b], [rs.b])
                        STT(nmr.t[0:n, :], mean.t[0:n, :], -1.0, rs.t[0:n, :], ALU.mult, ALU.mult, [mean.b, rs.b], [nmr.b])
                        ACT(fa, fa, AF.Identity, [faccb[ti], rs.b, nmr.b], [faccb[ti]], scale=rs.t[0:n, :], bias=nmr.t[0:n, :])
                        TTo(fa, fa, lnc.t[0:n, 0, :], ALU.mult, [faccb[ti], lnc.b], [faccb[ti]])
                        TTo(fa, fa, lnc.t[0:n, 1, :], ALU.add, [faccb[ti], lnc.b], [faccb[ti]])

                    with Scope() as pd:
                        hTg = sb("hTg", [128, 32, 584], BF16, pd)
                        with Scope() as pl:
                            lnc = sb("lnc", [128, 2, 4096], F32, pl)
                            P.dma("sp", lnc.t[:, 0, :], lnp[0].partition_broadcast(128), lnc.b, writes=[lnc.b])
                            P.dma("sp", lnc.t[:, 1, :], lnp[1].partition_broadcast(128), lnc.b, writes=[lnc.b])
                            TSM(lnc.t[:, 1, :], lnc.t[:, 1, :], ALPHA, [lnc.b], [lnc.b])
                            for ti in range(len(tiles)):
                                n = tn[ti]
                                layer_norm(ti, n, lnc, ALPHA)
                                for q in range(8):
                                    bk = nb()
                                    TR([(bk.t[:, a * 128:a * 128 + n], facc.t[0:n, ti, (4 * q + a) * 128:(4 * q + a + 1) * 128],
                                         identf[0:n, 0:n]) for a in range(4)], [faccb[ti], identF.b], [bk.b])
                                    src = bk.t[:].rearrange("p (a t) -> p a t", a=4)[:, :, 0:n]
                                    dst = hTg.t[:, 4 * q:4 * q + 4, ti * 128:ti * 128 + n]
                                    if q % 2 == 0:
                                        ACT(dst, src, AF.Identity, [bk.b], [hTg.b], scale=1.0 / ALPHA)
                                    else:
                                        TSM(dst, src, 1.0 / ALPHA, [bk.b], [hTg.b])
                        with Scope() as pm:
                            h1T = ring("h1T", [128, 8, 584], BF16, 2, pm)
                            wdn = ring("wdn", [128, 8, 512], BF16, 2, pm)
                            rl = ring("rl", [128, 512], F32, 2, pm)
                            for sc in range(16):
                                h1 = h1T()
                                for hk in range(8):
                                    slot = wl(w_up_t[sc * 8 + hk])
                                    for (c0, c1) in tchunks:
                                        bk = nb()
                                        MM([(bk.t[:, 0:c1 - c0], slot.t[:, kt, :], hTg.t[:, kt, c0:c1], kt == 0, kt == 31)
                                            for kt in range(32)], [slot.b, hTg.b], [bk.b])
                                        r_ = rl()
                                        ACT(r_.t[:, 0:c1 - c0], bk.t[:, 0:c1 - c0], AF.Relu, [bk.b], [r_.b])
                                        TTo(h1.t[:, hk, c0:c1], r_.t[:, 0:c1 - c0], r_.t[:, 0:c1 - c0], ALU.mult, [r_.b], [h1.b])
                                for ng in range(8):
                                    wd = wdn()
                                    P.dma("pool", wd.t[:], w_down[sc * 1024:(sc + 1) * 1024, ng * 512:(ng + 1) * 512]
                                          .rearrange("(hk p) c -> p hk c", p=128), wd.b, writes=[wd.b])
                                    for ti in range(len(tiles)):
                                        n = tn[ti]
                                        bk = nb()
                                        MM([(bk.t[0:n, :], h1.t[:, hk, ti * 128:ti * 128 + n], wd.t[:, hk, :], hk == 0, hk == 7)
                                            for hk in range(8)], [h1.b, wd.b], [bk.b])
                                        TTo(facc.t[0:n, ti, ng * 512:(ng + 1) * 512], facc.t[0:n, ti, ng * 512:(ng + 1) * 512],
                                            bk.t[0:n, :], ALU.add, [faccb[ti], bk.b], [faccb[ti]])
                    with Scope() as pl2:
                        lnc2 = sb("lnc2", [128, 2, 4096], F32, pl2)
                        P.dma("sp", lnc2.t[:, 0, :], lnp[2].partition_broadcast(128), lnc2.b, writes=[lnc2.b])
                        P.dma("sp", lnc2.t[:, 1, :], lnp[3].partition_broadcast(128), lnc2.b, writes=[lnc2.b])
                        for ti, t in enumerate(tiles):
                            n = tn[ti]
                            layer_norm(ti, n, lnc2, 1.0)
                            P.dma("sp", y_o[t * 128:t * 128 + n, :], facc.t[0:n, ti, :], faccb[ti], reads=[faccb[ti]], is_out=True)

        P.finish()
        P.emit()
    return nc


def _cmask():
    m = np.zeros((128, NCM), np.float32)
    s = np.arange(128)[:, None]
    t = np.arange(128)[None, :]
    tri = (s <= t).astype(np.float32)
    R = (s > t).astype(np.float32)

    def put(n, a):
        a0, a1 = CM[n]
        m[:a.shape[0], a0:a0 + a.shape[1]] = a
    put("ident", np.eye(128, dtype=np.float32))
    put("tri", tri)
    put("ntri16", -tri / 16.0)
    put("nR16", -R / 16.0)
    put("negtri", -tri)
    put("ones", np.ones((128, 128), np.float32))
    put("negmask4", np.tile(NEG * R, (1, 4)))
    put("Rs", R)
    s6 = np.arange(64)[:, None]
    t6 = np.arange(64)[None, :]
    same = (s6 // 4 == t6 // 4)
    btri = (same & (s6 <= t6)).astype(np.float32)
    Rb = (same & (s6 > t6)).astype(np.float32)
    put("btri", btri)
    put("nbtri16", -btri / 16.0)
    put("nRb16", -Rb / 16.0)
    put("negbtri", -btri)
    put("negbmask4", np.tile(NEG * (1.0 - btri), (1, 4)))
    put("Rbs", Rb)
    put("seqmask", (s6 // 4 == np.arange(16)[None, :]).astype(np.float32))
    col = (np.arange(16)[:, None] == (np.arange(64)[None, :] // 4)).astype(np.float32).reshape(1, 1024)
    put("colmask", np.broadcast_to(col, (128, 1024)))
    return m


def _pvec(inp, flag):
    m = np.zeros((128, NPV), np.float32)

    def put(n, a):
        a0, a1 = PV[n]
        m[:a.shape[0], a0:a0 + a.shape[1]] = a
    put("dtb", np.broadcast_to(inp["dt_bias"][0][None, :], (128, 32)))
    put("alog", np.broadcast_to(inp["a_log"][0][None, :], (128, 32)))
    put("bgk", inp["b_gk"][0][None, :])
    put("wgk", inp["w_gk_up"][0])
    put("gnw", inp["gla_norm_w"][0].reshape(4, 128).T)
    put("convw", inp["conv_w"][0].reshape(4, 32, 128).transpose(2, 1, 0).reshape(128, 128))
    put("convb", inp["conv_b"][0].reshape(32, 128).T)
    put("dsk", np.repeat(inp["d_skip"][0].reshape(16, 2), 64, axis=1).T)
    put("snw", inp["ssd_norm_w"][0].reshape(16, 128).T)
    put("flag", np.full((128, 1), flag, np.float32))
    return m


def _win_cols():
    cols = -np.ones((97, 128), np.int64)
    cols[0, 0:16] = np.arange(6144, 6160)
    cols[0, 32:64] = np.arange(12304, 12336)
    r = np.arange(128)
    for h in range(4):
        b = 1 + 12 * h
        for j in range(2):
            cols[b + j] = 1024 + h * 256 + j * 128 + r
            cols[b + 6 + j] = h * 256 + j * 128 + r
        for j in range(4):
            cols[b + 2 + j] = 2048 + h * 512 + j * 128 + r
            cols[b + 8 + j] = 4096 + h * 512 + j * 128 + r
    for g in range(8):
        b = 49 + 6 * g
        for j in range(2):
            cols[b + j] = 8208 + g * 256 + j * 128 + r
            cols[b + 4 + j] = 6160 + g * 256 + j * 128 + r
        cols[b + 2] = 10256 + g * 128 + r
        cols[b + 3] = 11280 + g * 128 + r
    return cols


_NC_CACHE = {}


def _prep_shared(inp):
    w_in = np.asarray(inp["w_in"][0])
    cols = _win_cols().reshape(-1)
    wz = np.concatenate([w_in, np.zeros((4096, 1), np.float32)], axis=1)
    g = wz[:, np.where(cols < 0, w_in.shape[1], cols)]
    w_in_t = np.ascontiguousarray(g.reshape(32, 128, 97, 128).transpose(2, 1, 0, 3))
    w_out_t = np.ascontiguousarray(np.asarray(inp["w_out"][0]).reshape(32, 128, 32, 128).transpose(2, 1, 0, 3))
    w_up_t = np.ascontiguousarray(np.asarray(inp["w_up"][0]).reshape(32, 128, 128, 128).transpose(2, 1, 0, 3))
    w_down = np.ascontiguousarray(np.asarray(inp["w_down"][0]))
    lnp = np.ascontiguousarray(np.stack([inp["ln1_g"][0], inp["ln1_b"][0], inp["ln2_g"][0], inp["ln2_b"][0]]).astype(np.float32))
    return dict(w_in_t=w_in_t, w_out_t=w_out_t, w_up_t=w_up_t, w_down=w_down, lnp=lnp, cmask=_cmask())


def _make_in_maps(inp):
    inp = {k: np.asarray(v) for k, v in inp.items()}
    shared = _prep_shared(inp)
    xfull = np.concatenate([np.broadcast_to(inp["meta_tokens"][None], (4, 16, 4096)), inp["x_prompt"]], axis=1)
    xs = inp["x_sample"]
    in_maps = []
    for c in range(8):
        b, hh = c // 2, c % 2
        main = xfull[b, hh * 1032:(hh + 1) * 1032]
        samp = xs[16 * c:16 * c + 16].reshape(64, 4096)
        xm = np.ascontiguousarray(np.concatenate([main[:1024], samp, main[1024:]], axis=0))
        xp = np.ascontiguousarray(xfull[b, 0:1032]) if hh == 1 else np.zeros((NPRE, 4096), np.float32)
        m = dict(shared)
        m.update(xm=xm, xp=xp, pvec=_pvec(inp, float(hh)),
                 sgla=np.ascontiguousarray(inp["state_gla"][0, 16 * c:16 * c + 16]),
                 sssm=np.ascontiguousarray(inp["state_ssm"][0, 16 * c:16 * c + 16]),
                 sconv=np.ascontiguousarray(inp["state_conv"][0, 16 * c:16 * c + 16].reshape(48, 4096)))
        in_maps.append(m)
    return in_maps


def _assemble(R, cores=range(8)):
    y_prompt = np.zeros((4, 2048, 4096), np.float32)
    y_sample = np.zeros((128, 4, 4096), np.float32)
    gla_p = np.zeros((1, 4, 4, 256, 512), np.float32)
    ssm_p = np.zeros((1, 4, 32, 64, 128), np.float32)
    conv_p = np.zeros((1, 4, 3, 4096), np.float32)
    gla_s = np.zeros((1, 128, 4, 256, 512), np.float32)
    ssm_s = np.zeros((1, 128, 32, 64, 128), np.float32)
    conv_s = np.zeros((1, 128, 3, 4096), np.float32)
    for k, c in enumerate(cores):
        b, hh = c // 2, c % 2
        r = R[k]
        y = np.asarray(r["y"])
        main = np.concatenate([y[:1024], y[1088:1096]], axis=0)
        if hh == 0:
            y_prompt[b, 0:1016] = main[16:]
        else:
            y_prompt[b, 1016:2048] = main
            gla_p[0, b] = np.asarray(r["gla_p"])
            ssm_p[0, b] = np.asarray(r["ssm_p"])
            conv_p[0, b] = np.asarray(r["conv_p"])
        y_sample[16 * c:16 * c + 16] = y[1024:1088].reshape(16, 4, 4096)
        gla_s[0, 16 * c:16 * c + 16] = np.asarray(r["gla_s"])
        ssm_s[0, 16 * c:16 * c + 16] = np.asarray(r["ssm_s"])
        conv_s[0, 16 * c:16 * c + 16] = np.asarray(r["conv_s"]).reshape(16, 3, 4096)
    return (y_prompt, y_sample, gla_p, ssm_p, conv_p, gla_s, ssm_s, conv_s)


def kernel(**inp):
    in_maps = _make_in_maps(inp)
    if 99 not in _NC_CACHE:
        _NC_CACHE[99] = build_nc(99)
    res = run_bass_kernel_spmd(_NC_CACHE[99], in_maps, core_ids=list(range(8)))
    return _assemble(res.results)
```

```python
import contextlib
import threading
import numpy as np
import concourse.bass as bass
import concourse.mybir as mybir
from concourse.bass_utils import run_bass_kernel_spmd

F32 = mybir.dt.float32
BF16 = mybir.dt.bfloat16
AF = mybir.ActivationFunctionType
ALU = mybir.AluOpType

NT = 1096
NPRE = 1032
NTX = 1147
ALPHA = 2.0 ** 0.25
EPS = 1e-5
NEG = -30000.0

CM = {}
_o = 0
for _n, _w in [("ident", 128), ("tri", 128), ("ntri16", 128), ("nR16", 128), ("negtri", 128), ("ones", 128),
               ("negmask4", 512), ("Rs", 128), ("btri", 64), ("nbtri16", 64), ("nRb16", 64), ("negbtri", 64),
               ("negbmask4", 256), ("Rbs", 64), ("seqmask", 16), ("colmask", 1024)]:
    CM[_n] = (_o, _o + _w)
    _o += _w
NCM = _o
PV = {}
_o = 0
for _n, _w in [("dtb", 32), ("alog", 32), ("bgk", 1024), ("wgk", 1024), ("gnw", 4), ("convw", 128), ("convb", 32),
               ("dsk", 16), ("snw", 16), ("flag", 1)]:
    PV[_n] = (_o, _o + _w)
    _o += _w
NPV = _o


class Ev:
    __slots__ = ("sem", "val", "eng")

    def __init__(self, sem, val, eng):
        self.sem, self.val, self.eng = sem, val, eng


class Buf:
    def __init__(self, name):
        self.name = name
        self.last_write = None
        self.readers = []
        self.dsem = None
        self.dcount = 0


class TT_:
    def __init__(self, t, name):
        self.t = t
        self.b = Buf(name)


class Prog:
    ENGS = ("pe", "act", "dve", "pool", "sp")

    def __init__(self, nc, stack):
        self.nc = nc
        self.stack = stack
        self.ops = {e: [] for e in self.ENGS}
        self.esem = {e: stack.enter_context(nc.semaphore("es_" + e)) for e in self.ENGS}
        self.ecount = {e: 0 for e in self.ENGS}
        self.known = {e: {} for e in self.ENGS}
        self.out_events = []
        self.dma_owners = []

    def _need(self, eng, ev, raw):
        if ev is None:
            return None
        if ev.eng == eng and ev.sem is self.esem[eng]:
            if eng == "pe":
                return None
        if self.known[eng].get(id(ev.sem), 0) >= ev.val:
            return None
        return ev

    def _collect(self, eng, reads, writes):
        waits = {}

        def add(ev, raw):
            ev = self._need(eng, ev, raw)
            if ev is None:
                return
            cur = waits.get(id(ev.sem))
            if cur is None or cur.val < ev.val:
                waits[id(ev.sem)] = ev

        for b in reads:
            add(b.last_write, True)
        for b in writes:
            add(b.last_write, False)
            for r in b.readers:
                add(r, False)
        out = []
        for ev in waits.values():
            self.known[eng][id(ev.sem)] = ev.val
            out.append((ev.sem, ev.val))
        return out

    def _commit(self, ev, reads, writes):
        for b in reads:
            b.readers.append(ev)
            if len(b.readers) > 16:
                best = {}
                for r in b.readers:
                    c = best.get(id(r.sem))
                    if c is None or c.val < r.val:
                        best[id(r.sem)] = r
                b.readers = list(best.values())
        for b in writes:
            b.last_write = ev
            b.readers = []

    def op(self, eng, fn, reads=(), writes=()):
        waits = self._collect(eng, reads, writes)
        self.ecount[eng] += 1
        ev = Ev(self.esem[eng], self.ecount[eng], eng)
        self._commit(ev, reads, writes)
        self.ops[eng].append((waits, fn, (self.esem[eng], 1)))
        INTER.switch()
        return ev

    def dma(self, eng, out_ap, in_ap, owner, reads=(), writes=(), is_out=False):
        if owner.dsem is None:
            owner.dsem = self.stack.enter_context(self.nc.semaphore("ds_" + owner.name))
            self.dma_owners.append(owner)
        waits = self._collect(eng, reads, writes)
        owner.dcount += 16
        ev = Ev(owner.dsem, owner.dcount, "dma")
        self._commit(ev, reads, writes)
        self.ops[eng].append((waits, lambda e: e.dma_start(out=out_ap, in_=in_ap), (owner.dsem, 16)))
        if is_out:
            self.out_events.append(ev)
        INTER.switch()
        return ev

    def finish(self):
        self.ops["sp"].append(([(o.dsem, o.dcount) for o in self.dma_owners], None, None))

    def emit(self):
        nc = self.nc
        with nc.Block() as block:
            def run(e, lst):
                for waits, fn, inc in lst:
                    for sem, val in waits:
                        e.wait_ge(sem, val)
                    if fn is not None:
                        ins = fn(e)
                        if inc is not None:
                            ins.then_inc(inc[0], inc[1])

            @block.sync
            def _(e):
                run(e, self.ops["sp"])

            @block.tensor
            def _(e):
                run(e, self.ops["pe"])

            @block.scalar
            def _(e):
                run(e, self.ops["act"])

            @block.vector
            def _(e):
                run(e, self.ops["dve"])

            @block.gpsimd
            def _(e):
                run(e, self.ops["pool"])


class Inter:
    def __init__(self):
        self.cv = threading.Condition()
        self.active = None
        self.alive = {}
        self.exc = None
        self.first = None
        self.ratio = 1
        self.cnt = 0

    def switch(self):
        me = threading.get_ident()
        if me not in self.alive:
            return
        if me == self.first:
            self.cnt += 1
            if self.cnt % self.ratio:
                return
        with self.cv:
            other = [t for t in self.alive if t != me and self.alive[t]]
            if not other:
                return
            self.active = other[0]
            self.cv.notify_all()
            while self.active != me:
                self.cv.wait()

    def run(self, fa, fb, ratio=1):
        self.ratio = ratio
        self.cnt = 0
        self.first = None
        if fb is None:
            return fa()
        if fa is None:
            return fb()
        self.alive = {}
        self.active = None
        self.exc = None

        def w(f):
            me = threading.get_ident()
            with self.cv:
                while self.active != me:
                    self.cv.wait()
            try:
                f()
            except BaseException as e:
                self.exc = e
            finally:
                with self.cv:
                    self.alive[me] = False
                    other = [t for t in self.alive if self.alive[t]]
                    self.active = other[0] if other else -1
                    self.cv.notify_all()
        ta = threading.Thread(target=w, args=(fa,))
        tb = threading.Thread(target=w, args=(fb,))
        ta.start()
        tb.start()
        with self.cv:
            self.alive = {ta.ident: True, tb.ident: True}
            self.first = ta.ident
            self.active = ta.ident
            self.cv.notify_all()
        ta.join()
        tb.join()
        self.alive = {}
        if self.exc is not None:
            raise self.exc


INTER = Inter()


class Sel:
    def __init__(self, items):
        self.items = items
        self.idx = {}

    def set(self, k):
        self.idx[threading.get_ident()] = k % len(self.items)

    @property
    def t(self):
        return self.items[self.idx.get(threading.get_ident(), 0)].t

    @property
    def b(self):
        return self.items[self.idx.get(threading.get_ident(), 0)].b


class Ring:
    def __init__(self, items):
        self.items = items
        self.i = 0

    def __call__(self):
        x = self.items[self.i % len(self.items)]
        self.i += 1
        return x


def build_nc(stage=99):
    nc = bass.Bass("TRN2", target_bir_lowering=False)

    def D(name, shape, dt=F32, kind="ExternalInput"):
        return nc.dram_tensor(name, shape, dt, kind=kind).ap()

    xm = D("xm", [NT, 4096])
    xp = D("xp", [NPRE, 4096])
    cmask_d = D("cmask", [128, NCM])
    pvec_d = D("pvec", [128, NPV])
    w_in_t = D("w_in_t", [97, 128, 32, 128])
    w_out_t = D("w_out_t", [32, 128, 32, 128])
    w_up_t = D("w_up_t", [128, 128, 32, 128])
    w_down = D("w_down", [16384, 4096])
    lnp = D("lnp", [4, 4096])
    sgla = D("sgla", [16, 4, 256, 512])
    sssm = D("sssm", [16, 32, 64, 128])
    sconv = D("sconv", [48, 4096])
    y_o = D("y", [NT, 4096], kind="ExternalOutput")
    gla_p_o = D("gla_p", [4, 256, 512], kind="ExternalOutput")
    ssm_p_o = D("ssm_p", [32, 64, 128], kind="ExternalOutput")
    conv_p_o = D("conv_p", [3, 4096], kind="ExternalOutput")
    gla_s_o = D("gla_s", [16, 4, 256, 512], kind="ExternalOutput")
    ssm_s_o = D("ssm_s", [16, 32, 64, 128], kind="ExternalOutput")
    conv_s_o = D("conv_s", [48, 4096], kind="ExternalOutput")
    mix_scr = D("mix_scr", [9, 128, 32, 128], BF16, kind="Internal")
    S_scr = D("S_scr", [4, 128, 2, 512], F32, kind="Internal")
    hT_scr = D("hT_scr", [8, 128, 256], F32, kind="Internal")
    mixb = [Buf(f"mixb{t}") for t in range(9)]
    Sscrb = [Buf(f"Sscrb{h}") for h in range(4)]
    hscrb = [Buf(f"hscrb{g}") for g in range(8)]

    with contextlib.ExitStack() as st:
        P = Prog(nc, st)
        P.fence = {}

        class Scope:
            def __init__(self):
                self.stack = contextlib.ExitStack()
                self.bufs = []

            def __enter__(self):
                self.stack.__enter__()
                return self

            def __exit__(self, *a):
                for b in self.bufs:
                    for ev in ([b.last_write] if b.last_write else []) + b.readers:
                        c = P.fence.get(id(ev.sem))
                        if c is None or c.val < ev.val:
                            P.fence[id(ev.sem)] = ev
                return self.stack.__exit__(*a)

        root = Scope()
        root.stack = st
        uid = [0]

        def sb(name, shape, dt=F32, sc=None):
            sc = sc or root
            uid[0] += 1
            nm = f"{name}_{uid[0]}"
            t = TT_(sc.stack.enter_context(nc.sbuf_tensor(nm, shape, dt)), nm)
            t.b.readers = list(P.fence.values())
            sc.bufs.append(t.b)
            return t

        def ring(name, shape, dt, n, sc=None):
            return Ring([sb(f"{name}{i}", shape, dt, sc) for i in range(n)])

        def ACT(out, in_, func, reads, writes, **kw):
            return P.op("act", lambda e: e.activation(out=out, in_=in_, func=func, **kw), reads, writes)

        def TTo(out, in0, in1, op, reads, writes, eng="dve"):
            return P.op(eng, lambda e: e.tensor_tensor(out=out, in0=in0, in1=in1, op=op), reads, writes)

        def TSM(out, in0, s1, reads, writes, eng="dve"):
            return P.op(eng, lambda e: e.tensor_scalar_mul(out=out, in0=in0, scalar1=s1), reads, writes)

        def STT(out, in0, scalar, in1, op0, op1, reads, writes, eng="dve"):
            return P.op(eng, lambda e: e.scalar_tensor_tensor(out=out, in0=in0, scalar=scalar, in1=in1, op0=op0,
                                                              op1=op1), reads, writes)

        def CP(eng, out, in_, reads, writes):
            if eng == "act":
                return P.op("act", lambda e: e.copy(out=out, in_=in_), reads, writes)
            return P.op(eng, lambda e: e.tensor_copy(out=out, in_=in_), reads, writes)

        def MM(lst, reads, writes):
            def fn(e):
                ins = None
                for (o, l, r, s0, s1) in lst:
                    ins = e.matmul(o, lhsT=l, rhs=r, start=s0, stop=s1)
                return ins
            return P.op("pe", fn, reads, writes)

        def TR(lst, reads, writes):
            def fn(e):
                ins = None
                for (o, i, idn) in lst:
                    ins = e.transpose(o, i, idn)
                return ins
            return P.op("pe", fn, reads, writes)

        def MEMSET(ap, val, writes):
            return P.op("dve", lambda e: e.memset(ap, val), [], writes)

        def RSTD(ss_, tmp, rstd_, L, scale):
            ACT(tmp.t[0:L, :], ss_.t[0:L, :], AF.Sqrt, [ss_.b, epsc.b], [tmp.b], scale=scale, bias=epsc.t[0:L, :])
            P.op("dve", lambda e: e.reciprocal(out=rstd_.t[0:L, :], in_=tmp.t[0:L, :]), [tmp.b], [rstd_.b])

        banks = [TT_(st.enter_context(nc.psum_tensor(f"pb{i}", [128, 512], F32)), f"pb{i}") for i in range(8)]
        nb = Ring(banks[0:5])
        byring = Ring([banks[5], banks[6]])
        acc_bank = banks[7]

        def bf(bank):
            return bank.t[:].bitcast(BF16)

        identF = sb("identF", [128, 128])
        identb = sb("identb", [128, 128], BF16)
        epsc = sb("epsc", [128, 1])
        P.dma("sp", identF.t[:], cmask_d[:, CM["ident"][0]:CM["ident"][1]], identF.b, writes=[identF.b])
        CP("act", identb.t[:], identF.t[:], [identF.b], [identb.b])
        MEMSET(epsc.t[:], EPS, [epsc.b])
        identf = identF.t[:]
        wring = ring("wsl", [128, 32, 128], BF16, 3)

        def wl(dram_ap):
            s = wring()
            P.dma("pool", s.t[:], dram_ap, s.b, writes=[s.b])
            return s

        with Scope() as phAB:
            cm = sb("cm", [128, NCM], F32, phAB)
            pv = sb("pv", [128, NPV], F32, phAB)
            a_bc = sb("a_bc", [128, 32], F32, phAB)
            diagD = sb("diagD", [128, 16, 128], BF16, phAB)
            halo_all = sb("halo_all", [128, 32, 3], F32, phAB)
            P.dma("sp", cm.t[:], cmask_d, cm.b, writes=[cm.b])
            P.dma("sp", pv.t[:], pvec_d, pv.b, writes=[pv.b])

            def C(name, r=slice(0, 128), c=None):
                a, b_ = CM[name]
                if c is None:
                    return cm.t[r, a:b_]
                return cm.t[r, a + c.start:a + c.stop]

            def Pv(name, r=slice(0, 128), c=None):
                a, b_ = PV[name]
                if c is None:
                    return pv.t[r, a:b_]
                return pv.t[r, a + c.start:a + c.stop]

            ACT(a_bc.t[:], Pv("alog"), AF.Exp, [pv.b], [a_bc.b])
            TSM(a_bc.t[:], a_bc.t[:], -1.0, [a_bc.b], [a_bc.b])
            for xt in range(16):
                TSM(diagD.t[:, xt, :], identf, Pv("dsk", c=slice(xt, xt + 1)), [identF.b, pv.b], [diagD.b])
            MEMSET(halo_all.t[:], 0.0, [halo_all.b])
            xT = sb("xT", [128, 32, NT], BF16, phAB)
            xTb = [[Buf(f"xT{t}_{k}") for k in range(2)] for t in range(9)]
            for l_ in xTb:
                phAB.bufs.extend(l_)

            def build_xT(x_d, ntok):
                bufs = []
                with Scope() as sx:
                    xst = ring("xst", [128, 4096], F32, 2, sx)
                    for t in range((ntok + 127) // 128):
                        r0 = t * 128
                        n = min(128, ntok - r0)
                        xs_ = xst()
                        P.dma("sp", xs_.t[0:n, :], x_d[r0:r0 + n, :], xs_.b, writes=[xs_.b])
                        for q in range(8):
                            bk = nb()
                            TR([(bk.t[:, j * 128:j * 128 + n], xs_.t[0:n, (4 * q + j) * 128:(4 * q + j + 1) * 128],
                                 identf[0:n, 0:n]) for j in range(4)], [xs_.b, identF.b], [bk.b])
                            src = bk.t[:].rearrange("p (a t) -> p a t", a=4)[:, :, 0:n]
                            CP("act" if q % 2 == 0 else "dve", xT.t[:, 4 * q:4 * q + 4, r0:r0 + n], src, [bk.b],
                               [xTb[t][q % 2]])
                        bufs += xTb[t]
                return bufs

            def gemm_tile(slot, M, xbufs, chunks, evac, mcol0=0):
                for (c0, c1) in chunks:
                    bk = nb()
                    MM([(bk.t[0:M, 0:c1 - c0], slot.t[:, kt, mcol0:mcol0 + M], xT.t[:, kt, c0:c1], kt == 0, kt == 31)
                        for kt in range(32)], [slot.b] + xbufs, [bk.b])
                    evac(bk, c0, c1)

            for pre in (True, False):
                ntok = NPRE if pre else NT
                xbufs = build_xT(xp if pre else xm, ntok)
                if pre:
                    chunks = [(0, 512), (512, 1024), (1024, 1032)]
                    scan = [(128, 128 * c, c) for c in range(8)] + [(8, 1024, 9)]
                else:
                    chunks = [(0, 512), (512, 1024), (1024, 1096)]
                    scan = [(128, 128 * c, c) for c in range(8)] + [(8, 1088, 9)]
                with Scope() as ph:
                    gklT = sb("gklT", [16, NT], F32, ph)
                    dt_tm = sb("dt_tm", [128, 10, 32], F32, ph)
                    dta_tm = sb("dta_tm", [128, 10, 32], F32, ph)
                    sdt = Scope()
                    sdt.__enter__()
                    dtT = sb("dtT", [32, NT], F32, sdt)
                    slot = wl(w_in_t[0])
                    gemm_tile(slot, 16, xbufs, chunks,
                              lambda bk, c0, c1: CP("act", gklT.t[:, c0:c1], bk.t[0:16, 0:c1 - c0], [bk.b], [gklT.b]))
                    gemm_tile(slot, 32, xbufs, chunks,
                              lambda bk, c0, c1: CP("act", dtT.t[:, c0:c1], bk.t[0:32, 0:c1 - c0], [bk.b], [dtT.b]),
                              mcol0=32)
                    dtl = list(scan) + ([] if pre else [(64, 1024, 8)])
                    dtmp = sb("dtmp", [128, 32], F32, sdt)
                    for (L, col, ci) in dtl:
                        bk = nb()
                        TR([(bk.t[0:L, 0:32], dtT.t[0:32, col:col + L], identf[0:32, 0:32])], [dtT.b, identF.b], [bk.b])
                        TTo(dtmp.t[0:L, :], bk.t[0:L, 0:32], Pv("dtb", slice(0, L)), ALU.add, [bk.b, pv.b], [dtmp.b])
                        ACT(dtmp.t[0:L, :], dtmp.t[0:L, :], AF.Exp, [dtmp.b], [dtmp.b])
                        ACT(dt_tm.t[0:L, ci, :], dtmp.t[0:L, :], AF.Ln, [dtmp.b], [dt_tm.b], bias=1.0)
                        TTo(dta_tm.t[0:L, ci, :], dt_tm.t[0:L, ci, :], a_bc.t[0:L, :], ALU.mult, [dt_tm.b, a_bc.b],
                            [dta_tm.b])
                    sdt.__exit__(None, None, None)

                    if stage >= 1:
                      with Scope() as pg:
                        kT = sb("kT", [128, 2, NT], F32, pg)
                        qT = sb("qT", [128, 2, NT], F32, pg)
                        v_tm = sb("v_tm", [128, 10, 512], BF16, pg)
                        rgT = sb("rgT", [128, 4, NT], BF16, pg)
                        vstage = ring("vstage", [128, 512], BF16, 1, pg)
                        rtmp = ring("rtmp", [128, 512], F32, 1, pg)
                        e1 = ring("e1", [128, 256], F32, 1, pg)
                        spb = ring("spb", [128, 256], F32, 1, pg)
                        E4 = ring("E4", [128, 4, 128], F32, 2, pg)
                        Ekn = ring("Ekn", [128, 2, 128], F32, 1, pg)
                        qp = ring("qp", [128, 2, 128], BF16, 2, pg)
                        kp = ring("kp", [128, 2, 128], BF16, 2, pg)
                        kpp = ring("kpp", [128, 2, 128], BF16, 2, pg)
                        kpptm = ring("kpptm", [128, 256], BF16, 2, pg)
                        AT = ring("AT", [128, 128], BF16, 2, pg)
                        ss = ring("ss", [128, 1], F32, 2, pg)
                        ss2 = ring("ss2", [128, 1], F32, 2, pg)
                        rstd = ring("rstd", [128, 1], F32, 2, pg)
                        on = ring("on", [128, 512], BF16, 2, pg)
                        mixst = ring("mixst", [128, 4, 128], BF16, 2, pg)
                        S = sb("S", [128, 2, 512], F32, pg)
                        Sbf = sb("Sbf", [128, 2, 512], BF16, pg)
                        if not pre:
                            kppm = ring("kppm", [64, 256], BF16, 2, pg)
                            qpm = ring("qpm", [128, 2, 64], BF16, 2, pg)
                            Sin = ring("Sin", [128, 2, 512], F32, 2, pg)
                            Sinb = ring("Sinb", [128, 2, 512], BF16, 2, pg)

                        def v_evac(j):
                            def f(bk, c0, c1):
                                vs = vstage()
                                n = c1 - c0
                                CP("act", vs.t[:, 0:n], bk.t[:, 0:n], [bk.b], [vs.b])
                                tb = nb()
                                tbv = bf(tb)
                                if n == 512:
                                    TR([(tbv[:, a * 128:(a + 1) * 128], vs.t[:, a * 128:(a + 1) * 128], identb.t[:, :])
                                        for a in range(4)], [vs.b, identb.b], [tb.b])
                                    ci0 = c0 // 128
                                    CP("dve", v_tm.t[:, ci0:ci0 + 4, j * 128:(j + 1) * 128],
                                       tbv[:, 0:512].rearrange("p (a c) -> p a c", a=4), [tb.b], [v_tm.b])
                                elif pre:
                                    TR([(tbv[0:8, 0:128], vs.t[:, 0:8], identb.t[:, :])], [vs.b, identb.b], [tb.b])
                                    CP("dve", v_tm.t[0:8, 9, j * 128:(j + 1) * 128], tbv[0:8, 0:128], [tb.b], [v_tm.b])
                                else:
                                    TR([(tbv[0:64, 0:128], vs.t[:, 0:64], identb.t[:, :]),
                                        (tbv[0:8, 128:256], vs.t[:, 64:72], identb.t[:, :])], [vs.b, identb.b], [tb.b])
                                    CP("dve", v_tm.t[0:64, 8, j * 128:(j + 1) * 128], tbv[0:64, 0:128], [tb.b], [v_tm.b])
                                    CP("dve", v_tm.t[0:8, 9, j * 128:(j + 1) * 128], tbv[0:8, 128:256], [tb.b], [v_tm.b])
                            return f

                        def r_evac(j):
                            def f(bk, c0, c1):
                                rt = rtmp()
                                n = c1 - c0
                                ACT(rt.t[:, 0:n], bk.t[:, 0:n], AF.Silu, [bk.b], [rt.b])
                                TSM(rgT.t[:, j, c0:c1], rt.t[:, 0:n], Pv("gnw", c=slice(j, j + 1)), [rt.b, pv.b], [rgT.b])
                            return f

                        def gla_front(h, L, col, blk):
                            tri16 = C("nbtri16" if blk else "ntri16", slice(0, L), slice(0, L))
                            r16 = C("nRb16" if blk else "nR16", slice(0, L), slice(0, L))
                            bk = nb()
                            MM([(bk.t[0:L, 0:256], gklT.t[0:16, col:col + L],
                                 Pv("wgk", slice(0, 16), slice(h * 256, h * 256 + 256)), True, False),
                                (bk.t[0:L, 0:256], C("ones", slice(0, 1), slice(0, L)),
                                 Pv("bgk", slice(0, 1), slice(h * 256, h * 256 + 256)), False, True)],
                               [gklT.b, pv.b, cm.b], [bk.b])
                            e1_, sp_ = e1(), spb()
                            ACT(e1_.t[0:L, :], bk.t[0:L, 0:256], AF.Exp, [bk.b], [e1_.b], scale=-1.0)
                            ACT(sp_.t[0:L, :], e1_.t[0:L, :], AF.Ln, [e1_.b], [sp_.b], bias=1.0)
                            b5 = nb()
                            b5v = b5.t[:].rearrange("p (a t) -> p a t", a=4)
                            MM([(b5v[:, kt, 0:L], sp_.t[0:L, kt * 128:(kt + 1) * 128], tri16, True, True) for kt in range(2)] +
                               [(b5v[:, 2 + kt, 0:L], sp_.t[0:L, kt * 128:(kt + 1) * 128], r16, True, True) for kt in range(2)],
                               [sp_.b, cm.b], [b5.b])
                            E4_ = E4()
                            ACT(E4_.t[:, :, 0:L], b5v[:, :, 0:L], AF.Exp, [b5.b], [E4_.b])
                            qp_ = kp_ = None
                            if not pre:
                                Ekn_ = Ekn()
                                ACT(Ekn_.t[:, :, 0:L], b5v[:, 0:2, 0:L], AF.Exp, [b5.b], [Ekn_.b], scale=-1.0)
                                qp_, kp_ = qp(), kp()
                                STT(qp_.t[:, :, 0:L], qT.t[:, :, col:col + L], 0.0625, E4_.t[:, 0:2, 0:L], ALU.mult, ALU.mult,
                                    [qT.b, E4_.b], [qp_.b])
                                TTo(kp_.t[:, :, 0:L], kT.t[:, :, col:col + L], Ekn_.t[:, :, 0:L], ALU.mult, [kT.b, Ekn_.b], [kp_.b])
                            kpp_ = kpp()
                            TTo(kpp_.t[:, :, 0:L], kT.t[:, :, col:col + L], E4_.t[:, 2:4, 0:L], ALU.mult, [kT.b, E4_.b], [kpp_.b])
                            tb = nb()
                            tbv = bf(tb)
                            TR([(tbv[0:L, kt * 128:(kt + 1) * 128], kpp_.t[:, kt, 0:L], identb.t[:, :]) for kt in range(2)],
                               [kpp_.b, identb.b], [tb.b])
                            kt_ = kpptm()
                            CP("act", kt_.t[0:L, :], tbv[0:L, 0:256], [tb.b], [kt_.b])
                            return E4_, qp_, kp_, kt_

                        def gla_out(h, L, col, ob, tile, off):
                            ss_, ss2_, rstd_, on_, mx = ss(), ss2(), rstd(), on(), mixst()
                            ACT(on_.t[0:L, :], ob.t[0:L, :], AF.Square, [ob.b], [on_.b, ss_.b], accum_out=ss_.t[0:L, 0:1])
                            RSTD(ss_, ss2_, rstd_, L, 1.0 / 512)
                            ACT(on_.t[0:L, :], ob.t[0:L, :], AF.Identity, [ob.b, rstd_.b], [on_.b], scale=rstd_.t[0:L, 0:1])
                            tb = nb()
                            tbv = bf(tb)
                            TR([(tbv[:, j * 128:j * 128 + L], on_.t[0:L, j * 128:(j + 1) * 128], identb.t[0:L, 0:L])
                                for j in range(4)], [on_.b, identb.b], [tb.b])
                            TTo(mx.t[:, :, 0:L], tbv[:, 0:512].rearrange("p (a c) -> p a c", a=4)[:, :, 0:L],
                                rgT.t[:, :, col:col + L], ALU.mult, [tb.b, rgT.b], [mx.b])
                            P.dma("sp", mix_scr[tile, :, 4 * h:4 * h + 4, off:off + L], mx.t[:, :, 0:L], mx.b,
                                  reads=[mx.b, mixb[tile]])

                        for h in range(4):
                            base = 1 + 12 * h
                            for j in range(2):
                                gemm_tile(wl(w_in_t[base + j]), 128, xbufs, chunks,
                                          lambda bk, c0, c1, j=j: CP("act", kT.t[:, j, c0:c1], bk.t[:, 0:c1 - c0], [bk.b], [kT.b]))
                            for j in range(4):
                                gemm_tile(wl(w_in_t[base + 2 + j]), 128, xbufs, chunks, v_evac(j))
                            if pre:
                                MEMSET(S.t[:], 0.0, [S.b])
                            else:
                                for j in range(2):
                                    gemm_tile(wl(w_in_t[base + 6 + j]), 128, xbufs, chunks,
                                              lambda bk, c0, c1, j=j: CP("act", qT.t[:, j, c0:c1], bk.t[:, 0:c1 - c0], [bk.b], [qT.b]))
                                for j in range(4):
                                    gemm_tile(wl(w_in_t[base + 8 + j]), 128, xbufs, chunks, r_evac(j))
                                P.dma("sp", S.t[:], S_scr[h], S.b, reads=[Sscrb[h]], writes=[S.b])
                                CP("act", Sbf.t[:], S.t[:], [S.b], [Sbf.b])
                            HG = {}

                            def gF(k, h=h):
                                L, col, ci = scan[k]
                                E4_, qp_, kp_, kt_ = gla_front(h, L, col, False)
                                AT_ = None
                                if not pre:
                                    sbk = nb()
                                    MM([(sbk.t[0:L, 0:L], kp_.t[:, kt, 0:L], qp_.t[:, kt, 0:L], kt == 0, kt == 1) for kt in range(2)],
                                       [kp_.b, qp_.b], [sbk.b])
                                    AT_ = AT()
                                    TTo(AT_.t[0:L, 0:L], sbk.t[0:L, 0:L], C("tri", slice(0, L), slice(0, L)), ALU.mult,
                                        [sbk.b, cm.b], [AT_.b])
                                HG[k] = (E4_, qp_, kp_, kt_, AT_)

                            def gT(k, h=h):
                                L, col, ci = scan[k]
                                E4_, qp_, kp_, kt_, AT_ = HG.pop(k)
                                if not pre:
                                    ob = byring()
                                    MM([(ob.t[0:L, :], AT_.t[0:L, 0:L], v_tm.t[0:L, ci, :], True, False)] +
                                       [(ob.t[0:L, :], qp_.t[:, kt, 0:L], Sbf.t[:, kt, :], False, kt == 1) for kt in range(2)],
                                       [AT_.b, v_tm.b, qp_.b, Sbf.b], [ob.b])
                                for kt in range(2):
                                    ub = nb()
                                    MM([(ub.t[:, :], kt_.t[0:L, kt * 128:(kt + 1) * 128], v_tm.t[0:L, ci, :], True, True)],
                                       [kt_.b, v_tm.b], [ub.b])
                                    STT(S.t[:, kt, :], S.t[:, kt, :], E4_.t[:, kt, L - 1:L], ub.t[:, :], ALU.mult, ALU.add,
                                        [S.b, E4_.b, ub.b], [S.b])
                                if not pre:
                                    CP("act", Sbf.t[:], S.t[:], [S.b], [Sbf.b])
                                    tile, off = (ci, 0) if ci < 8 else (8, 64)
                                    gla_out(h, L, col, ob, tile, off)

                            gF(0)
                            for k in range(len(scan)):
                                INTER.run(lambda k=k: gT(k), (lambda k=k: gF(k + 1)) if k + 1 < len(scan) else None)
                            if pre:
                                TSM(S.t[:], S.t[:], Pv("flag"), [S.b, pv.b], [S.b])
                                P.dma("sp", S_scr[h], S.t[:], S.b, reads=[S.b], writes=[Sscrb[h]])
                            else:
                                P.dma("sp", gla_p_o[h].rearrange("(kt p) v -> p kt v", p=128), S.t[:], S.b, reads=[S.b], is_out=True)
                                L, col, ci = 64, 1024, 8

                                def ld(i, h=h):
                                    s1, s2 = Sin(), Sinb()
                                    src = sgla[i, h].rearrange("(kt p) v -> p kt v", p=128)
                                    P.dma("sp", s1.t[:], src, s1.b, writes=[s1.b])
                                    P.dma("pool", s2.t[:], src, s2.b, writes=[s2.b])
                                    return s1, s2
                                lds = [ld(0)]
                                E4_, qp_, kp_, kt_ = gla_front(h, L, col, True)
                                sbk = nb()
                                MM([(sbk.t[0:L, 0:L], kp_.t[:, kt, 0:L], qp_.t[:, kt, 0:L], kt == 0, kt == 1) for kt in range(2)],
                                   [kp_.b, qp_.b], [sbk.b])
                                AT_ = AT()
                                TTo(AT_.t[0:L, 0:L], sbk.t[0:L, 0:L], C("btri", slice(0, L)), ALU.mult, [sbk.b, cm.b], [AT_.b])
                                ob = acc_bank
                                MM([(ob.t[0:L, :], AT_.t[0:L, 0:L], v_tm.t[0:L, ci, :], True, False)], [AT_.b, v_tm.b], [ob.b])
                                for i in range(16):
                                    if i + 1 < 16:
                                        lds.append(ld(i + 1))
                                    s1, s2 = lds[i]
                                    qm, km = qpm(), kppm()
                                    TTo(qm.t[:, :, :], qp_.t[:, :, 0:64],
                                        C("colmask", c=slice(64 * i, 64 * i + 64)).unsqueeze(1).broadcast_to([128, 2, 64]),
                                        ALU.mult, [qp_.b, cm.b], [qm.b])
                                    TSM(km.t[0:64, :], kt_.t[0:64, :], C("seqmask", slice(0, 64), slice(i, i + 1)), [kt_.b, cm.b], [km.b])
                                    MM([(ob.t[0:L, :], qm.t[:, kt, :], s2.t[:, kt, :], False, (i == 15 and kt == 1))
                                        for kt in range(2)], [qm.b, s2.b], [ob.b])
                                    so = s1
                                    for kt in range(2):
                                        ub = nb()
                                        MM([(ub.t[:, :], km.t[0:64, kt * 128:(kt + 1) * 128], v_tm.t[0:64, ci, :], True, True)],
                                           [km.b, v_tm.b], [ub.b])
                                        STT(so.t[:, kt, :], s1.t[:, kt, :], E4_.t[:, kt, 4 * i + 3:4 * i + 4], ub.t[:, :], ALU.mult,
                                            ALU.add, [E4_.b, ub.b], [so.b])
                                    P.dma("sp", gla_s_o[i, h].rearrange("(kt p) v -> p kt v", p=128), so.t[:], so.b,
                                          reads=[so.b], is_out=True)
                                gla_out(h, L, col, ob, 8, 0)

                    DBG = 9.0
                    if stage >= 2 and (pre or DBG >= 2.1):
                      with Scope() as pq:
                        xbcT = sb("xbcT", [128, 4, NTX], F32, pq)
                        acc = sb("acc", [128, NTX], F32, pq)
                        xcp = Sel([sb("xcp", [128, 4, NPRE], BF16, pq) for _ in range(2)])
                        xcs = Sel([sb("xcs", [128, 4, 64], BF16, pq) for _ in range(2)])
                        zs_tm = Sel([sb("zs_tm", [128, 10, 256], BF16, pq) for _ in range(2)])

                        def setbuf(k):
                            xcp.set(k)
                            xcs.set(k)
                            zs_tm.set(k)
                        zstage = ring("zstage", [128, 512], BF16, 1, pq)
                        R1 = ring("R1", [128, 4, 128], F32, 1, pq)
                        R2 = ring("R2", [128, 4, 128], F32, 1, pq)
                        Lm = ring("Lm", [128, 4, 128], F32, 1, pq)
                        MT = ring("MT", [128, 4, 128], BF16, 1, pq)
                        xpr = ring("xpr", [128, 256], BF16, 1, pq)
                        xpp = ring("xpp", [128, 256], BF16, 2, pq)
                        Btm = ring("Btm", [128, 128], BF16, 2, pq)
                        wend = ring("wend", [128, 4], F32, 2, pq)
                        etm = ring("etm", [128, 4], F32, 2, pq)
                        El = ring("El", [128, 4], F32, 2, pq)
                        t0 = ring("t0", [128, 256], F32, 1, pq)
                        t1 = ring("t1", [128, 256], F32, 1, pq)
                        ssd_ss = ring("sss", [128, 1], F32, 2, pq)
                        ssd_s2 = ring("sss2", [128, 1], F32, 2, pq)
                        ssd_rs = ring("ssrs", [128, 1], F32, 2, pq)
                        junk2 = sb("junk2", [128, 256], BF16, pq)
                        yn = ring("yn", [128, 256], BF16, 1, pq)
                        mixs2 = ring("mixs2", [128, 2, 128], BF16, 2, pq)
                        hT = sb("hTw", [128, 256], F32, pq)
                        hTb = sb("hTb", [128, 256], BF16, pq)
                        htmp = sb("htmp", [128, 256], F32, pq)
                        cstg = sb("cstg", [128, 51], F32, pq)
                        cout = ring("cout", [51, 128], F32, 2, pq)
                        if not pre:
                            convT_all = sb("convT_all", [128, 32, 48], F32, pq)
                            with Scope() as s0:
                                cst = ring("cst", [48, 512], F32, 1, s0)
                                for q in range(8):
                                    cs_ = cst()
                                    P.dma("sp", cs_.t[:], sconv[:, q * 512:(q + 1) * 512], cs_.b, writes=[cs_.b])
                                    bk = nb()
                                    TR([(bk.t[:, j * 48:(j + 1) * 48], cs_.t[0:48, j * 128:(j + 1) * 128], identf[0:48, 0:48])
                                        for j in range(4)], [cs_.b, identF.b], [bk.b])
                                    CP("act", convT_all.t[:, 4 * q:4 * q + 4, :], bk.t[:, 0:192].rearrange("p (a c) -> p a c", a=4),
                                       [bk.b], [convT_all.b])
                            hnat = ring("hnat", [128, 2, 128], F32, 2, pq)
                            hTbi = ring("hTbi", [128, 256], BF16, 2, pq)
                            hout = ring("hout", [128, 2, 128], F32, 2, pq)
                            Cm_ = ring("Cm", [128, 64], BF16, 2, pq)
                            xpm = ring("xpm", [64, 256], BF16, 2, pq)
                            dta_e = sb("dta_e", [64, 2, 128], F32, pq)
                            dec = sb("dec", [128, 2, 16], F32, pq)

                        def xbc_evac(q):
                            def f(bk, c0, c1):
                                if pre or c1 <= 1024:
                                    CP("act", xbcT.t[:, q, 3 + c0:3 + c1], bk.t[:, 0:c1 - c0], [bk.b], [xbcT.b])
                                else:
                                    CP("act", xbcT.t[:, q, 1035:1147].rearrange("p (s w) -> p s w", w=7)[:, :, 3:7],
                                       bk.t[:, 0:64].rearrange("p (s w) -> p s w", w=4), [bk.b], [xbcT.b])
                                    CP("act", xbcT.t[:, q, 1027:1035], bk.t[:, 64:72], [bk.b], [xbcT.b])
                            return f

                        def z_evac(j):
                            def f(bk, c0, c1):
                                vs = zstage()
                                n = c1 - c0
                                ACT(vs.t[:, 0:n], bk.t[:, 0:n], AF.Silu, [bk.b], [vs.b])
                                tb = nb()
                                tbv = bf(tb)
                                if n == 512:
                                    TR([(tbv[:, a * 128:(a + 1) * 128], vs.t[:, a * 128:(a + 1) * 128], identb.t[:, :])
                                        for a in range(4)], [vs.b, identb.b], [tb.b])
                                    ci0 = c0 // 128
                                    CP("dve", zs_tm.t[:, ci0:ci0 + 4, j * 128:(j + 1) * 128],
                                       tbv[:, 0:512].rearrange("p (a c) -> p a c", a=4), [tb.b], [zs_tm.b])
                                else:
                                    TR([(tbv[0:64, 0:128], vs.t[:, 0:64], identb.t[:, :]),
                                        (tbv[0:8, 128:256], vs.t[:, 64:72], identb.t[:, :])], [vs.b, identb.b], [tb.b])
                                    CP("dve", zs_tm.t[0:64, 8, j * 128:(j + 1) * 128], tbv[0:64, 0:128], [tb.b], [zs_tm.b])
                                    CP("dve", zs_tm.t[0:8, 9, j * 128:(j + 1) * 128], tbv[0:8, 128:256], [tb.b], [zs_tm.b])
                            return f

                        def conv_tile(q, gt):
                            npr = NPRE
                            ncol = (3 + npr) if pre else NTX
                            if pre:
                                MEMSET(xbcT.t[:, q, 0:3], 0.0, [xbcT.b])
                            else:
                                CP("act", xbcT.t[:, q, 0:3], halo_all.t[:, gt, :], [halo_all.b], [xbcT.b])
                                CP("act", xbcT.t[:, q, 1035:1147].rearrange("p (s w) -> p s w", w=7)[:, :, 0:3],
                                   convT_all.t[:, gt, :].rearrange("p (s w) -> p s w", w=3), [convT_all.b], [xbcT.b])
                            m = ncol - 3
                            ACT(acc.t[:, 0:m], xbcT.t[:, q, 3:ncol], AF.Identity, [xbcT.b, pv.b], [acc.b],
                                scale=Pv("convw", c=slice(4 * gt + 3, 4 * gt + 4)), bias=Pv("convb", c=slice(gt, gt + 1)))
                            for jj in range(3):
                                STT(acc.t[:, 0:m], xbcT.t[:, q, jj:jj + m], Pv("convw", c=slice(4 * gt + jj, 4 * gt + jj + 1)),
                                    acc.t[:, 0:m], ALU.mult, ALU.add, [xbcT.b, pv.b, acc.b], [acc.b])
                            ACT(xcp.t[:, q, 0:npr], acc.t[:, 0:npr], AF.Silu, [acc.b], [xcp.b])
                            if not pre:
                                ACT(xcs.t[:, q, :].rearrange("p (s w) -> p s w", w=4),
                                    acc.t[:, 1032:1144].rearrange("p (s w) -> p s w", w=7)[:, :, 3:7], AF.Silu, [acc.b], [xcs.b])
                                CP("act", cstg.t[:, 0:48].rearrange("p (s w) -> p s w", w=3),
                                   xbcT.t[:, q, 1035:1147].rearrange("p (s w) -> p s w", w=7)[:, :, 4:7], [xbcT.b], [cstg.b])
                                CP("act", cstg.t[:, 48:51], xbcT.t[:, q, 3 + 1029:3 + 1032], [xbcT.b], [cstg.b])
                                bk = nb()
                                TR([(bk.t[0:51, 0:128], cstg.t[:, 0:51], identf)], [cstg.b, identF.b], [bk.b])
                                co = cout()
                                CP("dve", co.t[:, :], bk.t[0:51, 0:128], [bk.b], [co.b])
                                P.dma("sp", conv_s_o[:, gt * 128:(gt + 1) * 128], co.t[0:48, :], co.b, reads=[co.b], is_out=True)
                                P.dma("sp", conv_p_o[:, gt * 128:(gt + 1) * 128], co.t[48:51, :], co.b, reads=[co.b], is_out=True)
                            else:
                                TSM(halo_all.t[:, gt, :], xbcT.t[:, q, 3 + 1029:3 + 1032], Pv("flag"), [xbcT.b, pv.b], [halo_all.b])

                        def ssd_front(g, L, xsrc, c0, ci, blk):
                            dta_c = dta_tm.t[0:L, ci, 4 * g:4 * g + 4]
                            dt_c = dt_tm.t[0:L, ci, 4 * g:4 * g + 4]
                            Lm_ = None
                            if not pre:
                                R1_, R2_ = R1(), R2()
                                TTo(R1_.t[0:L, :, 0:L], dta_c.unsqueeze(2).broadcast_to([L, 4, L]),
                                    C("btri" if blk else "tri", slice(0, L), slice(0, L)).unsqueeze(1).broadcast_to([L, 4, L]),
                                    ALU.mult, [dta_tm.b, cm.b], [R1_.b])
                                CP("dve", R2_.t[0:L, :, 0:L], dta_c.unsqueeze(2).broadcast_to([L, 4, L]), [dta_tm.b], [R2_.b])
                                bD = nb()
                                bDv = bD.t[0:L, 0:4 * L].rearrange("p (a t) -> p a t", a=4)
                                nm = C("negbmask4" if blk else "negmask4", slice(0, L)).rearrange("p (a t) -> p a t", a=4)[:, :, 0:L]
                                MM([(bDv, C("ones", slice(0, L), slice(0, L)), R1_.t[0:L, :, 0:L], True, False),
                                    (bDv, C("negbtri" if blk else "negtri", slice(0, L), slice(0, L)), R2_.t[0:L, :, 0:L], False, False),
                                    (bDv, identf[0:L, 0:L], nm, False, True)], [R1_.b, R2_.b, cm.b, identF.b], [bD.b])
                                Lm_ = Lm()
                                ACT(Lm_.t[0:L, :, 0:L], bDv, AF.Exp, [bD.b], [Lm_.b])
                            bw = nb()
                            MM([(bw.t[0:L, 0:32], C("Rbs" if blk else "Rs", slice(0, L), slice(0, L)), dta_tm.t[0:L, ci, :], True, True)],
                               [dta_tm.b, cm.b], [bw.b])
                            we = wend()
                            ACT(we.t[0:L, :], bw.t[0:L, 4 * g:4 * g + 4], AF.Exp, [bw.b], [we.b])
                            tb = nb()
                            tbv = bf(tb)
                            TR([(tbv[0:L, a * 128:(a + 1) * 128], xsrc.t[:, a, c0:c0 + L], identb.t[:, :]) for a in range(3)],
                               [xsrc.b, identb.b], [tb.b])
                            xpr_, xpp_, Btm_ = xpr(), xpp(), Btm()
                            TTo(xpr_.t[0:L, :].rearrange("p (j c) -> p j c", j=4), tbv[0:L, 0:256].rearrange("p (j c) -> p j c", j=4),
                                dt_c.unsqueeze(2).broadcast_to([L, 4, 64]), ALU.mult, [tb.b, dt_tm.b], [xpr_.b])
                            CP("dve", Btm_.t[0:L, :], tbv[0:L, 256:384], [tb.b], [Btm_.b])
                            TTo(xpp_.t[0:L, :].rearrange("p (j c) -> p j c", j=4), xpr_.t[0:L, :].rearrange("p (j c) -> p j c", j=4),
                                we.t[0:L, :].unsqueeze(2).broadcast_to([L, 4, 64]), ALU.mult, [xpr_.b, we.b], [xpp_.b])
                            return Lm_, xpr_, xpp_, Btm_, dta_c

                        def ssd_y(g, L, xsrc, c0, Lm_, xpr_, blk):
                            bcb = nb()
                            MM([(bcb.t[0:L, 0:L], xsrc.t[:, 2, c0:c0 + L], xsrc.t[:, 3, c0:c0 + L], True, True)], [xsrc.b], [bcb.b])
                            MT_ = MT()
                            TTo(MT_.t[0:L, :, 0:L], bcb.t[0:L, 0:L].unsqueeze(1).broadcast_to([L, 4, L]), Lm_.t[0:L, :, 0:L], ALU.mult,
                                [bcb.b, Lm_.b], [MT_.b])
                            by = acc_bank if blk else byring()
                            lst = []
                            for j in range(4):
                                xt = 2 * g + j // 2
                                lst.append((by.t[0:L, 64 * j:64 * j + 64], MT_.t[0:L, j, 0:L], xpr_.t[0:L, 64 * j:64 * j + 64], True, False))
                                lst.append((by.t[0:L, 64 * j:64 * j + 64], xsrc.t[:, j // 2, c0:c0 + L],
                                            diagD.t[:, xt, 64 * (j % 2):64 * (j % 2) + 64], False, True))
                            MM(lst, [MT_.b, xpr_.b, xsrc.b, diagD.b], [by.b])
                            return by

                        def ssd_et(g, L, ci, blk):
                            bc = nb()
                            MM([(bc.t[0:L, 0:32], C("btri" if blk else "tri", slice(0, L), slice(0, L)), dta_tm.t[0:L, ci, :], True, True)],
                               [dta_tm.b, cm.b], [bc.b])
                            et = etm()
                            ACT(et.t[0:L, :], bc.t[0:L, 4 * g:4 * g + 4], AF.Exp, [bc.b], [et.b])
                            return et

                        def ssd_out(g, L, ci, by, dta_c, blk, tile, off, et=None):
                            if et is None:
                                et = ssd_et(g, L, ci, blk)
                            t0_, t1_ = t0(), t1()
                            TTo(t0_.t[0:L, :].rearrange("p (j c) -> p j c", j=4), by.t[0:L, 256:512].rearrange("p (j c) -> p j c", j=4),
                                et.t[0:L, :].unsqueeze(2).broadcast_to([L, 4, 64]), ALU.mult, [by.b, et.b], [t0_.b])
                            TTo(t1_.t[0:L, :], by.t[0:L, 0:256], t0_.t[0:L, :], ALU.add, [by.b, t0_.b], [t1_.b])
                            TTo(t1_.t[0:L, :], t1_.t[0:L, :], zs_tm.t[0:L, ci, :], ALU.mult, [t1_.b, zs_tm.b], [t1_.b])
                            s_, s2_, rs_ = ssd_ss(), ssd_s2(), ssd_rs()
                            ACT(junk2.t[0:L, :], t1_.t[0:L, :], AF.Square, [t1_.b], [junk2.b, s_.b], accum_out=s_.t[0:L, 0:1])
                            RSTD(s_, s2_, rs_, L, 1.0 / 256)
                            yn_ = yn()
                            ACT(yn_.t[0:L, :], t1_.t[0:L, :], AF.Identity, [t1_.b, rs_.b], [yn_.b], scale=rs_.t[0:L, 0:1])
                            tb = nb()
                            tbv = bf(tb)
                            TR([(tbv[:, a * 128:a * 128 + L], yn_.t[0:L, a * 128:(a + 1) * 128], identb.t[0:L, 0:L]) for a in range(2)],
                               [yn_.b, identb.b], [tb.b])
                            mx = mixs2()
                            TTo(mx.t[:, :, 0:L], tbv[:, 0:256].rearrange("p (a c) -> p a c", a=2)[:, :, 0:L],
                                Pv("snw", c=slice(2 * g, 2 * g + 2)).unsqueeze(2).broadcast_to([128, 2, L]), ALU.mult,
                                [tb.b, pv.b], [mx.b])
                            P.dma("sp", mix_scr[tile, :, 16 + 2 * g:16 + 2 * g + 2, off:off + L], mx.t[:, :, 0:L], mx.b,
                                  reads=[mx.b, mixb[tile]])

                        def ssd_prep(g):
                            setbuf(g)
                            base = 49 + 6 * g
                            gts = [2 * g, 2 * g + 1, 16 + g, 24 + g]
                            for q in range(3):
                                gemm_tile(wl(w_in_t[base + q]), 128, xbufs, chunks, xbc_evac(q))
                                conv_tile(q, gts[q])
                            if pre:
                                gemm_tile(wl(w_in_t[base + 3]), 128, xbufs, [(1024, 1032)],
                                          lambda bk, c0, c1: TSM(halo_all.t[:, 24 + g, :], bk.t[:, 5:8], Pv("flag"), [bk.b, pv.b], [halo_all.b]))
                            else:
                                gemm_tile(wl(w_in_t[base + 3]), 128, xbufs, chunks, xbc_evac(3))
                                conv_tile(3, gts[3])
                                for j in range(2):
                                    gemm_tile(wl(w_in_t[base + 4 + j]), 128, xbufs, chunks, z_evac(j))

                        def ssd_scan(g):
                            setbuf(g)
                            if pre:
                                MEMSET(hT.t[:], 0.0, [hT.b])
                            else:
                                P.dma("sp", hT.t[:], hT_scr[g], hT.b, reads=[hscrb[g]], writes=[hT.b])
                                CP("act", hTb.t[:], hT.t[:], [hT.b], [hTb.b])
                            HS = {}

                            def sF(k, g=g):
                                L, col, ci = scan[k]
                                pc = 128 * ci if ci < 8 else 1024
                                Lm_, xpr_, xpp_, Btm_, dta_c = ssd_front(g, L, xcp, pc, ci, False)
                                by = et = None
                                if not pre:
                                    by = ssd_y(g, L, xcp, pc, Lm_, xpr_, False)
                                    et = ssd_et(g, L, ci, False)
                                bcl = nb()
                                MM([(bcl.t[:, 0:32], C("ones", slice(0, L), slice(0, 128)), dta_tm.t[0:L, ci, :], True, True)], [dta_tm.b, cm.b], [bcl.b])
                                El_ = El()
                                ACT(El_.t[:, :], bcl.t[:, 4 * g:4 * g + 4], AF.Exp, [bcl.b], [El_.b])
                                HS[k] = (xpp_, Btm_, dta_c, by, et, El_)

                            def sT(k, g=g):
                                L, col, ci = scan[k]
                                pc = 128 * ci if ci < 8 else 1024
                                xpp_, Btm_, dta_c, by, et, El_ = HS.pop(k)
                                if not pre:
                                    MM([(by.t[0:L, 256:512], xcp.t[:, 3, pc:pc + L], hTb.t[:, :], True, True)], [xcp.b, hTb.b], [by.b])
                                bh = nb()
                                MM([(bh.t[:, 0:256], Btm_.t[0:L, :], xpp_.t[0:L, :], True, True)], [Btm_.b, xpp_.b], [bh.b])
                                TTo(htmp.t[:, :].rearrange("p (j c) -> p j c", j=4), hT.t[:, :].rearrange("p (j c) -> p j c", j=4),
                                    El_.t[:, :].unsqueeze(2).broadcast_to([128, 4, 64]), ALU.mult, [hT.b, El_.b], [htmp.b])
                                TTo(hT.t[:, :], htmp.t[:, :], bh.t[:, 0:256], ALU.add, [htmp.b, bh.b], [hT.b])
                                if not pre:
                                    CP("act", hTb.t[:], hT.t[:], [hT.b], [hTb.b])
                                    tile, off = (ci, 0) if ci < 8 else (8, 64)
                                    ssd_out(g, L, ci, by, dta_c, False, tile, off, et)

                            for k in range(len(scan)):
                                sF(k)
                                sT(k)
                            if pre:
                                TSM(hT.t[:], hT.t[:], Pv("flag"), [hT.b, pv.b], [hT.b])
                                P.dma("sp", hT_scr[g], hT.t[:], hT.b, reads=[hT.b], writes=[hscrb[g]])
                            else:
                                bk = nb()
                                TR([(bk.t[:, a * 128:(a + 1) * 128], hT.t[:, a * 128:(a + 1) * 128], identf) for a in range(2)],
                                   [hT.b, identF.b], [bk.b])
                                ho = hout()
                                CP("dve", ho.t[:, :, :], bk.t[:, 0:256].rearrange("p (a c) -> p a c", a=2), [bk.b], [ho.b])
                                P.dma("sp", ssm_p_o[4 * g:4 * g + 4].rearrange("(t jj) p n -> (jj p) t n", jj=2), ho.t[:], ho.b,
                                      reads=[ho.b], is_out=True)
                                L, ci = 64, 8

                                def ldh(i, g=g):
                                    hn = hnat()
                                    P.dma("sp", hn.t[:], sssm[i, 4 * g:4 * g + 4].rearrange("(t jj) p n -> (jj p) t n", jj=2), hn.b,
                                          writes=[hn.b])
                                    return hn
                                lds = [ldh(0)]
                                Lm_, xpr_, xpp_, Btm_, dta_c = ssd_front(g, L, xcs, 0, ci, True)
                                by = ssd_y(g, L, xcs, 0, Lm_, xpr_, True)
                                CP("dve", dta_e.t[:, :, :].rearrange("p t (jj c) -> p t jj c", jj=2),
                                   dta_c.rearrange("p (t jj) -> p t jj", t=2).unsqueeze(3).broadcast_to([64, 2, 2, 64]),
                                   [dta_tm.b], [dta_e.b])
                                bdec = nb()
                                MM([(bdec.t[:, a * 16:(a + 1) * 16], dta_e.t[0:64, a, :], C("seqmask", slice(0, 64)), True, True)
                                    for a in range(2)], [dta_e.b, cm.b], [bdec.b])
                                ACT(dec.t[:, :, :], bdec.t[:, 0:32].rearrange("p (a s) -> p a s", a=2), AF.Exp, [bdec.b], [dec.b])
                                for i in range(16):
                                    if i + 1 < 16:
                                        lds.append(ldh(i + 1))
                                    hn = lds[i]
                                    bt = nb()
                                    TR([(bt.t[:, a * 128:(a + 1) * 128], hn.t[:, a, :], identf) for a in range(2)], [hn.b, identF.b], [bt.b])
                                    hb = hTbi()
                                    CP("act", hb.t[:, :], bt.t[:, 0:256], [bt.b], [hb.b])
                                    cmk, xm_ = Cm_(), xpm()
                                    TTo(cmk.t[:, :], xcs.t[:, 3, :], C("colmask", c=slice(64 * i, 64 * i + 64)), ALU.mult, [xcs.b, cm.b], [cmk.b])
                                    MM([(by.t[0:L, 256:512], cmk.t[:, :], hb.t[:, :], i == 0, i == 15)], [cmk.b, hb.b], [by.b])
                                    TSM(xm_.t[0:64, :], xpp_.t[0:64, :], C("seqmask", slice(0, 64), slice(i, i + 1)), [xpp_.b, cm.b], [xm_.b])
                                    bh = nb()
                                    MM([(bh.t[:, a * 128:(a + 1) * 128], xm_.t[0:64, a * 128:(a + 1) * 128], Btm_.t[0:64, :], True, True)
                                        for a in range(2)], [xm_.b, Btm_.b], [bh.b])
                                    ho = hout()
                                    for a in range(2):
                                        STT(ho.t[:, a, :], hn.t[:, a, :], dec.t[:, a, i:i + 1], bh.t[:, a * 128:(a + 1) * 128], ALU.mult, ALU.add,
                                            [hn.b, dec.b, bh.b], [ho.b])
                                    P.dma("sp", ssm_s_o[i, 4 * g:4 * g + 4].rearrange("(t jj) p n -> (jj p) t n", jj=2), ho.t[:], ho.b,
                                          reads=[ho.b], is_out=True)
                                ssd_out(g, L, ci, by, dta_c, True, 8, 0)

                        ssd_prep(0)
                        for g in range(8):
                            INTER.run(lambda g=g: ssd_scan(g), (lambda g=g: ssd_prep(g + 1)) if g + 1 < 8 else None, ratio=5)

        if stage >= 3:
            groups = [([0, 1, 2, 3], 512), ([4, 5, 6, 7, 8], 584)]
            with Scope() as pc:
                facc = sb("facc", [128, 5, 4096], F32, pc)
                faccb = [Buf(f"facc{t}") for t in range(5)]
                pc.bufs.extend(faccb)
                junkL = sb("junkL", [128, 4096], BF16, pc)
                st_ = {k: ring(k, [128, 1], F32, 2, pc) for k in ["s1", "s2", "mean", "msq", "var", "sd", "rs", "nmr"]}
                for gi, (tiles, ntg) in enumerate(groups):
                    tn = [(128 if t < 8 else 72) for t in tiles]
                    tchunks = [(0, 512)] if ntg == 512 else [(0, 292), (292, 584)]
                    for ti, t in enumerate(tiles):
                        P.dma("sp", facc.t[0:tn[ti], ti, :], xm[t * 128:t * 128 + tn[ti], :], faccb[ti], writes=[faccb[ti]])
                    with Scope() as pcc:
                        mixT = sb("mixT", [128, 32, 584], BF16, pcc)
                        ostage = sb("ostage", [128, 4, 584], F32, pcc)
                        for ti, t in enumerate(tiles):
                            P.dma("sp", mixT.t[:, :, ti * 128:ti * 128 + tn[ti]], mix_scr[t, :, :, 0:tn[ti]], mixT.b,
                                  writes=[mixT.b, mixb[t]])
                        for nq in range(8):
                            for a in range(4):
                                slot = wl(w_out_t[4 * nq + a])
                                for (c0, c1) in tchunks:
                                    bk = nb()
                                    MM([(bk.t[:, 0:c1 - c0], slot.t[:, ft, :], mixT.t[:, ft, c0:c1], ft == 0, ft == 31)
                                        for ft in range(32)], [slot.b, mixT.b], [bk.b])
                                    CP("act", ostage.t[:, a, c0:c1], bk.t[:, 0:c1 - c0], [bk.b], [ostage.b])
                            for ti in range(len(tiles)):
                                n = tn[ti]
                                bk = nb()
                                TR([(bk.t[0:n, a * 128:(a + 1) * 128], ostage.t[:, a, ti * 128:ti * 128 + n], identf) for a in range(4)],
                                   [ostage.b, identF.b], [bk.b])
                                STT(facc.t[0:n, ti, nq * 512:(nq + 1) * 512], facc.t[0:n, ti, nq * 512:(nq + 1) * 512], ALPHA,
                                    bk.t[0:n, :], ALU.mult, ALU.add, [faccb[ti], bk.b], [faccb[ti]])

                    def layer_norm(ti, n, lnc, scale):
                        fa = facc.t[0:n, ti, :]
                        s1, s2, mean, msq, var, sd, rs, nmr = [st_[k]() for k in ["s1", "s2", "mean", "msq", "var", "sd", "rs", "nmr"]]
                        ACT(junkL.t[0:n, :], fa, AF.Identity, [faccb[ti]], [junkL.b, s1.b], accum_out=s1.t[0:n, :])
                        ACT(junkL.t[0:n, :], fa, AF.Square, [faccb[ti]], [junkL.b, s2.b], accum_out=s2.t[0:n, :])
                        TSM(mean.t[0:n, :], s1.t[0:n, :], 1.0 / 4096, [s1.b], [mean.b])
                        TTo(msq.t[0:n, :], mean.t[0:n, :], mean.t[0:n, :], ALU.mult, [mean.b], [msq.b])
                        STT(var.t[0:n, :], s2.t[0:n, :], 1.0 / 4096, msq.t[0:n, :], ALU.mult, ALU.subtract, [s2.b, msq.b], [var.b])
                        ACT(sd.t[0:n, :], var.t[0:n, :], AF.Sqrt, [var.b, epsc.b], [sd.b], bias=epsc.t[0:n, :])
                        P.op("dve", lambda e: e.reciprocal(out=rs.t[0:n, :], in_=sd.t[0:n, :]), [sd.b], [rs.b])
                        if scale != 1.0:
                            TSM(rs.t[0:n, :], rs.t[0:n, :], scale, [rs.b], [rs.b])
                        STT(nmr.t[0:n, :], mean.t[0:n, :], -1.0, rs.t[0:n, :], ALU.mult, ALU.mult, [mean.b, rs.b], [nmr.b])
                        ACT(fa, fa, AF.Identity, [faccb[ti], rs.b, nmr.b], [faccb[ti]], scale=rs.t[0:n, :], bias=nmr.t[0:n, :])
                        TTo(fa, fa, lnc.t[0:n, 0, :], ALU.mult, [faccb[ti], lnc.b], [faccb[ti]])
                        TTo(fa, fa, lnc.t[0:n, 1, :], ALU.add, [faccb[ti], lnc.b], [faccb[ti]])

                    with Scope() as pd:
                        hTg = sb("hTg", [128, 32, 584], BF16, pd)
                        with Scope() as pl:
                            lnc = sb("lnc", [128, 2, 4096], F32, pl)
                            P.dma("sp", lnc.t[:, 0, :], lnp[0].partition_broadcast(128), lnc.b, writes=[lnc.b])
                            P.dma("sp", lnc.t[:, 1, :], lnp[1].partition_broadcast(128), lnc.b, writes=[lnc.b])
                            TSM(lnc.t[:, 1, :], lnc.t[:, 1, :], ALPHA, [lnc.b], [lnc.b])
                            for ti in range(len(tiles)):
                                n = tn[ti]
                                layer_norm(ti, n, lnc, ALPHA)
                                for q in range(8):
                                    bk = nb()
                                    TR([(bk.t[:, a * 128:a * 128 + n], facc.t[0:n, ti, (4 * q + a) * 128:(4 * q + a + 1) * 128],
                                         identf[0:n, 0:n]) for a in range(4)], [faccb[ti], identF.b], [bk.b])
                                    src = bk.t[:].rearrange("p (a t) -> p a t", a=4)[:, :, 0:n]
                                    dst = hTg.t[:, 4 * q:4 * q + 4, ti * 128:ti * 128 + n]
                                    if q % 2 == 0:
                                        ACT(dst, src, AF.Identity, [bk.b], [hTg.b], scale=1.0 / ALPHA)
                                    else:
                                        TSM(dst, src, 1.0 / ALPHA, [bk.b], [hTg.b])
                        with Scope() as pm:
                            h1T = ring("h1T", [128, 8, 584], BF16, 2, pm)
                            wdn = ring("wdn", [128, 8, 512], BF16, 2, pm)
                            rl = ring("rl", [128, 512], F32, 2, pm)
                            for sc in range(16):
                                h1 = h1T()
                                for hk in range(8):
                                    slot = wl(w_up_t[sc * 8 + hk])
                                    for (c0, c1) in tchunks:
                                        bk = nb()
                                        MM([(bk.t[:, 0:c1 - c0], slot.t[:, kt, :], hTg.t[:, kt, c0:c1], kt == 0, kt == 31)
                                            for kt in range(32)], [slot.b, hTg.b], [bk.b])
                                        r_ = rl()
                                        ACT(r_.t[:, 0:c1 - c0], bk.t[:, 0:c1 - c0], AF.Relu, [bk.b], [r_.b])
                                        TTo(h1.t[:, hk, c0:c1], r_.t[:, 0:c1 - c0], r_.t[:, 0:c1 - c0], ALU.mult, [r_.b], [h1.b])
                                for ng in range(8):
                                    wd = wdn()
                                    P.dma("pool", wd.t[:], w_down[sc * 1024:(sc + 1) * 1024, ng * 512:(ng + 1) * 512]
                                          .rearrange("(hk p) c -> p hk c", p=128), wd.b, writes=[wd.b])
                                    for ti in range(len(tiles)):
                                        n = tn[ti]
                                        bk = nb()
                                        MM([(bk.t[0:n, :], h1.t[:, hk, ti * 128:ti * 128 + n], wd.t[:, hk, :], hk == 0, hk == 7)
                                            for hk in range(8)], [h1.b, wd.b], [bk.b])
                                        TTo(facc.t[0:n, ti, ng * 512:(ng + 1) * 512], facc.t[0:n, ti, ng * 512:(ng + 1) * 512],
                                            bk.t[0:n, :], ALU.add, [faccb[ti], bk.b], [faccb[ti]])
                    with Scope() as pl2:
                        lnc2 = sb("lnc2", [128, 2, 4096], F32, pl2)
                        P.dma("sp", lnc2.t[:, 0, :], lnp[2].partition_broadcast(128), lnc2.b, writes=[lnc2.b])
                        P.dma("sp", lnc2.t[:, 1, :], lnp[3].partition_broadcast(128), lnc2.b, writes=[lnc2.b])
                        for ti, t in enumerate(tiles):
                            n = tn[ti]
                            layer_norm(ti, n, lnc2, 1.0)
                            P.dma("sp", y_o[t * 128:t * 128 + n, :], facc.t[0:n, ti, :], faccb[ti], reads=[faccb[ti]], is_out=True)

        P.finish()
        P.emit()
    return nc


def _cmask():
    m = np.zeros((128, NCM), np.float32)
    s = np.arange(128)[:, None]
    t = np.arange(128)[None, :]
    tri = (s <= t).astype(np.float32)
    R = (s > t).astype(np.float32)

    def put(n, a):
        a0, a1 = CM[n]
        m[:a.shape[0], a0:a0 + a.shape[1]] = a
    put("ident", np.eye(128, dtype=np.float32))
    put("tri", tri)
    put("ntri16", -tri / 16.0)
    put("nR16", -R / 16.0)
    put("negtri", -tri)
    put("ones", np.ones((128, 128), np.float32))
    put("negmask4", np.tile(NEG * R, (1, 4)))
    put("Rs", R)
    s6 = np.arange(64)[:, None]
    t6 = np.arange(64)[None, :]
    same = (s6 // 4 == t6 // 4)
    btri = (same & (s6 <= t6)).astype(np.float32)
    Rb = (same & (s6 > t6)).astype(np.float32)
    put("btri", btri)
    put("nbtri16", -btri / 16.0)
    put("nRb16", -Rb / 16.0)
    put("negbtri", -btri)
    put("negbmask4", np.tile(NEG * (1.0 - btri), (1, 4)))
    put("Rbs", Rb)
    put("seqmask", (s6 // 4 == np.arange(16)[None, :]).astype(np.float32))
    col = (np.arange(16)[:, None] == (np.arange(64)[None, :] // 4)).astype(np.float32).reshape(1, 1024)
    put("colmask", np.broadcast_to(col, (128, 1024)))
    return m


def _pvec(inp, flag):
    m = np.zeros((128, NPV), np.float32)

    def put(n, a):
        a0, a1 = PV[n]
        m[:a.shape[0], a0:a0 + a.shape[1]] = a
    put("dtb", np.broadcast_to(inp["dt_bias"][0][None, :], (128, 32)))
    put("alog", np.broadcast_to(inp["a_log"][0][None, :], (128, 32)))
    put("bgk", inp["b_gk"][0][None, :])
    put("wgk", inp["w_gk_up"][0])
    put("gnw", inp["gla_norm_w"][0].reshape(4, 128).T)
    put("convw", inp["conv_w"][0].reshape(4, 32, 128).transpose(2, 1, 0).reshape(128, 128))
    put("convb", inp["conv_b"][0].reshape(32, 128).T)
    put("dsk", np.repeat(inp["d_skip"][0].reshape(16, 2), 64, axis=1).T)
    put("snw", inp["ssd_norm_w"][0].reshape(16, 128).T)
    put("flag", np.full((128, 1), flag, np.float32))
    return m


def _win_cols():
    cols = -np.ones((97, 128), np.int64)
    cols[0, 0:16] = np.arange(6144, 6160)
    cols[0, 32:64] = np.arange(12304, 12336)
    r = np.arange(128)
    for h in range(4):
        b = 1 + 12 * h
        for j in range(2):
            cols[b + j] = 1024 + h * 256 + j * 128 + r
            cols[b + 6 + j] = h * 256 + j * 128 + r
        for j in range(4):
            cols[b + 2 + j] = 2048 + h * 512 + j * 128 + r
            cols[b + 8 + j] = 4096 + h * 512 + j * 128 + r
    for g in range(8):
        b = 49 + 6 * g
        for j in range(2):
            cols[b + j] = 8208 + g * 256 + j * 128 + r
            cols[b + 4 + j] = 6160 + g * 256 + j * 128 + r
        cols[b + 2] = 10256 + g * 128 + r
        cols[b + 3] = 11280 + g * 128 + r
    return cols


_NC_CACHE = {}


def _prep_shared(inp):
    w_in = np.asarray(inp["w_in"][0])
    cols = _win_cols().reshape(-1)
    wz = np.concatenate([w_in, np.zeros((4096, 1), np.float32)], axis=1)
    g = wz[:, np.where(cols < 0, w_in.shape[1], cols)]
    w_in_t = np.ascontiguousarray(g.reshape(32, 128, 97, 128).transpose(2, 1, 0, 3))
    w_out_t = np.ascontiguousarray(np.asarray(inp["w_out"][0]).reshape(32, 128, 32, 128).transpose(2, 1, 0, 3))
    w_up_t = np.ascontiguousarray(np.asarray(inp["w_up"][0]).reshape(32, 128, 128, 128).transpose(2, 1, 0, 3))
    w_down = np.ascontiguousarray(np.asarray(inp["w_down"][0]))
    lnp = np.ascontiguousarray(np.stack([inp["ln1_g"][0], inp["ln1_b"][0], inp["ln2_g"][0], inp["ln2_b"][0]]).astype(np.float32))
    return dict(w_in_t=w_in_t, w_out_t=w_out_t, w_up_t=w_up_t, w_down=w_down, lnp=lnp, cmask=_cmask())


def _make_in_maps(inp):
    inp = {k: np.asarray(v) for k, v in inp.items()}
    shared = _prep_shared(inp)
    xfull = np.concatenate([np.broadcast_to(inp["meta_tokens"][None], (4, 16, 4096)), inp["x_prompt"]], axis=1)
    xs = inp["x_sample"]
    in_maps = []
    for c in range(8):
        b, hh = c // 2, c % 2
        main = xfull[b, hh * 1032:(hh + 1) * 1032]
        samp = xs[16 * c:16 * c + 16].reshape(64, 4096)
        xm = np.ascontiguousarray(np.concatenate([main[:1024], samp, main[1024:]], axis=0))
        xp = np.ascontiguousarray(xfull[b, 0:1032]) if hh == 1 else np.zeros((NPRE, 4096), np.float32)
        m = dict(shared)
        m.update(xm=xm, xp=xp, pvec=_pvec(inp, float(hh)),
                 sgla=np.ascontiguousarray(inp["state_gla"][0, 16 * c:16 * c + 16]),
                 sssm=np.ascontiguousarray(inp["state_ssm"][0, 16 * c:16 * c + 16]),
                 sconv=np.ascontiguousarray(inp["state_conv"][0, 16 * c:16 * c + 16].reshape(48, 4096)))
        in_maps.append(m)
    return in_maps


def _assemble(R, cores=range(8)):
    y_prompt = np.zeros((4, 2048, 4096), np.float32)
    y_sample = np.zeros((128, 4, 4096), np.float32)
    gla_p = np.zeros((1, 4, 4, 256, 512), np.float32)
    ssm_p = np.zeros((1, 4, 32, 64, 128), np.float32)
    conv_p = np.zeros((1, 4, 3, 4096), np.float32)
    gla_s = np.zeros((1, 128, 4, 256, 512), np.float32)
    ssm_s = np.zeros((1, 128, 32, 64, 128), np.float32)
    conv_s = np.zeros((1, 128, 3, 4096), np.float32)
    for k, c in enumerate(cores):
        b, hh = c // 2, c % 2
        r = R[k]
        y = np.asarray(r["y"])
        main = np.concatenate([y[:1024], y[1088:1096]], axis=0)
        if hh == 0:
            y_prompt[b, 0:1016] = main[16:]
        else:
            y_prompt[b, 1016:2048] = main
            gla_p[0, b] = np.asarray(r["gla_p"])
            ssm_p[0, b] = np.asarray(r["ssm_p"])
            conv_p[0, b] = np.asarray(r["conv_p"])
        y_sample[16 * c:16 * c + 16] = y[1024:1088].reshape(16, 4, 4096)
        gla_s[0, 16 * c:16 * c + 16] = np.asarray(r["gla_s"])
        ssm_s[0, 16 * c:16 * c + 16] = np.asarray(r["ssm_s"])
        conv_s[0, 16 * c:16 * c + 16] = np.asarray(r["conv_s"]).reshape(16, 3, 4096)
    return (y_prompt, y_sample, gla_p, ssm_p, conv_p, gla_s, ssm_s, conv_s)


def kernel(**inp):
    in_maps = _make_in_maps(inp)
    if 99 not in _NC_CACHE:
        _NC_CACHE[99] = build_nc(99)
    res = run_bass_kernel_spmd(_NC_CACHE[99], in_maps, core_ids=list(range(8)))
    return _assemble(res.results)
```

```python
import contextlib
import threading
import numpy as np
import concourse.bass as bass
import concourse.mybir as mybir
from concourse.bass_utils import run_bass_kernel_spmd

F32 = mybir.dt.float32
BF16 = mybir.dt.bfloat16
AF = mybir.ActivationFunctionType
ALU = mybir.AluOpType

NT = 1096
NPRE = 1032
NTX = 1147
ALPHA = 2.0 ** 0.25
EPS = 1e-5
NEG = -30000.0

CM = {}
_o = 0
for _n, _w in [("ident", 128), ("tri", 128), ("ntri16", 128), ("nR16", 128), ("negtri", 128), ("ones", 128),
               ("negmask4", 512), ("Rs", 128), ("btri", 64), ("nbtri16", 64), ("nRb16", 64), ("negbtri", 64),
               ("negbmask4", 256), ("Rbs", 64), ("seqmask", 16), ("colmask", 1024)]:
    CM[_n] = (_o, _o + _w)
    _o += _w
NCM = _o
PV = {}
_o = 0
for _n, _w in [("dtb", 32), ("alog", 32), ("bgk", 1024), ("wgk", 1024), ("gnw", 4), ("convw", 128), ("convb", 32),
               ("dsk", 16), ("snw", 16), ("flag", 1)]:
    PV[_n] = (_o, _o + _w)
    _o += _w
NPV = _o


class Ev:
    __slots__ = ("sem", "val", "eng")

    def __init__(self, sem, val, eng):
        self.sem, self.val, self.eng = sem, val, eng


class Buf:
    def __init__(self, name):
        self.name = name
        self.last_write = None
        self.readers = []
        self.dsem = None
        self.dcount = 0


class TT_:
    def __init__(self, t, name):
        self.t = t
        self.b = Buf(name)


class Prog:
    ENGS = ("pe", "act", "dve", "pool", "sp")

    def __init__(self, nc, stack):
        self.nc = nc
        self.stack = stack
        self.ops = {e: [] for e in self.ENGS}
        self.esem = {e: stack.enter_context(nc.semaphore("es_" + e)) for e in self.ENGS}
        self.ecount = {e: 0 for e in self.ENGS}
        self.known = {e: {} for e in self.ENGS}
        self.out_events = []
        self.dma_owners = []

    def _need(self, eng, ev, raw):
        if ev is None:
            return None
        if ev.eng == eng and ev.sem is self.esem[eng]:
            if eng == "pe":
                return None
        if self.known[eng].get(id(ev.sem), 0) >= ev.val:
            return None
        return ev

    def _collect(self, eng, reads, writes):
        waits = {}

        def add(ev, raw):
            ev = self._need(eng, ev, raw)
            if ev is None:
                return
            cur = waits.get(id(ev.sem))
            if cur is None or cur.val < ev.val:
                waits[id(ev.sem)] = ev

        for b in reads:
            add(b.last_write, True)
        for b in writes:
            add(b.last_write, False)
            for r in b.readers:
                add(r, False)
        out = []
        for ev in waits.values():
            self.known[eng][id(ev.sem)] = ev.val
            out.append((ev.sem, ev.val))
        return out

    def _commit(self, ev, reads, writes):
        for b in reads:
            b.readers.append(ev)
            if len(b.readers) > 16:
                best = {}
                for r in b.readers:
                    c = best.get(id(r.sem))
                    if c is None or c.val < r.val:
                        best[id(r.sem)] = r
                b.readers = list(best.values())
        for b in writes:
            b.last_write = ev
            b.readers = []

    def op(self, eng, fn, reads=(), writes=()):
        waits = self._collect(eng, reads, writes)
        self.ecount[eng] += 1
        ev = Ev(self.esem[eng], self.ecount[eng], eng)
        self._commit(ev, reads, writes)
        self.ops[eng].append((waits, fn, (self.esem[eng], 1)))
        INTER.switch()
        return ev

    def dma(self, eng, out_ap, in_ap, owner, reads=(), writes=(), is_out=False):
        if owner.dsem is None:
            owner.dsem = self.stack.enter_context(self.nc.semaphore("ds_" + owner.name))
            self.dma_owners.append(owner)
        waits = self._collect(eng, reads, writes)
        owner.dcount += 16
        ev = Ev(owner.dsem, owner.dcount, "dma")
        self._commit(ev, reads, writes)
        self.ops[eng].append((waits, lambda e: e.dma_start(out=out_ap, in_=in_ap), (owner.dsem, 16)))
        if is_out:
            self.out_events.append(ev)
        INTER.switch()
        return ev

    def finish(self):
        self.ops["sp"].append(([(o.dsem, o.dcount) for o in self.dma_owners], None, None))

    def emit(self):
        nc = self.nc
        with nc.Block() as block:
            def run(e, lst):
                for waits, fn, inc in lst:
                    for sem, val in waits:
                        e.wait_ge(sem, val)
                    if fn is not None:
                        ins = fn(e)
                        if inc is not None:
                            ins.then_inc(inc[0], inc[1])

            @block.sync
            def _(e):
                run(e, self.ops["sp"])

            @block.tensor
            def _(e):
                run(e, self.ops["pe"])

            @block.scalar
            def _(e):
                run(e, self.ops["act"])

            @block.vector
            def _(e):
                run(e, self.ops["dve"])

            @block.gpsimd
            def _(e):
                run(e, self.ops["pool"])


class Inter:
    def __init__(self):
        self.cv = threading.Condition()
        self.active = None
        self.alive = {}
        self.exc = None

    def switch(self):
        me = threading.get_ident()
        if me not in self.alive:
            return
        with self.cv:
            other = [t for t in self.alive if t != me and self.alive[t]]
            if not other:
                return
            self.active = other[0]
            self.cv.notify_all()
            while self.active != me:
                self.cv.wait()

    def run(self, fa, fb):
        if fb is None:
            return fa()
        if fa is None:
            return fb()
        self.alive = {}
        self.active = None
        self.exc = None

        def w(f):
            me = threading.get_ident()
            with self.cv:
                while self.active != me:
                    self.cv.wait()
            try:
                f()
            except BaseException as e:
                self.exc = e
            finally:
                with self.cv:
                    self.alive[me] = False
                    other = [t for t in self.alive if self.alive[t]]
                    self.active = other[0] if other else -1
                    self.cv.notify_all()
        ta = threading.Thread(target=w, args=(fa,))
        tb = threading.Thread(target=w, args=(fb,))
        ta.start()
        tb.start()
        with self.cv:
            self.alive = {ta.ident: True, tb.ident: True}
            self.active = ta.ident
            self.cv.notify_all()
        ta.join()
        tb.join()
        self.alive = {}
        if self.exc is not None:
            raise self.exc


INTER = Inter()


class Ring:
    def __init__(self, items):
        self.items = items
        self.i = 0

    def __call__(self):
        x = self.items[self.i % len(self.items)]
        self.i += 1
        return x


def build_nc(stage=99):
    nc = bass.Bass("TRN2", target_bir_lowering=False)

    def D(name, shape, dt=F32, kind="ExternalInput"):
        return nc.dram_tensor(name, shape, dt, kind=kind).ap()

    xm = D("xm", [NT, 4096])
    xp = D("xp", [NPRE, 4096])
    cmask_d = D("cmask", [128, NCM])
    pvec_d = D("pvec", [128, NPV])
    w_in_t = D("w_in_t", [97, 128, 32, 128])
    w_out_t = D("w_out_t", [32, 128, 32, 128])
    w_up_t = D("w_up_t", [128, 128, 32, 128])
    w_down = D("w_down", [16384, 4096])
    lnp = D("lnp", [4, 4096])
    sgla = D("sgla", [16, 4, 256, 512])
    sssm = D("sssm", [16, 32, 64, 128])
    sconv = D("sconv", [48, 4096])
    y_o = D("y", [NT, 4096], kind="ExternalOutput")
    gla_p_o = D("gla_p", [4, 256, 512], kind="ExternalOutput")
    ssm_p_o = D("ssm_p", [32, 64, 128], kind="ExternalOutput")
    conv_p_o = D("conv_p", [3, 4096], kind="ExternalOutput")
    gla_s_o = D("gla_s", [16, 4, 256, 512], kind="ExternalOutput")
    ssm_s_o = D("ssm_s", [16, 32, 64, 128], kind="ExternalOutput")
    conv_s_o = D("conv_s", [48, 4096], kind="ExternalOutput")
    mix_scr = D("mix_scr", [9, 128, 32, 128], BF16, kind="Internal")
    S_scr = D("S_scr", [4, 128, 2, 512], F32, kind="Internal")
    hT_scr = D("hT_scr", [8, 128, 256], F32, kind="Internal")
    mixb = [Buf(f"mixb{t}") for t in range(9)]
    Sscrb = [Buf(f"Sscrb{h}") for h in range(4)]
    hscrb = [Buf(f"hscrb{g}") for g in range(8)]

    with contextlib.ExitStack() as st:
        P = Prog(nc, st)
        P.fence = {}

        class Scope:
            def __init__(self):
                self.stack = contextlib.ExitStack()
                self.bufs = []

            def __enter__(self):
                self.stack.__enter__()
                return self

            def __exit__(self, *a):
                for b in self.bufs:
                    for ev in ([b.last_write] if b.last_write else []) + b.readers:
                        c = P.fence.get(id(ev.sem))
                        if c is None or c.val < ev.val:
                            P.fence[id(ev.sem)] = ev
                return self.stack.__exit__(*a)

        root = Scope()
        root.stack = st
        uid = [0]

        def sb(name, shape, dt=F32, sc=None):
            sc = sc or root
            uid[0] += 1
            nm = f"{name}_{uid[0]}"
            t = TT_(sc.stack.enter_context(nc.sbuf_tensor(nm, shape, dt)), nm)
            t.b.readers = list(P.fence.values())
            sc.bufs.append(t.b)
            return t

        def ring(name, shape, dt, n, sc=None):
            return Ring([sb(f"{name}{i}", shape, dt, sc) for i in range(n)])

        def ACT(out, in_, func, reads, writes, **kw):
            return P.op("act", lambda e: e.activation(out=out, in_=in_, func=func, **kw), reads, writes)

        def TTo(out, in0, in1, op, reads, writes, eng="dve"):
            return P.op(eng, lambda e: e.tensor_tensor(out=out, in0=in0, in1=in1, op=op), reads, writes)

        def TSM(out, in0, s1, reads, writes, eng="dve"):
            return P.op(eng, lambda e: e.tensor_scalar_mul(out=out, in0=in0, scalar1=s1), reads, writes)

        def STT(out, in0, scalar, in1, op0, op1, reads, writes, eng="dve"):
            return P.op(eng, lambda e: e.scalar_tensor_tensor(out=out, in0=in0, scalar=scalar, in1=in1, op0=op0,
                                                              op1=op1), reads, writes)

        def CP(eng, out, in_, reads, writes):
            if eng == "act":
                return P.op("act", lambda e: e.copy(out=out, in_=in_), reads, writes)
            return P.op(eng, lambda e: e.tensor_copy(out=out, in_=in_), reads, writes)

        def MM(lst, reads, writes):
            def fn(e):
                ins = None
                for (o, l, r, s0, s1) in lst:
                    ins = e.matmul(o, lhsT=l, rhs=r, start=s0, stop=s1)
                return ins
            return P.op("pe", fn, reads, writes)

        def TR(lst, reads, writes):
            def fn(e):
                ins = None
                for (o, i, idn) in lst:
                    ins = e.transpose(o, i, idn)
                return ins
            return P.op("pe", fn, reads, writes)

        def MEMSET(ap, val, writes):
            return P.op("dve", lambda e: e.memset(ap, val), [], writes)

        def RSTD(ss_, tmp, rstd_, L, scale):
            ACT(tmp.t[0:L, :], ss_.t[0:L, :], AF.Ln, [ss_.b, epsc.b], [tmp.b], scale=scale, bias=epsc.t[0:L, :])
            ACT(rstd_.t[0:L, :], tmp.t[0:L, :], AF.Exp, [tmp.b], [rstd_.b], scale=-0.5)

        banks = [TT_(st.enter_context(nc.psum_tensor(f"pb{i}", [128, 512], F32)), f"pb{i}") for i in range(8)]
        nb = Ring(banks[0:5])
        byring = Ring([banks[5], banks[6]])
        acc_bank = banks[7]

        def bf(bank):
            return bank.t[:].bitcast(BF16)

        identF = sb("identF", [128, 128])
        identb = sb("identb", [128, 128], BF16)
        epsc = sb("epsc", [128, 1])
        P.dma("sp", identF.t[:], cmask_d[:, CM["ident"][0]:CM["ident"][1]], identF.b, writes=[identF.b])
        CP("act", identb.t[:], identF.t[:], [identF.b], [identb.b])
        MEMSET(epsc.t[:], EPS, [epsc.b])
        identf = identF.t[:]
        wring = ring("wsl", [128, 32, 128], BF16, 3)

        def wl(dram_ap):
            s = wring()
            P.dma("pool", s.t[:], dram_ap, s.b, writes=[s.b])
            return s

        with Scope() as phAB:
            cm = sb("cm", [128, NCM], F32, phAB)
            pv = sb("pv", [128, NPV], F32, phAB)
            a_bc = sb("a_bc", [128, 32], F32, phAB)
            diagD = sb("diagD", [128, 16, 128], BF16, phAB)
            halo_all = sb("halo_all", [128, 32, 3], F32, phAB)
            P.dma("sp", cm.t[:], cmask_d, cm.b, writes=[cm.b])
            P.dma("sp", pv.t[:], pvec_d, pv.b, writes=[pv.b])

            def C(name, r=slice(0, 128), c=None):
                a, b_ = CM[name]
                if c is None:
                    return cm.t[r, a:b_]
                return cm.t[r, a + c.start:a + c.stop]

            def Pv(name, r=slice(0, 128), c=None):
                a, b_ = PV[name]
                if c is None:
                    return pv.t[r, a:b_]
                return pv.t[r, a + c.start:a + c.stop]

            ACT(a_bc.t[:], Pv("alog"), AF.Exp, [pv.b], [a_bc.b])
            TSM(a_bc.t[:], a_bc.t[:], -1.0, [a_bc.b], [a_bc.b])
            for xt in range(16):
                TSM(diagD.t[:, xt, :], identf, Pv("dsk", c=slice(xt, xt + 1)), [identF.b, pv.b], [diagD.b])
            MEMSET(halo_all.t[:], 0.0, [halo_all.b])
            xT = sb("xT", [128, 32, NT], BF16, phAB)
            xTb = [[Buf(f"xT{t}_{k}") for k in range(2)] for t in range(9)]
            for l_ in xTb:
                phAB.bufs.extend(l_)

            def build_xT(x_d, ntok):
                bufs = []
                with Scope() as sx:
                    xst = ring("xst", [128, 4096], F32, 2, sx)
                    for t in range((ntok + 127) // 128):
                        r0 = t * 128
                        n = min(128, ntok - r0)
                        xs_ = xst()
                        P.dma("sp", xs_.t[0:n, :], x_d[r0:r0 + n, :], xs_.b, writes=[xs_.b])
                        for q in range(8):
                            bk = nb()
                            TR([(bk.t[:, j * 128:j * 128 + n], xs_.t[0:n, (4 * q + j) * 128:(4 * q + j + 1) * 128],
                                 identf[0:n, 0:n]) for j in range(4)], [xs_.b, identF.b], [bk.b])
                            src = bk.t[:].rearrange("p (a t) -> p a t", a=4)[:, :, 0:n]
                            CP("act" if q % 2 == 0 else "dve", xT.t[:, 4 * q:4 * q + 4, r0:r0 + n], src, [bk.b],
                               [xTb[t][q % 2]])
                        bufs += xTb[t]
                return bufs

            def gemm_tile(slot, M, xbufs, chunks, evac, mcol0=0):
                for (c0, c1) in chunks:
                    bk = nb()
                    MM([(bk.t[0:M, 0:c1 - c0], slot.t[:, kt, mcol0:mcol0 + M], xT.t[:, kt, c0:c1], kt == 0, kt == 31)
                        for kt in range(32)], [slot.b] + xbufs, [bk.b])
                    evac(bk, c0, c1)

            for pre in (True, False):
                ntok = NPRE if pre else NT
                xbufs = build_xT(xp if pre else xm, ntok)
                if pre:
                    chunks = [(0, 512), (512, 1024), (1024, 1032)]
                    scan = [(128, 128 * c, c) for c in range(8)] + [(8, 1024, 9)]
                else:
                    chunks = [(0, 512), (512, 1024), (1024, 1096)]
                    scan = [(128, 128 * c, c) for c in range(8)] + [(8, 1088, 9)]
                with Scope() as ph:
                    gklT = sb("gklT", [16, NT], F32, ph)
                    dt_tm = sb("dt_tm", [128, 10, 32], F32, ph)
                    dta_tm = sb("dta_tm", [128, 10, 32], F32, ph)
                    sdt = Scope()
                    sdt.__enter__()
                    dtT = sb("dtT", [32, NT], F32, sdt)
                    slot = wl(w_in_t[0])
                    gemm_tile(slot, 16, xbufs, chunks,
                              lambda bk, c0, c1: CP("act", gklT.t[:, c0:c1], bk.t[0:16, 0:c1 - c0], [bk.b], [gklT.b]))
                    gemm_tile(slot, 32, xbufs, chunks,
                              lambda bk, c0, c1: CP("act", dtT.t[:, c0:c1], bk.t[0:32, 0:c1 - c0], [bk.b], [dtT.b]),
                              mcol0=32)
                    dtl = list(scan) + ([] if pre else [(64, 1024, 8)])
                    dtmp = sb("dtmp", [128, 32], F32, sdt)
                    for (L, col, ci) in dtl:
                        bk = nb()
                        TR([(bk.t[0:L, 0:32], dtT.t[0:32, col:col + L], identf[0:32, 0:32])], [dtT.b, identF.b], [bk.b])
                        TTo(dtmp.t[0:L, :], bk.t[0:L, 0:32], Pv("dtb", slice(0, L)), ALU.add, [bk.b, pv.b], [dtmp.b])
                        ACT(dtmp.t[0:L, :], dtmp.t[0:L, :], AF.Exp, [dtmp.b], [dtmp.b])
                        ACT(dt_tm.t[0:L, ci, :], dtmp.t[0:L, :], AF.Ln, [dtmp.b], [dt_tm.b], bias=1.0)
                        TTo(dta_tm.t[0:L, ci, :], dt_tm.t[0:L, ci, :], a_bc.t[0:L, :], ALU.mult, [dt_tm.b, a_bc.b],
                            [dta_tm.b])
                    sdt.__exit__(None, None, None)

                    if stage >= 1:
                      with Scope() as pg:
                        kT = sb("kT", [128, 2, NT], F32, pg)
                        qT = sb("qT", [128, 2, NT], F32, pg)
                        v_tm = sb("v_tm", [128, 10, 512], BF16, pg)
                        rgT = sb("rgT", [128, 4, NT], BF16, pg)
                        vstage = ring("vstage", [128, 512], BF16, 1, pg)
                        rtmp = ring("rtmp", [128, 512], F32, 1, pg)
                        e1 = ring("e1", [128, 256], F32, 1, pg)
                        spb = ring("spb", [128, 256], F32, 1, pg)
                        E4 = ring("E4", [128, 4, 128], F32, 2, pg)
                        Ekn = ring("Ekn", [128, 2, 128], F32, 1, pg)
                        qp = ring("qp", [128, 2, 128], BF16, 2, pg)
                        kp = ring("kp", [128, 2, 128], BF16, 2, pg)
                        kpp = ring("kpp", [128, 2, 128], BF16, 2, pg)
                        kpptm = ring("kpptm", [128, 256], BF16, 2, pg)
                        AT = ring("AT", [128, 128], BF16, 2, pg)
                        ss = ring("ss", [128, 1], F32, 2, pg)
                        ss2 = ring("ss2", [128, 1], F32, 2, pg)
                        rstd = ring("rstd", [128, 1], F32, 2, pg)
                        on = ring("on", [128, 512], BF16, 2, pg)
                        mixst = ring("mixst", [128, 4, 128], BF16, 2, pg)
                        S = sb("S", [128, 2, 512], F32, pg)
                        Sbf = sb("Sbf", [128, 2, 512], BF16, pg)
                        if not pre:
                            kppm = ring("kppm", [64, 256], BF16, 2, pg)
                            qpm = ring("qpm", [128, 2, 64], BF16, 2, pg)
                            Sin = ring("Sin", [128, 2, 512], F32, 2, pg)
                            Sinb = ring("Sinb", [128, 2, 512], BF16, 2, pg)

                        def v_evac(j):
                            def f(bk, c0, c1):
                                vs = vstage()
                                n = c1 - c0
                                CP("act", vs.t[:, 0:n], bk.t[:, 0:n], [bk.b], [vs.b])
                                tb = nb()
                                tbv = bf(tb)
                                if n == 512:
                                    TR([(tbv[:, a * 128:(a + 1) * 128], vs.t[:, a * 128:(a + 1) * 128], identb.t[:, :])
                                        for a in range(4)], [vs.b, identb.b], [tb.b])
                                    ci0 = c0 // 128
                                    CP("dve", v_tm.t[:, ci0:ci0 + 4, j * 128:(j + 1) * 128],
                                       tbv[:, 0:512].rearrange("p (a c) -> p a c", a=4), [tb.b], [v_tm.b])
                                elif pre:
                                    TR([(tbv[0:8, 0:128], vs.t[:, 0:8], identb.t[:, :])], [vs.b, identb.b], [tb.b])
                                    CP("dve", v_tm.t[0:8, 9, j * 128:(j + 1) * 128], tbv[0:8, 0:128], [tb.b], [v_tm.b])
                                else:
                                    TR([(tbv[0:64, 0:128], vs.t[:, 0:64], identb.t[:, :]),
                                        (tbv[0:8, 128:256], vs.t[:, 64:72], identb.t[:, :])], [vs.b, identb.b], [tb.b])
                                    CP("dve", v_tm.t[0:64, 8, j * 128:(j + 1) * 128], tbv[0:64, 0:128], [tb.b], [v_tm.b])
                                    CP("dve", v_tm.t[0:8, 9, j * 128:(j + 1) * 128], tbv[0:8, 128:256], [tb.b], [v_tm.b])
                            return f

                        def r_evac(j):
                            def f(bk, c0, c1):
                                rt = rtmp()
                                n = c1 - c0
                                ACT(rt.t[:, 0:n], bk.t[:, 0:n], AF.Silu, [bk.b], [rt.b])
                                TSM(rgT.t[:, j, c0:c1], rt.t[:, 0:n], Pv("gnw", c=slice(j, j + 1)), [rt.b, pv.b], [rgT.b])
                            return f

                        def gla_front(h, L, col, blk):
                            tri16 = C("nbtri16" if blk else "ntri16", slice(0, L), slice(0, L))
                            r16 = C("nRb16" if blk else "nR16", slice(0, L), slice(0, L))
                            bk = nb()
                            MM([(bk.t[0:L, 0:256], gklT.t[0:16, col:col + L],
                                 Pv("wgk", slice(0, 16), slice(h * 256, h * 256 + 256)), True, False),
                                (bk.t[0:L, 0:256], C("ones", slice(0, 1), slice(0, L)),
                                 Pv("bgk", slice(0, 1), slice(h * 256, h * 256 + 256)), False, True)],
                               [gklT.b, pv.b, cm.b], [bk.b])
                            e1_, sp_ = e1(), spb()
                            ACT(e1_.t[0:L, :], bk.t[0:L, 0:256], AF.Exp, [bk.b], [e1_.b], scale=-1.0)
                            ACT(sp_.t[0:L, :], e1_.t[0:L, :], AF.Ln, [e1_.b], [sp_.b], bias=1.0)
                            b5 = nb()
                            b5v = b5.t[:].rearrange("p (a t) -> p a t", a=4)
                            MM([(b5v[:, kt, 0:L], sp_.t[0:L, kt * 128:(kt + 1) * 128], tri16, True, True) for kt in range(2)] +
                               [(b5v[:, 2 + kt, 0:L], sp_.t[0:L, kt * 128:(kt + 1) * 128], r16, True, True) for kt in range(2)],
                               [sp_.b, cm.b], [b5.b])
                            E4_ = E4()
                            ACT(E4_.t[:, :, 0:L], b5v[:, :, 0:L], AF.Exp, [b5.b], [E4_.b])
                            qp_ = kp_ = None
                            if not pre:
                                Ekn_ = Ekn()
                                ACT(Ekn_.t[:, :, 0:L], b5v[:, 0:2, 0:L], AF.Exp, [b5.b], [Ekn_.b], scale=-1.0)
                                qp_, kp_ = qp(), kp()
                                STT(qp_.t[:, :, 0:L], qT.t[:, :, col:col + L], 0.0625, E4_.t[:, 0:2, 0:L], ALU.mult, ALU.mult,
                                    [qT.b, E4_.b], [qp_.b])
                                TTo(kp_.t[:, :, 0:L], kT.t[:, :, col:col + L], Ekn_.t[:, :, 0:L], ALU.mult, [kT.b, Ekn_.b], [kp_.b])
                            kpp_ = kpp()
                            TTo(kpp_.t[:, :, 0:L], kT.t[:, :, col:col + L], E4_.t[:, 2:4, 0:L], ALU.mult, [kT.b, E4_.b], [kpp_.b])
                            tb = nb()
                            tbv = bf(tb)
                            TR([(tbv[0:L, kt * 128:(kt + 1) * 128], kpp_.t[:, kt, 0:L], identb.t[:, :]) for kt in range(2)],
                               [kpp_.b, identb.b], [tb.b])
                            kt_ = kpptm()
                            CP("act", kt_.t[0:L, :], tbv[0:L, 0:256], [tb.b], [kt_.b])
                            return E4_, qp_, kp_, kt_

                        def gla_out(h, L, col, ob, tile, off):
                            ss_, ss2_, rstd_, on_, mx = ss(), ss2(), rstd(), on(), mixst()
                            ACT(on_.t[0:L, :], ob.t[0:L, :], AF.Square, [ob.b], [on_.b, ss_.b], accum_out=ss_.t[0:L, 0:1])
                            RSTD(ss_, ss2_, rstd_, L, 1.0 / 512)
                            ACT(on_.t[0:L, :], ob.t[0:L, :], AF.Identity, [ob.b, rstd_.b], [on_.b], scale=rstd_.t[0:L, 0:1])
                            tb = nb()
                            tbv = bf(tb)
                            TR([(tbv[:, j * 128:j * 128 + L], on_.t[0:L, j * 128:(j + 1) * 128], identb.t[0:L, 0:L])
                                for j in range(4)], [on_.b, identb.b], [tb.b])
                            TTo(mx.t[:, :, 0:L], tbv[:, 0:512].rearrange("p (a c) -> p a c", a=4)[:, :, 0:L],
                                rgT.t[:, :, col:col + L], ALU.mult, [tb.b, rgT.b], [mx.b])
                            P.dma("sp", mix_scr[tile, :, 4 * h:4 * h + 4, off:off + L], mx.t[:, :, 0:L], mx.b,
                                  reads=[mx.b, mixb[tile]])

                        for h in range(4):
                            base = 1 + 12 * h
                            for j in range(2):
                                gemm_tile(wl(w_in_t[base + j]), 128, xbufs, chunks,
                                          lambda bk, c0, c1, j=j: CP("act", kT.t[:, j, c0:c1], bk.t[:, 0:c1 - c0], [bk.b], [kT.b]))
                            for j in range(4):
                                gemm_tile(wl(w_in_t[base + 2 + j]), 128, xbufs, chunks, v_evac(j))
                            if pre:
                                MEMSET(S.t[:], 0.0, [S.b])
                            else:
                                for j in range(2):
                                    gemm_tile(wl(w_in_t[base + 6 + j]), 128, xbufs, chunks,
                                              lambda bk, c0, c1, j=j: CP("act", qT.t[:, j, c0:c1], bk.t[:, 0:c1 - c0], [bk.b], [qT.b]))
                                for j in range(4):
                                    gemm_tile(wl(w_in_t[base + 8 + j]), 128, xbufs, chunks, r_evac(j))
                                P.dma("sp", S.t[:], S_scr[h], S.b, reads=[Sscrb[h]], writes=[S.b])
                                CP("act", Sbf.t[:], S.t[:], [S.b], [Sbf.b])
                            HG = {}

                            def gF(k, h=h):
                                L, col, ci = scan[k]
                                E4_, qp_, kp_, kt_ = gla_front(h, L, col, False)
                                AT_ = None
                                if not pre:
                                    sbk = nb()
                                    MM([(sbk.t[0:L, 0:L], kp_.t[:, kt, 0:L], qp_.t[:, kt, 0:L], kt == 0, kt == 1) for kt in range(2)],
                                       [kp_.b, qp_.b], [sbk.b])
                                    AT_ = AT()
                                    TTo(AT_.t[0:L, 0:L], sbk.t[0:L, 0:L], C("tri", slice(0, L), slice(0, L)), ALU.mult,
                                        [sbk.b, cm.b], [AT_.b])
                                HG[k] = (E4_, qp_, kp_, kt_, AT_)

                            def gT(k, h=h):
                                L, col, ci = scan[k]
                                E4_, qp_, kp_, kt_, AT_ = HG.pop(k)
                                if not pre:
                                    ob = byring()
                                    MM([(ob.t[0:L, :], AT_.t[0:L, 0:L], v_tm.t[0:L, ci, :], True, False)] +
                                       [(ob.t[0:L, :], qp_.t[:, kt, 0:L], Sbf.t[:, kt, :], False, kt == 1) for kt in range(2)],
                                       [AT_.b, v_tm.b, qp_.b, Sbf.b], [ob.b])
                                for kt in range(2):
                                    ub = nb()
                                    MM([(ub.t[:, :], kt_.t[0:L, kt * 128:(kt + 1) * 128], v_tm.t[0:L, ci, :], True, True)],
                                       [kt_.b, v_tm.b], [ub.b])
                                    STT(S.t[:, kt, :], S.t[:, kt, :], E4_.t[:, kt, L - 1:L], ub.t[:, :], ALU.mult, ALU.add,
                                        [S.b, E4_.b, ub.b], [S.b])
                                if not pre:
                                    CP("act", Sbf.t[:], S.t[:], [S.b], [Sbf.b])
                                    tile, off = (ci, 0) if ci < 8 else (8, 64)
                                    gla_out(h, L, col, ob, tile, off)

                            gF(0)
                            for k in range(len(scan)):
                                INTER.run(lambda k=k: gT(k), (lambda k=k: gF(k + 1)) if k + 1 < len(scan) else None)
                            if pre:
                                TSM(S.t[:], S.t[:], Pv("flag"), [S.b, pv.b], [S.b])
                                P.dma("sp", S_scr[h], S.t[:], S.b, reads=[S.b], writes=[Sscrb[h]])
                            else:
                                P.dma("sp", gla_p_o[h].rearrange("(kt p) v -> p kt v", p=128), S.t[:], S.b, reads=[S.b], is_out=True)
                                L, col, ci = 64, 1024, 8

                                def ld(i, h=h):
                                    s1, s2 = Sin(), Sinb()
                                    src = sgla[i, h].rearrange("(kt p) v -> p kt v", p=128)
                                    P.dma("sp", s1.t[:], src, s1.b, writes=[s1.b])
                                    P.dma("pool", s2.t[:], src, s2.b, writes=[s2.b])
                                    return s1, s2
                                lds = [ld(0)]
                                E4_, qp_, kp_, kt_ = gla_front(h, L, col, True)
                                sbk = nb()
                                MM([(sbk.t[0:L, 0:L], kp_.t[:, kt, 0:L], qp_.t[:, kt, 0:L], kt == 0, kt == 1) for kt in range(2)],
                                   [kp_.b, qp_.b], [sbk.b])
                                AT_ = AT()
                                TTo(AT_.t[0:L, 0:L], sbk.t[0:L, 0:L], C("btri", slice(0, L)), ALU.mult, [sbk.b, cm.b], [AT_.b])
                                ob = acc_bank
                                MM([(ob.t[0:L, :], AT_.t[0:L, 0:L], v_tm.t[0:L, ci, :], True, False)], [AT_.b, v_tm.b], [ob.b])
                                for i in range(16):
                                    if i + 1 < 16:
                                        lds.append(ld(i + 1))
                                    s1, s2 = lds[i]
                                    qm, km = qpm(), kppm()
                                    TTo(qm.t[:, :, :], qp_.t[:, :, 0:64],
                                        C("colmask", c=slice(64 * i, 64 * i + 64)).unsqueeze(1).broadcast_to([128, 2, 64]),
                                        ALU.mult, [qp_.b, cm.b], [qm.b])
                                    TSM(km.t[0:64, :], kt_.t[0:64, :], C("seqmask", slice(0, 64), slice(i, i + 1)), [kt_.b, cm.b], [km.b])
                                    MM([(ob.t[0:L, :], qm.t[:, kt, :], s2.t[:, kt, :], False, (i == 15 and kt == 1))
                                        for kt in range(2)], [qm.b, s2.b], [ob.b])
                                    so = s1
                                    for kt in range(2):
                                        ub = nb()
                                        MM([(ub.t[:, :], km.t[0:64, kt * 128:(kt + 1) * 128], v_tm.t[0:64, ci, :], True, True)],
                                           [km.b, v_tm.b], [ub.b])
                                        STT(so.t[:, kt, :], s1.t[:, kt, :], E4_.t[:, kt, 4 * i + 3:4 * i + 4], ub.t[:, :], ALU.mult,
                                            ALU.add, [E4_.b, ub.b], [so.b])
                                    P.dma("sp", gla_s_o[i, h].rearrange("(kt p) v -> p kt v", p=128), so.t[:], so.b,
                                          reads=[so.b], is_out=True)
                                gla_out(h, L, col, ob, 8, 0)

                    DBG = 9.0
                    if stage >= 2 and (pre or DBG >= 2.1):
                      with Scope() as pq:
                        xbcT = sb("xbcT", [128, 4, NTX], F32, pq)
                        acc = sb("acc", [128, NTX], F32, pq)
                        xcp = sb("xcp", [128, 4, NPRE], BF16, pq)
                        xcs = sb("xcs", [128, 4, 64], BF16, pq)
                        zs_tm = sb("zs_tm", [128, 10, 256], BF16, pq)
                        zstage = ring("zstage", [128, 512], BF16, 2, pq)
                        R1 = ring("R1", [128, 4, 128], F32, 1, pq)
                        R2 = ring("R2", [128, 4, 128], F32, 1, pq)
                        Lm = ring("Lm", [128, 4, 128], F32, 2, pq)
                        MT = ring("MT", [128, 4, 128], BF16, 2, pq)
                        xpr = ring("xpr", [128, 256], BF16, 2, pq)
                        xpp = ring("xpp", [128, 256], BF16, 2, pq)
                        Btm = ring("Btm", [128, 128], BF16, 2, pq)
                        wend = ring("wend", [128, 4], F32, 2, pq)
                        etm = ring("etm", [128, 4], F32, 2, pq)
                        El = ring("El", [128, 4], F32, 2, pq)
                        t0 = ring("t0", [128, 256], F32, 2, pq)
                        t1 = ring("t1", [128, 256], F32, 2, pq)
                        ssd_ss = ring("sss", [128, 1], F32, 2, pq)
                        ssd_s2 = ring("sss2", [128, 1], F32, 2, pq)
                        ssd_rs = ring("ssrs", [128, 1], F32, 2, pq)
                        junk2 = sb("junk2", [128, 256], BF16, pq)
                        yn = ring("yn", [128, 256], BF16, 2, pq)
                        mixs2 = ring("mixs2", [128, 2, 128], BF16, 2, pq)
                        hT = sb("hTw", [128, 256], F32, pq)
                        hTb = sb("hTb", [128, 256], BF16, pq)
                        htmp = sb("htmp", [128, 256], F32, pq)
                        cstg = sb("cstg", [128, 51], F32, pq)
                        cout = ring("cout", [51, 128], F32, 2, pq)
                        if not pre:
                            convT_all = sb("convT_all", [128, 32, 48], F32, pq)
                            with Scope() as s0:
                                cst = ring("cst", [48, 512], F32, 2, s0)
                                for q in range(8):
                                    cs_ = cst()
                                    P.dma("sp", cs_.t[:], sconv[:, q * 512:(q + 1) * 512], cs_.b, writes=[cs_.b])
                                    bk = nb()
                                    TR([(bk.t[:, j * 48:(j + 1) * 48], cs_.t[0:48, j * 128:(j + 1) * 128], identf[0:48, 0:48])
                                        for j in range(4)], [cs_.b, identF.b], [bk.b])
                                    CP("act", convT_all.t[:, 4 * q:4 * q + 4, :], bk.t[:, 0:192].rearrange("p (a c) -> p a c", a=4),
                                       [bk.b], [convT_all.b])
                            hnat = ring("hnat", [128, 2, 128], F32, 2, pq)
                            hTbi = ring("hTbi", [128, 256], BF16, 2, pq)
                            hout = ring("hout", [128, 2, 128], F32, 2, pq)
                            Cm_ = ring("Cm", [128, 64], BF16, 2, pq)
                            xpm = ring("xpm", [64, 256], BF16, 2, pq)
                            dta_e = sb("dta_e", [64, 2, 128], F32, pq)
                            dec = sb("dec", [128, 2, 16], F32, pq)

                        def xbc_evac(q):
                            def f(bk, c0, c1):
                                if pre or c1 <= 1024:
                                    CP("act", xbcT.t[:, q, 3 + c0:3 + c1], bk.t[:, 0:c1 - c0], [bk.b], [xbcT.b])
                                else:
                                    CP("act", xbcT.t[:, q, 1035:1147].rearrange("p (s w) -> p s w", w=7)[:, :, 3:7],
                                       bk.t[:, 0:64].rearrange("p (s w) -> p s w", w=4), [bk.b], [xbcT.b])
                                    CP("act", xbcT.t[:, q, 1027:1035], bk.t[:, 64:72], [bk.b], [xbcT.b])
                            return f

                        def z_evac(j):
                            def f(bk, c0, c1):
                                vs = zstage()
                                n = c1 - c0
                                ACT(vs.t[:, 0:n], bk.t[:, 0:n], AF.Silu, [bk.b], [vs.b])
                                tb = nb()
                                tbv = bf(tb)
                                if n == 512:
                                    TR([(tbv[:, a * 128:(a + 1) * 128], vs.t[:, a * 128:(a + 1) * 128], identb.t[:, :])
                                        for a in range(4)], [vs.b, identb.b], [tb.b])
                                    ci0 = c0 // 128
                                    CP("dve", zs_tm.t[:, ci0:ci0 + 4, j * 128:(j + 1) * 128],
                                       tbv[:, 0:512].rearrange("p (a c) -> p a c", a=4), [tb.b], [zs_tm.b])
                                else:
                                    TR([(tbv[0:64, 0:128], vs.t[:, 0:64], identb.t[:, :]),
                                        (tbv[0:8, 128:256], vs.t[:, 64:72], identb.t[:, :])], [vs.b, identb.b], [tb.b])
                                    CP("dve", zs_tm.t[0:64, 8, j * 128:(j + 1) * 128], tbv[0:64, 0:128], [tb.b], [zs_tm.b])
                                    CP("dve", zs_tm.t[0:8, 9, j * 128:(j + 1) * 128], tbv[0:8, 128:256], [tb.b], [zs_tm.b])
                            return f

                        def conv_tile(q, gt):
                            npr = NPRE
                            ncol = (3 + npr) if pre else NTX
                            if pre:
                                MEMSET(xbcT.t[:, q, 0:3], 0.0, [xbcT.b])
                            else:
                                CP("act", xbcT.t[:, q, 0:3], halo_all.t[:, gt, :], [halo_all.b], [xbcT.b])
                                CP("act", xbcT.t[:, q, 1035:1147].rearrange("p (s w) -> p s w", w=7)[:, :, 0:3],
                                   convT_all.t[:, gt, :].rearrange("p (s w) -> p s w", w=3), [convT_all.b], [xbcT.b])
                            m = ncol - 3
                            ACT(acc.t[:, 0:m], xbcT.t[:, q, 3:ncol], AF.Identity, [xbcT.b, pv.b], [acc.b],
                                scale=Pv("convw", c=slice(4 * gt + 3, 4 * gt + 4)), bias=Pv("convb", c=slice(gt, gt + 1)))
                            for jj in range(3):
                                STT(acc.t[:, 0:m], xbcT.t[:, q, jj:jj + m], Pv("convw", c=slice(4 * gt + jj, 4 * gt + jj + 1)),
                                    acc.t[:, 0:m], ALU.mult, ALU.add, [xbcT.b, pv.b, acc.b], [acc.b])
                            ACT(xcp.t[:, q, 0:npr], acc.t[:, 0:npr], AF.Silu, [acc.b], [xcp.b])
                            if not pre:
                                ACT(xcs.t[:, q, :].rearrange("p (s w) -> p s w", w=4),
                                    acc.t[:, 1032:1144].rearrange("p (s w) -> p s w", w=7)[:, :, 3:7], AF.Silu, [acc.b], [xcs.b])
                                CP("act", cstg.t[:, 0:48].rearrange("p (s w) -> p s w", w=3),
                                   xbcT.t[:, q, 1035:1147].rearrange("p (s w) -> p s w", w=7)[:, :, 4:7], [xbcT.b], [cstg.b])
                                CP("act", cstg.t[:, 48:51], xbcT.t[:, q, 3 + 1029:3 + 1032], [xbcT.b], [cstg.b])
                                bk = nb()
                                TR([(bk.t[0:51, 0:128], cstg.t[:, 0:51], identf)], [cstg.b, identF.b], [bk.b])
                                co = cout()
                                CP("dve", co.t[:, :], bk.t[0:51, 0:128], [bk.b], [co.b])
                                P.dma("sp", conv_s_o[:, gt * 128:(gt + 1) * 128], co.t[0:48, :], co.b, reads=[co.b], is_out=True)
                                P.dma("sp", conv_p_o[:, gt * 128:(gt + 1) * 128], co.t[48:51, :], co.b, reads=[co.b], is_out=True)
                            else:
                                TSM(halo_all.t[:, gt, :], xbcT.t[:, q, 3 + 1029:3 + 1032], Pv("flag"), [xbcT.b, pv.b], [halo_all.b])

                        def ssd_front(g, L, xsrc, c0, ci, blk):
                            dta_c = dta_tm.t[0:L, ci, 4 * g:4 * g + 4]
                            dt_c = dt_tm.t[0:L, ci, 4 * g:4 * g + 4]
                            Lm_ = None
                            if not pre:
                                R1_, R2_ = R1(), R2()
                                TTo(R1_.t[0:L, :, 0:L], dta_c.unsqueeze(2).broadcast_to([L, 4, L]),
                                    C("btri" if blk else "tri", slice(0, L), slice(0, L)).unsqueeze(1).broadcast_to([L, 4, L]),
                                    ALU.mult, [dta_tm.b, cm.b], [R1_.b])
                                CP("dve", R2_.t[0:L, :, 0:L], dta_c.unsqueeze(2).broadcast_to([L, 4, L]), [dta_tm.b], [R2_.b])
                                bD = nb()
                                bDv = bD.t[0:L, 0:4 * L].rearrange("p (a t) -> p a t", a=4)
                                nm = C("negbmask4" if blk else "negmask4", slice(0, L)).rearrange("p (a t) -> p a t", a=4)[:, :, 0:L]
                                MM([(bDv, C("ones", slice(0, L), slice(0, L)), R1_.t[0:L, :, 0:L], True, False),
                                    (bDv, C("negbtri" if blk else "negtri", slice(0, L), slice(0, L)), R2_.t[0:L, :, 0:L], False, False),
                                    (bDv, identf[0:L, 0:L], nm, False, True)], [R1_.b, R2_.b, cm.b, identF.b], [bD.b])
                                Lm_ = Lm()
                                ACT(Lm_.t[0:L, :, 0:L], bDv, AF.Exp, [bD.b], [Lm_.b])
                            bw = nb()
                            MM([(bw.t[0:L, 0:32], C("Rbs" if blk else "Rs", slice(0, L), slice(0, L)), dta_tm.t[0:L, ci, :], True, True)],
                               [dta_tm.b, cm.b], [bw.b])
                            we = wend()
                            ACT(we.t[0:L, :], bw.t[0:L, 4 * g:4 * g + 4], AF.Exp, [bw.b], [we.b])
                            tb = nb()
                            tbv = bf(tb)
                            TR([(tbv[0:L, a * 128:(a + 1) * 128], xsrc.t[:, a, c0:c0 + L], identb.t[:, :]) for a in range(3)],
                               [xsrc.b, identb.b], [tb.b])
                            xpr_, xpp_, Btm_ = xpr(), xpp(), Btm()
                            TTo(xpr_.t[0:L, :].rearrange("p (j c) -> p j c", j=4), tbv[0:L, 0:256].rearrange("p (j c) -> p j c", j=4),
                                dt_c.unsqueeze(2).broadcast_to([L, 4, 64]), ALU.mult, [tb.b, dt_tm.b], [xpr_.b])
                            CP("dve", Btm_.t[0:L, :], tbv[0:L, 256:384], [tb.b], [Btm_.b])
                            TTo(xpp_.t[0:L, :].rearrange("p (j c) -> p j c", j=4), xpr_.t[0:L, :].rearrange("p (j c) -> p j c", j=4),
                                we.t[0:L, :].unsqueeze(2).broadcast_to([L, 4, 64]), ALU.mult, [xpr_.b, we.b], [xpp_.b])
                            return Lm_, xpr_, xpp_, Btm_, dta_c

                        def ssd_y(g, L, xsrc, c0, Lm_, xpr_, blk):
                            bcb = nb()
                            MM([(bcb.t[0:L, 0:L], xsrc.t[:, 2, c0:c0 + L], xsrc.t[:, 3, c0:c0 + L], True, True)], [xsrc.b], [bcb.b])
                            MT_ = MT()
                            TTo(MT_.t[0:L, :, 0:L], bcb.t[0:L, 0:L].unsqueeze(1).broadcast_to([L, 4, L]), Lm_.t[0:L, :, 0:L], ALU.mult,
                                [bcb.b, Lm_.b], [MT_.b])
                            by = acc_bank if blk else byring()
                            lst = []
                            for j in range(4):
                                xt = 2 * g + j // 2
                                lst.append((by.t[0:L, 64 * j:64 * j + 64], MT_.t[0:L, j, 0:L], xpr_.t[0:L, 64 * j:64 * j + 64], True, False))
                                lst.append((by.t[0:L, 64 * j:64 * j + 64], xsrc.t[:, j // 2, c0:c0 + L],
                                            diagD.t[:, xt, 64 * (j % 2):64 * (j % 2) + 64], False, True))
                            MM(lst, [MT_.b, xpr_.b, xsrc.b, diagD.b], [by.b])
                            return by

                        def ssd_et(g, L, ci, blk):
                            bc = nb()
                            MM([(bc.t[0:L, 0:32], C("btri" if blk else "tri", slice(0, L), slice(0, L)), dta_tm.t[0:L, ci, :], True, True)],
                               [dta_tm.b, cm.b], [bc.b])
                            et = etm()
                            ACT(et.t[0:L, :], bc.t[0:L, 4 * g:4 * g + 4], AF.Exp, [bc.b], [et.b])
                            return et

                        def ssd_out(g, L, ci, by, dta_c, blk, tile, off, et=None):
                            if et is None:
                                et = ssd_et(g, L, ci, blk)
                            t0_, t1_ = t0(), t1()
                            TTo(t0_.t[0:L, :].rearrange("p (j c) -> p j c", j=4), by.t[0:L, 256:512].rearrange("p (j c) -> p j c", j=4),
                                et.t[0:L, :].unsqueeze(2).broadcast_to([L, 4, 64]), ALU.mult, [by.b, et.b], [t0_.b])
                            TTo(t1_.t[0:L, :], by.t[0:L, 0:256], t0_.t[0:L, :], ALU.add, [by.b, t0_.b], [t1_.b])
                            TTo(t1_.t[0:L, :], t1_.t[0:L, :], zs_tm.t[0:L, ci, :], ALU.mult, [t1_.b, zs_tm.b], [t1_.b])
                            s_, s2_, rs_ = ssd_ss(), ssd_s2(), ssd_rs()
                            ACT(junk2.t[0:L, :], t1_.t[0:L, :], AF.Square, [t1_.b], [junk2.b, s_.b], accum_out=s_.t[0:L, 0:1])
                            RSTD(s_, s2_, rs_, L, 1.0 / 256)
                            yn_ = yn()
                            ACT(yn_.t[0:L, :], t1_.t[0:L, :], AF.Identity, [t1_.b, rs_.b], [yn_.b], scale=rs_.t[0:L, 0:1])
                            tb = nb()
                            tbv = bf(tb)
                            TR([(tbv[:, a * 128:a * 128 + L], yn_.t[0:L, a * 128:(a + 1) * 128], identb.t[0:L, 0:L]) for a in range(2)],
                               [yn_.b, identb.b], [tb.b])
                            mx = mixs2()
                            TTo(mx.t[:, :, 0:L], tbv[:, 0:256].rearrange("p (a c) -> p a c", a=2)[:, :, 0:L],
                                Pv("snw", c=slice(2 * g, 2 * g + 2)).unsqueeze(2).broadcast_to([128, 2, L]), ALU.mult,
                                [tb.b, pv.b], [mx.b])
                            P.dma("sp", mix_scr[tile, :, 16 + 2 * g:16 + 2 * g + 2, off:off + L], mx.t[:, :, 0:L], mx.b,
                                  reads=[mx.b, mixb[tile]])

                        for g in range(8):
                            base = 49 + 6 * g
                            gts = [2 * g, 2 * g + 1, 16 + g, 24 + g]
                            for q in range(3):
                                gemm_tile(wl(w_in_t[base + q]), 128, xbufs, chunks, xbc_evac(q))
                                if DBG >= 1.7:
                                    conv_tile(q, gts[q])
                            if pre and DBG < 1.8:
                                MEMSET(hT.t[:], 0.0, [hT.b])
                            elif pre:
                                gemm_tile(wl(w_in_t[base + 3]), 128, xbufs, [(1024, 1032)],
                                          lambda bk, c0, c1: TSM(halo_all.t[:, 24 + g, :], bk.t[:, 5:8], Pv("flag"), [bk.b, pv.b], [halo_all.b]))
                                MEMSET(hT.t[:], 0.0, [hT.b])
                            else:
                                gemm_tile(wl(w_in_t[base + 3]), 128, xbufs, chunks, xbc_evac(3))
                                conv_tile(3, gts[3])
                                for j in range(2):
                                    gemm_tile(wl(w_in_t[base + 4 + j]), 128, xbufs, chunks, z_evac(j))
                                P.dma("sp", hT.t[:], hT_scr[g], hT.b, reads=[hscrb[g]], writes=[hT.b])
                                CP("act", hTb.t[:], hT.t[:], [hT.b], [hTb.b])
                            HS = {}

                            def sF(k, g=g):
                                L, col, ci = scan[k]
                                pc = 128 * ci if ci < 8 else 1024
                                Lm_, xpr_, xpp_, Btm_, dta_c = ssd_front(g, L, xcp, pc, ci, False)
                                by = et = None
                                if not pre:
                                    by = ssd_y(g, L, xcp, pc, Lm_, xpr_, False)
                                    et = ssd_et(g, L, ci, False)
                                bcl = nb()
                                MM([(bcl.t[:, 0:32], C("ones", slice(0, L), slice(0, 128)), dta_tm.t[0:L, ci, :], True, True)], [dta_tm.b, cm.b], [bcl.b])
                                El_ = El()
                                ACT(El_.t[:, :], bcl.t[:, 4 * g:4 * g + 4], AF.Exp, [bcl.b], [El_.b])
                                HS[k] = (xpp_, Btm_, dta_c, by, et, El_)

                            def sT(k, g=g):
                                L, col, ci = scan[k]
                                pc = 128 * ci if ci < 8 else 1024
                                xpp_, Btm_, dta_c, by, et, El_ = HS.pop(k)
                                if not pre:
                                    MM([(by.t[0:L, 256:512], xcp.t[:, 3, pc:pc + L], hTb.t[:, :], True, True)], [xcp.b, hTb.b], [by.b])
                                bh = nb()
                                MM([(bh.t[:, 0:256], Btm_.t[0:L, :], xpp_.t[0:L, :], True, True)], [Btm_.b, xpp_.b], [bh.b])
                                TTo(htmp.t[:, :].rearrange("p (j c) -> p j c", j=4), hT.t[:, :].rearrange("p (j c) -> p j c", j=4),
                                    El_.t[:, :].unsqueeze(2).broadcast_to([128, 4, 64]), ALU.mult, [hT.b, El_.b], [htmp.b])
                                TTo(hT.t[:, :], htmp.t[:, :], bh.t[:, 0:256], ALU.add, [htmp.b, bh.b], [hT.b])
                                if not pre:
                                    CP("act", hTb.t[:], hT.t[:], [hT.b], [hTb.b])
                                    tile, off = (ci, 0) if ci < 8 else (8, 64)
                                    ssd_out(g, L, ci, by, dta_c, False, tile, off, et)

                            sF(0)
                            for k in range(len(scan)):
                                INTER.run(lambda k=k: sT(k), (lambda k=k: sF(k + 1)) if k + 1 < len(scan) else None)
                            if pre:
                                TSM(hT.t[:], hT.t[:], Pv("flag"), [hT.b, pv.b], [hT.b])
                                P.dma("sp", hT_scr[g], hT.t[:], hT.b, reads=[hT.b], writes=[hscrb[g]])
                            else:
                                bk = nb()
                                TR([(bk.t[:, a * 128:(a + 1) * 128], hT.t[:, a * 128:(a + 1) * 128], identf) for a in range(2)],
                                   [hT.b, identF.b], [bk.b])
                                ho = hout()
                                CP("dve", ho.t[:, :, :], bk.t[:, 0:256].rearrange("p (a c) -> p a c", a=2), [bk.b], [ho.b])
                                P.dma("sp", ssm_p_o[4 * g:4 * g + 4].rearrange("(t jj) p n -> (jj p) t n", jj=2), ho.t[:], ho.b,
                                      reads=[ho.b], is_out=True)
                                L, ci = 64, 8

                                def ldh(i, g=g):
                                    hn = hnat()
                                    P.dma("sp", hn.t[:], sssm[i, 4 * g:4 * g + 4].rearrange("(t jj) p n -> (jj p) t n", jj=2), hn.b,
                                          writes=[hn.b])
                                    return hn
                                lds = [ldh(0)]
                                Lm_, xpr_, xpp_, Btm_, dta_c = ssd_front(g, L, xcs, 0, ci, True)
                                by = ssd_y(g, L, xcs, 0, Lm_, xpr_, True)
                                CP("dve", dta_e.t[:, :, :].rearrange("p t (jj c) -> p t jj c", jj=2),
                                   dta_c.rearrange("p (t jj) -> p t jj", t=2).unsqueeze(3).broadcast_to([64, 2, 2, 64]),
                                   [dta_tm.b], [dta_e.b])
                                bdec = nb()
                                MM([(bdec.t[:, a * 16:(a + 1) * 16], dta_e.t[0:64, a, :], C("seqmask", slice(0, 64)), True, True)
                                    for a in range(2)], [dta_e.b, cm.b], [bdec.b])
                                ACT(dec.t[:, :, :], bdec.t[:, 0:32].rearrange("p (a s) -> p a s", a=2), AF.Exp, [bdec.b], [dec.b])
                                for i in range(16):
                                    if i + 1 < 16:
                                        lds.append(ldh(i + 1))
                                    hn = lds[i]
                                    bt = nb()
                                    TR([(bt.t[:, a * 128:(a + 1) * 128], hn.t[:, a, :], identf) for a in range(2)], [hn.b, identF.b], [bt.b])
                                    hb = hTbi()
                                    CP("act", hb.t[:, :], bt.t[:, 0:256], [bt.b], [hb.b])
                                    cmk, xm_ = Cm_(), xpm()
                                    TTo(cmk.t[:, :], xcs.t[:, 3, :], C("colmask", c=slice(64 * i, 64 * i + 64)), ALU.mult, [xcs.b, cm.b], [cmk.b])
                                    MM([(by.t[0:L, 256:512], cmk.t[:, :], hb.t[:, :], i == 0, i == 15)], [cmk.b, hb.b], [by.b])
                                    TSM(xm_.t[0:64, :], xpp_.t[0:64, :], C("seqmask", slice(0, 64), slice(i, i + 1)), [xpp_.b, cm.b], [xm_.b])
                                    bh = nb()
                                    MM([(bh.t[:, a * 128:(a + 1) * 128], xm_.t[0:64, a * 128:(a + 1) * 128], Btm_.t[0:64, :], True, True)
                                        for a in range(2)], [xm_.b, Btm_.b], [bh.b])
                                    ho = hout()
                                    for a in range(2):
                                        STT(ho.t[:, a, :], hn.t[:, a, :], dec.t[:, a, i:i + 1], bh.t[:, a * 128:(a + 1) * 128], ALU.mult, ALU.add,
                                            [hn.b, dec.b, bh.b], [ho.b])
                                    P.dma("sp", ssm_s_o[i, 4 * g:4 * g + 4].rearrange("(t jj) p n -> (jj p) t n", jj=2), ho.t[:], ho.b,
                                          reads=[ho.b], is_out=True)
                                ssd_out(g, L, ci, by, dta_c, True, 8, 0)

        if stage >= 3:
            groups = [([0, 1, 2, 3], 512), ([4, 5, 6, 7, 8], 584)]
            with Scope() as pc:
                facc = sb("facc", [128, 5, 4096], F32, pc)
                faccb = [Buf(f"facc{t}") for t in range(5)]
                pc.bufs.extend(faccb)
                junkL = sb("junkL", [128, 4096], BF16, pc)
                st_ = {k: ring(k, [128, 1], F32, 2, pc) for k in ["s1", "s2", "mean", "msq", "var", "sd", "rs", "nmr"]}
                for gi, (tiles, ntg) in enumerate(groups):
                    tn = [(128 if t < 8 else 72) for t in tiles]
                    tchunks = [(0, 512)] if ntg == 512 else [(0, 292), (292, 584)]
                    for ti, t in enumerate(tiles):
                        P.dma("sp", facc.t[0:tn[ti], ti, :], xm[t * 128:t * 128 + tn[ti], :], faccb[ti], writes=[faccb[ti]])
                    with Scope() as pcc:
                        mixT = sb("mixT", [128, 32, 584], BF16, pcc)
                        ostage = sb("ostage", [128, 4, 584], F32, pcc)
                        for ti, t in enumerate(tiles):
                            P.dma("sp", mixT.t[:, :, ti * 128:ti * 128 + tn[ti]], mix_scr[t, :, :, 0:tn[ti]], mixT.b,
                                  writes=[mixT.b, mixb[t]])
                        for nq in range(8):
                            for a in range(4):
                                slot = wl(w_out_t[4 * nq + a])
                                for (c0, c1) in tchunks:
                                    bk = nb()
                                    MM([(bk.t[:, 0:c1 - c0], slot.t[:, ft, :], mixT.t[:, ft, c0:c1], ft == 0, ft == 31)
                                        for ft in range(32)], [slot.b, mixT.b], [bk.b])
                                    CP("act", ostage.t[:, a, c0:c1], bk.t[:, 0:c1 - c0], [bk.b], [ostage.b])
                            for ti in range(len(tiles)):
                                n = tn[ti]
                                bk = nb()
                                TR([(bk.t[0:n, a * 128:(a + 1) * 128], ostage.t[:, a, ti * 128:ti * 128 + n], identf) for a in range(4)],
                                   [ostage.b, identF.b], [bk.b])
                                STT(facc.t[0:n, ti, nq * 512:(nq + 1) * 512], facc.t[0:n, ti, nq * 512:(nq + 1) * 512], ALPHA,
                                    bk.t[0:n, :], ALU.mult, ALU.add, [faccb[ti], bk.b], [faccb[ti]])

                    def layer_norm(ti, n, lnc, scale):
                        fa = facc.t[0:n, ti, :]
                        s1, s2, mean, msq, var, sd, rs, nmr = [st_[k]() for k in ["s1", "s2", "mean", "msq", "var", "sd", "rs", "nmr"]]
                        ACT(junkL.t[0:n, :], fa, AF.Identity, [faccb[ti]], [junkL.b, s1.b], accum_out=s1.t[0:n, :])
                        ACT(junkL.t[0:n, :], fa, AF.Square, [faccb[ti]], [junkL.b, s2.b], accum_out=s2.t[0:n, :])
                        TSM(mean.t[0:n, :], s1.t[0:n, :], 1.0 / 4096, [s1.b], [mean.b])
                        TTo(msq.t[0:n, :], mean.t[0:n, :], mean.t[0:n, :], ALU.mult, [mean.b], [msq.b])
                        STT(var.t[0:n, :], s2.t[0:n, :], 1.0 / 4096, msq.t[0:n, :], ALU.mult, ALU.subtract, [s2.b, msq.b], [var.b])
                        ACT(sd.t[0:n, :], var.t[0:n, :], AF.Ln, [var.b, epsc.b], [sd.b], bias=epsc.t[0:n, :])
                        ACT(rs.t[0:n, :], sd.t[0:n, :], AF.Exp, [sd.b], [rs.b], scale=-0.5)
                        if scale != 1.0:
                            TSM(rs.t[0:n, :], rs.t[0:n, :], scale, [rs.b], [rs.b])
                        STT(nmr.t[0:n, :], mean.t[0:n, :], -1.0, rs.t[0:n, :], ALU.mult, ALU.mult, [mean.b, rs.b], [nmr.b])
                        ACT(fa, fa, AF.Identity, [faccb[ti], rs.b, nmr.b], [faccb[ti]], scale=rs.t[0:n, :], bias=nmr.t[0:n, :])
                        TTo(fa, fa, lnc.t[0:n, 0, :], ALU.mult, [faccb[ti], lnc.b], [faccb[ti]])
                        TTo(fa, fa, lnc.t[0:n, 1, :], ALU.add, [faccb[ti], lnc.b], [faccb[ti]])

                    with Scope() as pd:
                        hTg = sb("hTg", [128, 32, 584], BF16, pd)
                        with Scope() as pl:
                            lnc = sb("lnc", [128, 2, 4096], F32, pl)
                            P.dma("sp", lnc.t[:, 0, :], lnp[0].partition_broadcast(128), lnc.b, writes=[lnc.b])
                            P.dma("sp", lnc.t[:, 1, :], lnp[1].partition_broadcast(128), lnc.b, writes=[lnc.b])
                            TSM(lnc.t[:, 1, :], lnc.t[:, 1, :], ALPHA, [lnc.b], [lnc.b])
                            for ti in range(len(tiles)):
                                n = tn[ti]
                                layer_norm(ti, n, lnc, ALPHA)
                                for q in range(8):
                                    bk = nb()
                                    TR([(bk.t[:, a * 128:a * 128 + n], facc.t[0:n, ti, (4 * q + a) * 128:(4 * q + a + 1) * 128],
                                         identf[0:n, 0:n]) for a in range(4)], [faccb[ti], identF.b], [bk.b])
                                    src = bk.t[:].rearrange("p (a t) -> p a t", a=4)[:, :, 0:n]
                                    dst = hTg.t[:, 4 * q:4 * q + 4, ti * 128:ti * 128 + n]
                                    if q % 2 == 0:
                                        ACT(dst, src, AF.Identity, [bk.b], [hTg.b], scale=1.0 / ALPHA)
                                    else:
                                        TSM(dst, src, 1.0 / ALPHA, [bk.b], [hTg.b])
                        with Scope() as pm:
                            h1T = ring("h1T", [128, 8, 584], BF16, 2, pm)
                            wdn = ring("wdn", [128, 8, 512], BF16, 2, pm)
                            rl = ring("rl", [128, 512], F32, 2, pm)
                            for sc in range(16):
                                h1 = h1T()
                                for hk in range(8):
                                    slot = wl(w_up_t[sc * 8 + hk])
                                    for (c0, c1) in tchunks:
                                        bk = nb()
                                        MM([(bk.t[:, 0:c1 - c0], slot.t[:, kt, :], hTg.t[:, kt, c0:c1], kt == 0, kt == 31)
                                            for kt in range(32)], [slot.b, hTg.b], [bk.b])
                                        r_ = rl()
                                        ACT(r_.t[:, 0:c1 - c0], bk.t[:, 0:c1 - c0], AF.Relu, [bk.b], [r_.b])
                                        TTo(h1.t[:, hk, c0:c1], r_.t[:, 0:c1 - c0], r_.t[:, 0:c1 - c0], ALU.mult, [r_.b], [h1.b])
                                for ng in range(8):
                                    wd = wdn()
                                    P.dma("pool", wd.t[:], w_down[sc * 1024:(sc + 1) * 1024, ng * 512:(ng + 1) * 512]
                                          .rearrange("(hk p) c -> p hk c", p=128), wd.b, writes=[wd.b])
                                    for ti in range(len(tiles)):
                                        n = tn[ti]
                                        bk = nb()
                                        MM([(bk.t[0:n, :], h1.t[:, hk, ti * 128:ti * 128 + n], wd.t[:, hk, :], hk == 0, hk == 7)
                                            for hk in range(8)], [h1.b, wd.b], [bk.b])
                                        TTo(facc.t[0:n, ti, ng * 512:(ng + 1) * 512], facc.t[0:n, ti, ng * 512:(ng + 1) * 512],
                                            bk.t[0:n, :], ALU.add, [faccb[ti], bk.b], [faccb[ti]])
                    with Scope() as pl2:
                        lnc2 = sb("lnc2", [128, 2, 4096], F32, pl2)
                        P.dma("sp", lnc2.t[:, 0, :], lnp[2].partition_broadcast(128), lnc2.b, writes=[lnc2.b])
                        P.dma("sp", lnc2.t[:, 1, :], lnp[3].partition_broadcast(128), lnc2.b, writes=[lnc2.b])
                        for ti, t in enumerate(tiles):
                            n = tn[ti]
                            layer_norm(ti, n, lnc2, 1.0)
                            P.dma("sp", y_o[t * 128:t * 128 + n, :], facc.t[0:n, ti, :], faccb[ti], reads=[faccb[ti]], is_out=True)

        P.finish()
        P.emit()
    return nc


def _cmask():
    m = np.zeros((128, NCM), np.float32)
    s = np.arange(128)[:, None]
    t = np.arange(128)[None, :]
    tri = (s <= t).astype(np.float32)
    R = (s > t).astype(np.float32)

    def put(n, a):
        a0, a1 = CM[n]
        m[:a.shape[0], a0:a0 + a.shape[1]] = a
    put("ident", np.eye(128, dtype=np.float32))
    put("tri", tri)
    put("ntri16", -tri / 16.0)
    put("nR16", -R / 16.0)
    put("negtri", -tri)
    put("ones", np.ones((128, 128), np.float32))
    put("negmask4", np.tile(NEG * R, (1, 4)))
    put("Rs", R)
    s6 = np.arange(64)[:, None]
    t6 = np.arange(64)[None, :]
    same = (s6 // 4 == t6 // 4)
    btri = (same & (s6 <= t6)).astype(np.float32)
    Rb = (same & (s6 > t6)).astype(np.float32)
    put("btri", btri)
    put("nbtri16", -btri / 16.0)
    put("nRb16", -Rb / 16.0)
    put("negbtri", -btri)
    put("negbmask4", np.tile(NEG * (1.0 - btri), (1, 4)))
    put("Rbs", Rb)
    put("seqmask", (s6 // 4 == np.arange(16)[None, :]).astype(np.float32))
    col = (np.arange(16)[:, None] == (np.arange(64)[None, :] // 4)).astype(np.float32).reshape(1, 1024)
    put("colmask", np.broadcast_to(col, (128, 1024)))
    return m


def _pvec(inp, flag):
    m = np.zeros((128, NPV), np.float32)

    def put(n, a):
        a0, a1 = PV[n]
        m[:a.shape[0], a0:a0 + a.shape[1]] = a
    put("dtb", np.broadcast_to(inp["dt_bias"][0][None, :], (128, 32)))
    put("alog", np.broadcast_to(inp["a_log"][0][None, :], (128, 32)))
    put("bgk", inp["b_gk"][0][None, :])
    put("wgk", inp["w_gk_up"][0])
    put("gnw", inp["gla_norm_w"][0].reshape(4, 128).T)
    put("convw", inp["conv_w"][0].reshape(4, 32, 128).transpose(2, 1, 0).reshape(128, 128))
    put("convb", inp["conv_b"][0].reshape(32, 128).T)
    put("dsk", np.repeat(inp["d_skip"][0].reshape(16, 2), 64, axis=1).T)
    put("snw", inp["ssd_norm_w"][0].reshape(16, 128).T)
    put("flag", np.full((128, 1), flag, np.float32))
    return m


def _win_cols():
    cols = -np.ones((97, 128), np.int64)
    cols[0, 0:16] = np.arange(6144, 6160)
    cols[0, 32:64] = np.arange(12304, 12336)
    r = np.arange(128)
    for h in range(4):
        b = 1 + 12 * h
        for j in range(2):
            cols[b + j] = 1024 + h * 256 + j * 128 + r
            cols[b + 6 + j] = h * 256 + j * 128 + r
        for j in range(4):
            cols[b + 2 + j] = 2048 + h * 512 + j * 128 + r
            cols[b + 8 + j] = 4096 + h * 512 + j * 128 + r
    for g in range(8):
        b = 49 + 6 * g
        for j in range(2):
            cols[b + j] = 8208 + g * 256 + j * 128 + r
            cols[b + 4 + j] = 6160 + g * 256 + j * 128 + r
        cols[b + 2] = 10256 + g * 128 + r
        cols[b + 3] = 11280 + g * 128 + r
    return cols


_NC_CACHE = {}


def _prep_shared(inp):
    w_in = np.asarray(inp["w_in"][0])
    cols = _win_cols().reshape(-1)
    wz = np.concatenate([w_in, np.zeros((4096, 1), np.float32)], axis=1)
    g = wz[:, np.where(cols < 0, w_in.shape[1], cols)]
    w_in_t = np.ascontiguousarray(g.reshape(32, 128, 97, 128).transpose(2, 1, 0, 3))
    w_out_t = np.ascontiguousarray(np.asarray(inp["w_out"][0]).reshape(32, 128, 32, 128).transpose(2, 1, 0, 3))
    w_up_t = np.ascontiguousarray(np.asarray(inp["w_up"][0]).reshape(32, 128, 128, 128).transpose(2, 1, 0, 3))
    w_down = np.ascontiguousarray(np.asarray(inp["w_down"][0]))
    lnp = np.ascontiguousarray(np.stack([inp["ln1_g"][0], inp["ln1_b"][0], inp["ln2_g"][0], inp["ln2_b"][0]]).astype(np.float32))
    return dict(w_in_t=w_in_t, w_out_t=w_out_t, w_up_t=w_up_t, w_down=w_down, lnp=lnp, cmask=_cmask())


def _make_in_maps(inp):
    inp = {k: np.asarray(v) for k, v in inp.items()}
    shared = _prep_shared(inp)
    xfull = np.concatenate([np.broadcast_to(inp["meta_tokens"][None], (4, 16, 4096)), inp["x_prompt"]], axis=1)
    xs = inp["x_sample"]
    in_maps = []
    for c in range(8):
        b, hh = c // 2, c % 2
        main = xfull[b, hh * 1032:(hh + 1) * 1032]
        samp = xs[16 * c:16 * c + 16].reshape(64, 4096)
        xm = np.ascontiguousarray(np.concatenate([main[:1024], samp, main[1024:]], axis=0))
        xp = np.ascontiguousarray(xfull[b, 0:1032]) if hh == 1 else np.zeros((NPRE, 4096), np.float32)
        m = dict(shared)
        m.update(xm=xm, xp=xp, pvec=_pvec(inp, float(hh)),
                 sgla=np.ascontiguousarray(inp["state_gla"][0, 16 * c:16 * c + 16]),
                 sssm=np.ascontiguousarray(inp["state_ssm"][0, 16 * c:16 * c + 16]),
                 sconv=np.ascontiguousarray(inp["state_conv"][0, 16 * c:16 * c + 16].reshape(48, 4096)))
        in_maps.append(m)
    return in_maps


def _assemble(R, cores=range(8)):
    y_prompt = np.zeros((4, 2048, 4096), np.float32)
    y_sample = np.zeros((128, 4, 4096), np.float32)
    gla_p = np.zeros((1, 4, 4, 256, 512), np.float32)
    ssm_p = np.zeros((1, 4, 32, 64, 128), np.float32)
    conv_p = np.zeros((1, 4, 3, 4096), np.float32)
    gla_s = np.zeros((1, 128, 4, 256, 512), np.float32)
    ssm_s = np.zeros((1, 128, 32, 64, 128), np.float32)
    conv_s = np.zeros((1, 128, 3, 4096), np.float32)
    for k, c in enumerate(cores):
        b, hh = c // 2, c % 2
        r = R[k]
        y = np.asarray(r["y"])
        main = np.concatenate([y[:1024], y[1088:1096]], axis=0)
        if hh == 0:
            y_prompt[b, 0:1016] = main[16:]
        else:
            y_prompt[b, 1016:2048] = main
            gla_p[0, b] = np.asarray(r["gla_p"])
            ssm_p[0, b] = np.asarray(r["ssm_p"])
            conv_p[0, b] = np.asarray(r["conv_p"])
        y_sample[16 * c:16 * c + 16] = y[1024:1088].reshape(16, 4, 4096)
        gla_s[0, 16 * c:16 * c + 16] = np.asarray(r["gla_s"])
        ssm_s[0, 16 * c:16 * c + 16] = np.asarray(r["ssm_s"])
        conv_s[0, 16 * c:16 * c + 16] = np.asarray(r["conv_s"]).reshape(16, 3, 4096)
    return (y_prompt, y_sample, gla_p, ssm_p, conv_p, gla_s, ssm_s, conv_s)


def kernel(**inp):
    in_maps = _make_in_maps(inp)
    if 99 not in _NC_CACHE:
        _NC_CACHE[99] = build_nc(99)
    res = run_bass_kernel_spmd(_NC_CACHE[99], in_maps, core_ids=list(range(8)))
    return _assemble(res.results)
```
